# Optimizing a Trainium2 kernel written in Bass

```python
import jax, jax.numpy as jnp
from jax import lax
import numpy as np

D_MODEL = 1024
BATCH = 4
SEQ = 4096
DEPTH = 2

GRID_W = 64
CTX_LEN = 256
EPS = 1e-6
NEG_INF = -1e30
N_MOD = 6
N_EVEN = (DEPTH + 1) // 2
N_ODD = DEPTH // 2
POOL_WINDOWS = (2, 4, 8, 16)
N_POOL_GROUPS = len(POOL_WINDOWS)
POOL_GROUP_DIM = D_MODEL // 8
POOL_WIDTH = N_POOL_GROUPS * POOL_GROUP_DIM
HEAD_DIM = 64
N_Q_HEADS = D_MODEL // 128
N_KV_HEADS = N_Q_HEADS // 4
GQA_GROUP = N_Q_HEADS // N_KV_HEADS
ATTN_WIDTH = N_Q_HEADS * HEAD_DIM
KV_WIDTH = N_KV_HEADS * HEAD_DIM
WINDOW = 128
ATTN_BLOCK = 128
ROPE_BASE = 10000.0
ROPE_FREQS = HEAD_DIM // 4
Q_END = POOL_WIDTH + ATTN_WIDTH
IN_EVEN = Q_END + 2 * KV_WIDTH
MIX_EVEN = POOL_WIDTH + ATTN_WIDTH
CHUNK = 128
N_SGU_GROUPS = 8
SGU_WIDTH = D_MODEL
SGU_GROUP_DIM = SGU_WIDTH // N_SGU_GROUPS
D_FF = 128 * ((8 * D_MODEL // 3 + 127) // 128)
CONV_W = 3

kernel_name = "hybrid_pool_swa_sgu_convffn_ctxprefix"

f32 = jnp.float32


def rmsnorm(x, g):
    xf = x.astype(f32)
    y = xf * lax.rsqrt(jnp.mean(xf * xf, axis=-1, keepdims=True) + EPS)
    return (y * g.astype(f32)).astype(x.dtype)


def layernorm(x, g, b):
    xf = x.astype(f32)
    mu = jnp.mean(xf, axis=-1, keepdims=True)
    var = jnp.mean(jnp.square(xf - mu), axis=-1, keepdims=True)
    y = (xf - mu) * lax.rsqrt(var + EPS)
    return (y * g.astype(f32) + b.astype(f32)).astype(x.dtype)


def adaln(cvec, w, b):
    m = jax.nn.silu(cvec) @ w + b
    return jnp.split(m[:, None, :], N_MOD, axis=-1)


def pre(x, g, shift, scale):
    return rmsnorm(x, g) * (1 + scale) + shift


def post(x, y, g, gate):
    return x + gate * rmsnorm(y, g)


def axial_rope_tables(L):
    rows = L // GRID_W
    row = jnp.repeat(jnp.arange(rows), GRID_W).astype(f32)
    col = jnp.tile(jnp.arange(GRID_W), rows).astype(f32)
    inv = ROPE_BASE ** (-jnp.arange(ROPE_FREQS, dtype=f32) / ROPE_FREQS)
    ang = jnp.concatenate([row[:, None] * inv, col[:, None] * inv], axis=-1)
    return jnp.cos(ang), jnp.sin(ang)


def apply_rope(x, cos, sin):
    B, L, H, _ = x.shape
    xr = x.astype(f32).reshape(B, L, H, 2, 2, ROPE_FREQS)
    x1, x2 = xr[..., 0, :], xr[..., 1, :]
    c = cos.reshape(L, 2, ROPE_FREQS)[None, :, None]
    s = sin.reshape(L, 2, ROPE_FREQS)[None, :, None]
    out = jnp.stack([x1 * c - x2 * s, x2 * c + x1 * s], axis=-2)
    return out.reshape(x.shape).astype(x.dtype)


def pool_mixer(u, w_pool, pool_scale):
    B, L, _ = u.shape
    ug = u.reshape(B, L, N_POOL_GROUPS, POOL_GROUP_DIM)
    cs = jnp.pad(jnp.cumsum(ug.astype(f32), axis=1), ((0, 0), (1, 0), (0, 0), (0, 0)))
    t = jnp.arange(L)[:, None]
    half = jnp.asarray(np.array([w // 2 for w in POOL_WINDOWS], dtype=np.int32))[None, :]
    start = jnp.clip(t - half, 0, L)
    end = jnp.clip(t + half, 0, L)
    g_idx = jnp.arange(N_POOL_GROUPS)[None, :]
    win_sum = cs[:, end, g_idx] - cs[:, start, g_idx]
    mean = win_sum / (end - start).astype(f32)[None, :, :, None]
    pooled = (mean - ug.astype(f32)).astype(u.dtype)
    y = jnp.einsum('blgc,gcd->blgd', pooled, w_pool)
    return y.reshape(B, L, POOL_WIDTH) * pool_scale


def band_mask(nb, L):
    i = jnp.arange(ATTN_BLOCK)[None, :, None]
    j = jnp.arange(3 * ATTN_BLOCK)[None, None, :]
    n = jnp.arange(nb)[:, None, None]
    q_pos = n * ATTN_BLOCK + i
    k_pos = (n - 1) * ATTN_BLOCK + j
    return (jnp.abs(k_pos - q_pos) <= WINDOW) & (k_pos >= 0) & (k_pos < L)


def windowed_gqa(q, k, v, kc, vc, sink):
    B, L = q.shape[:2]
    nb = L // ATTN_BLOCK
    scale = HEAD_DIM ** -0.5
    qb = q.reshape(B, nb, ATTN_BLOCK, N_KV_HEADS, GQA_GROUP, HEAD_DIM)

    def band(t):
        tb = jnp.pad(t.reshape(B, nb, ATTN_BLOCK, N_KV_HEADS, HEAD_DIM), ((0, 0), (1, 1), (0, 0), (0, 0), (0, 0)))
        return jnp.concatenate([tb[:, :-2], tb[:, 1:-1], tb[:, 2:]], axis=2)

    kb, vb = band(k), band(v)
    s_loc = jnp.einsum('bnqkgd,bnskd->bnkgqs', qb, kb, preferred_element_type=f32) * scale
    s_loc = jnp.where(band_mask(nb, L)[None, :, None, None], s_loc, NEG_INF)
    s_ctx = jnp.einsum('bnqkgd,bskd->bnkgqs', qb, kc, preferred_element_type=f32) * scale
    s_sink = jnp.broadcast_to(sink.astype(f32).reshape(N_KV_HEADS, GQA_GROUP, 1, 1), s_loc.shape[:-1] + (1,))
    p = jax.nn.softmax(jnp.concatenate([s_loc, s_ctx, s_sink], axis=-1), axis=-1)
    n_loc = 3 * ATTN_BLOCK
    p_loc = p[..., :n_loc].astype(v.dtype)
    p_ctx = p[..., n_loc:-1].astype(v.dtype)
    o = jnp.einsum('bnkgqs,bnskd->bnqkgd', p_loc, vb) + jnp.einsum('bnkgqs,bskd->bnqkgd', p_ctx, vc)
    return o.reshape(B, L, ATTN_WIDTH)


def context_gqa(qc, kc, vc, sink):
    B, C = qc.shape[:2]
    qg = qc.reshape(B, C, N_KV_HEADS, GQA_GROUP, HEAD_DIM)
    s = jnp.einsum('bqkgd,bskd->bkgqs', qg, kc, preferred_element_type=f32) * HEAD_DIM ** -0.5
    s_sink = jnp.broadcast_to(sink.astype(f32).reshape(N_KV_HEADS, GQA_GROUP, 1, 1), s.shape[:-1] + (1,))
    p = jax.nn.softmax(jnp.concatenate([s, s_sink], axis=-1), axis=-1)
    o = jnp.einsum('bkgqs,bskd->bqkgd', p[..., :-1].astype(vc.dtype), vc)
    return o.reshape(B, C, ATTN_WIDTH)


def even_mixer(h, hc, w_in, w_pool, pool_scale, sink, w_out, cos, sin, ctx_out):
    B, L, _ = h.shape
    C = hc.shape[1]
    z = h @ w_in
    u = z[..., :POOL_WIDTH]
    q = apply_rope(z[..., POOL_WIDTH:Q_END].reshape(B, L, N_Q_HEADS, HEAD_DIM), cos, sin)
    k = apply_rope(z[..., Q_END:Q_END + KV_WIDTH].reshape(B, L, N_KV_HEADS, HEAD_DIM), cos, sin)
    v = z[..., Q_END + KV_WIDTH:].reshape(B, L, N_KV_HEADS, HEAD_DIM)
    zc = hc @ w_in[:, Q_END:]
    kc = zc[..., :KV_WIDTH].reshape(B, C, N_KV_HEADS, HEAD_DIM)
    vc = zc[..., KV_WIDTH:].reshape(B, C, N_KV_HEADS, HEAD_DIM)
    y = jnp.concatenate([pool_mixer(u, w_pool, pool_scale), windowed_gqa(q, k, v, kc, vc, sink)], axis=-1) @ w_out
    yc = None
    if ctx_out:
        zq = hc @ w_in[:, :Q_END]
        qc = zq[..., POOL_WIDTH:].reshape(B, C, N_Q_HEADS, HEAD_DIM)
        yc = jnp.concatenate([pool_mixer(zq[..., :POOL_WIDTH], w_pool, pool_scale),
                              context_gqa(qc, kc, vc, sink)], axis=-1) @ w_out
    return y, yc


def sgu_mixer(h, w_in, ln_g, ln_b, w_s, b_s, w_out):
    B, L, _ = h.shape
    z = jax.nn.gelu(h @ w_in)
    u, v = jnp.split(z, 2, axis=-1)
    v = layernorm(v, ln_g, ln_b)
    vb = v.reshape(B, L // CHUNK, CHUNK, N_SGU_GROUPS, SGU_GROUP_DIM)
    s = jnp.einsum('gpq,bnqgc->bnpgc', w_s, vb) + b_s.T[:, :, None]
    return (u * s.reshape(B, L, SGU_WIDTH)) @ w_out


def conv_ffn(h, w_up, conv_w, conv_b, w_down):
    hu = h @ w_up
    hcv = lax.conv_general_dilated(hu, conv_w[:, None, :], window_strides=(1,), padding='SAME',
                                   dimension_numbers=('NWC', 'WIO', 'NWC'),
                                   feature_group_count=hu.shape[-1]) + conv_b
    gate, up = jnp.split(hcv, 2, axis=-1)
    return (jax.nn.silu(gate) * up) @ w_down


def setup_inputs(seed: int = 0) -> dict:
    key = jax.random.key(seed)
    ks = jax.random.split(key, 32)
    nrm = jax.random.normal
    D = D_MODEL
    return {
        "x": nrm(ks[0], (BATCH, SEQ, D), f32),
        "c": nrm(ks[1], (BATCH, D), f32),
        "ctx": nrm(ks[2], (BATCH, CTX_LEN, D), f32),
        "c_ctx": nrm(ks[3], (D,), f32),
        "w_ada": nrm(ks[4], (DEPTH, D, N_MOD * D), f32) * (0.5 * D ** -0.5),
        "b_ada": nrm(ks[5], (DEPTH, N_MOD * D), f32) * 0.01,
        "g_mix_pre": 1.0 + 0.1 * nrm(ks[6], (DEPTH, D), f32),
        "g_mix_post": 1.0 + 0.1 * nrm(ks[7], (DEPTH, D), f32),
        "g_ffn_pre": 1.0 + 0.1 * nrm(ks[8], (DEPTH, D), f32),
        "g_ffn_post": 1.0 + 0.1 * nrm(ks[9], (DEPTH, D), f32),
        "w_in_even": nrm(ks[10], (N_EVEN, D, IN_EVEN), f32) * D ** -0.5,
        "w_pool": nrm(ks[11], (N_EVEN, N_POOL_GROUPS, POOL_GROUP_DIM, POOL_GROUP_DIM), f32) * POOL_GROUP_DIM ** -0.5,
        "pool_scale": 1.0 + 0.1 * nrm(ks[12], (N_EVEN, POOL_WIDTH), f32),
        "attn_sink": 0.5 * nrm(ks[13], (N_EVEN, N_Q_HEADS), f32),
        "w_out_even": nrm(ks[14], (N_EVEN, MIX_EVEN, D), f32) * MIX_EVEN ** -0.5,
        "w_in_odd": nrm(ks[15], (N_ODD, D, 2 * SGU_WIDTH), f32) * D ** -0.5,
        "sgu_ln_g": 1.0 + 0.1 * nrm(ks[16], (N_ODD, SGU_WIDTH), f32),
        "sgu_ln_b": 0.01 * nrm(ks[17], (N_ODD, SGU_WIDTH), f32),
        "sgu_w": nrm(ks[18], (N_ODD, N_SGU_GROUPS, CHUNK, CHUNK), f32) * CHUNK ** -0.5,
        "sgu_b": 1.0 + 0.1 * nrm(ks[19], (N_ODD, N_SGU_GROUPS, CHUNK), f32),
        "w_out_odd": nrm(ks[20], (N_ODD, SGU_WIDTH, D), f32) * SGU_WIDTH ** -0.5,
        "w_ffn_up": nrm(ks[21], (DEPTH, D, 2 * D_FF), f32) * D ** -0.5,
        "ffn_conv_w": nrm(ks[22], (DEPTH, CONV_W, 2 * D_FF), f32) * CONV_W ** -0.5,
        "ffn_conv_b": 0.01 * nrm(ks[23], (DEPTH, 2 * D_FF), f32),
        "w_ffn_down": nrm(ks[24], (DEPTH, D_FF, D), f32) * D_FF ** -0.5,
    }


def reference(x, c, ctx, c_ctx, w_ada, b_ada, g_mix_pre, g_mix_post, g_ffn_pre, g_ffn_post,
              w_in_even, w_pool, pool_scale, attn_sink, w_out_even,
              w_in_odd, sgu_ln_g, sgu_ln_b, sgu_w, sgu_b, w_out_odd,
              w_ffn_up, ffn_conv_w, ffn_conv_b, w_ffn_down):
    L = x.shape[1]
    cos, sin = axial_rope_tables(L)
    xc = ctx
    for i in range(DEPTH):
        advance_ctx = any(j % 2 == 0 for j in range(i + 1, DEPTH))
        sh_m, sc_m, gt_m, sh_f, sc_f, gt_f = adaln(c, w_ada[i], b_ada[i])
        if i % 2 == 0 or advance_ctx:
            csh_m, csc_m, cgt_m, csh_f, csc_f, cgt_f = adaln(c_ctx[None, :], w_ada[i], b_ada[i])
        h = pre(x, g_mix_pre[i], sh_m, sc_m)
        if i % 2 == 0:
            e = i // 2
            hc = pre(xc, g_mix_pre[i], csh_m, csc_m)
            y, yc = even_mixer(h, hc, w_in_even[e], w_pool[e], pool_scale[e], attn_sink[e], w_out_even[e],
                               cos, sin, advance_ctx)
        else:
            o = i // 2
            y = sgu_mixer(h, w_in_odd[o], sgu_ln_g[o], sgu_ln_b[o], sgu_w[o], sgu_b[o], w_out_odd[o])
            yc = None
            if advance_ctx:
                yc = sgu_mixer(pre(xc, g_mix_pre[i], csh_m, csc_m), w_in_odd[o], sgu_ln_g[o], sgu_ln_b[o],
                               sgu_w[o], sgu_b[o], w_out_odd[o])
        x = post(x, y, g_mix_post[i], gt_m)
        x = post(x, conv_ffn(pre(x, g_ffn_pre[i], sh_f, sc_f), w_ffn_up[i], ffn_conv_w[i], ffn_conv_b[i],
                             w_ffn_down[i]), g_ffn_post[i], gt_f)
        if advance_ctx:
            xc = post(xc, yc, g_mix_post[i], cgt_m)
            xc = post(xc, conv_ffn(pre(xc, g_ffn_pre[i], csh_f, csc_f), w_ffn_up[i], ffn_conv_w[i],
                                   ffn_conv_b[i], w_ffn_down[i]), g_ffn_post[i], cgt_f)
    return x
```

```python
import contextlib
import numpy as np
import concourse.bass as bass
import concourse.mybir as mybir
from concourse.bass_utils import run_bass_kernel_spmd

F32 = mybir.dt.float32
BF16 = mybir.dt.bfloat16
AF = mybir.ActivationFunctionType
ALU = mybir.AluOpType

D = 1024
L = 4096
NT = 19
T = NT * 128
TOWN = 2048
CTX = 256
DFF = 2816
NF = 22
EPS = 1e-6
POOL_W = (2, 4, 8, 16)
N_L0_MIX = 18 * 128
N_L0_FFN = 17 * 128
N_L1_MIX = 17 * 128
FFN_BLK = 510


class Sched:
    def __init__(self, nc, st):
        self.nc = nc
        self.st = st
        self.eng = dict(pe=nc.tensor, act=nc.scalar, dve=nc.vector, pool=nc.gpsimd, sp=nc.sync)
        self.sem = {k: st.enter_context(nc.semaphore("sem_" + k)) for k in ("pe", "act", "dve", "pool")}
        self.cnt = {k: 0 for k in self.sem}
        self.dsem = {}
        self.dcnt = {}
        self.waited = {k: {} for k in self.eng}
        self.lastw = {}
        self.readers = {}

    def _deps(self, reads, writes):
        deps = {}

        def add(k, v):
            if deps.get(k, 0) < v:
                deps[k] = v

        for r in reads:
            t = self.lastw.get(r)
            if t:
                add(*t)
        for w in writes:
            t = self.lastw.get(w)
            if t:
                add(*t)
            for k, v in self.readers.get(w, {}).items():
                add(k, v)
        return deps

    def _semof(self, k):
        return self.sem[k] if k in self.sem else self.dsem[k]

    def _wait(self, eng, deps):
        e = self.eng[eng]
        for k, v in deps.items():
            if eng == "pe" and k == "pe":
                continue
            if self.waited[eng].get(k, 0) >= v:
                continue
            e.wait_ge(self._semof(k), v)
            self.waited[eng][k] = v

    def _commit(self, tok, reads, writes):
        k, v = tok
        for r in reads:
            d = self.readers.setdefault(r, {})
            if d.get(k, 0) < v:
                d[k] = v
        for w in writes:
            self.lastw[w] = tok
            self.readers[w] = {}

    def op(self, eng, fn, reads=(), writes=()):
        self._wait(eng, self._deps(reads, writes))
        inst = fn(self.eng[eng])
        self.cnt[eng] += 1
        inst.then_inc(self.sem[eng], 1)
        self._commit((eng, self.cnt[eng]), reads, writes)

    def dma(self, q, key, out, in_, reads=(), writes=()):
        if key not in self.dsem:
            self.dsem[key] = self.st.enter_context(self.nc.semaphore("dsem_" + key))
            self.dcnt[key] = 0
        self._wait(q, self._deps(reads, writes))
        self.eng[q].dma_start(out=out, in_=in_).then_inc(self.dsem[key], 16)
        self.dcnt[key] += 16
        self._commit((key, self.dcnt[key]), reads, writes)

    def seal(self, key, resources):
        for r in resources:
            self.lastw[r] = (key, self.dcnt[key])

    def barrier(self, engines=("pe", "act", "dve", "pool", "sp")):
        for e in engines:
            deps = {k: v for k, v in self.cnt.items() if v > 0}
            deps.update({k: v for k, v in self.dcnt.items() if v > 0})
            deps.pop(e, None) if e == "pe" else None
            self._wait(e, deps)


class Ctx:
    pass


def build_program(debug_stage=None):
    nc = bass.Bass("TRN2", target_bir_lowering=False)
    g = Ctx()

    def din(name, shape, dt=F32):
        return nc.dram_tensor(name, list(shape), dt, kind="ExternalInput").ap()

    x_loc = din("x_loc", [NT, 128, D])
    ctx_in = din("ctx_in", [2, 128, D])
    cvec = din("cvec", [128, 8, 2])
    w_ada = din("w_ada", [2, 12, 128, 8, 512])
    b_ada = din("b_ada", [128, 2, 48])
    gvec = din("gvec", [128, 2, 4, 8])
    w_in0 = din("w_in0", [128, 8, 1920])
    ropeC = din("ropeC", [128, T])
    ropeS = din("ropeS", [128, T])
    w_pool = din("w_pool", [128, 4, 128])
    pool_sc = din("pool_sc", [128, 4])
    mpool = din("mpool", [128, 4, 4, 128])
    sink_bc = din("sink_bc", [128, 8])
    negmask = din("negmask", [128, 2, 128])
    ident_in = din("ident_in", [128, 128])
    w_out0 = din("w_out0", [128, 8, D])
    w_in1 = din("w_in1", [128, 8, 2048])
    ln_gb = din("ln_gb", [128, 2, D])
    wsT = din("wsT", [128, 8, 128])
    bs_bc = din("bs_bc", [128, 8, 128])
    w_out1 = din("w_out1", [128, 8, D])
    w_up = din("w_up", [2, NF, 128, 8, 256])
    w_down = din("w_down", [2, 8, 128, NF, 128])
    conv_w = din("conv_w", [128, 2, 44, 3])
    conv_b = din("conv_b", [128, 2, 44])
    out_loc = nc.dram_tensor("out_loc", [TOWN // 128, 128, D], F32, kind="ExternalOutput").ap()
    qk_d = nc.dram_tensor("qk_d", [5, 128, T], BF16).ap()
    u_d = nc.dram_tensor("u_d", [NT, 128, 512], BF16).ap()
    v_d = nc.dram_tensor("v_d", [NT, 128, 128], BF16).ap()

    with contextlib.ExitStack() as st:
        E = st.enter_context
        S = Sched(nc, st)

        def sb(name, shape, dt=F32, stack=None):
            return (stack or st).enter_context(nc.sbuf_tensor(name, list(shape), dt))

        ps = [E(nc.psum_tensor(f"ps{i}", [128, 512], F32)) for i in range(8)]
        rr = {"mm": 0, "st": 0}

        bcfg = {"mm": [0, 1, 2, 3, 4, 5], "st": [6, 7]}

        def bank(pool="mm"):
            lst = bcfg[pool]
            ctr = "mm" if bcfg["st"] is bcfg["mm"] else pool
            i = lst[rr[ctr] % len(lst)]
            rr[ctr] += 1
            return i

        x_fm = sb("x_fm", [128, 8, T])
        ident = sb("ident", [128, 128])
        ident_bf = sb("ident_bf", [128, 128], BF16)
        ones_bf = sb("ones_bf", [128, 128], BF16)
        coef = sb("coef", [128, 2, 6, 8, 2])
        gv = sb("gv", [128, 2, 4, 8])
        cw = sb("cw", [128, 2, 44, 3])
        cb = sb("cb", [128, 2, 44])
        eps_t = sb("eps_t", [128, 1])
        sq = sb("sq", [128, 8, 512], BF16)
        rt = sb("rt", [128, 512])
        rstd = sb("rstd", [128, 512])
        tn = [sb(f"tn{i}", [128, 512]) for i in range(2)]
        kc_sb = sb("kc_sb", [128, CTX], BF16)
        vpc = sb("vpc", [128, 2, 2, 128], BF16)

        cv = sb("cv", [128, 8, 2])
        bada = sb("bada", [128, 2, 48])
        S.dma("sp", "c0", ident[:], ident_in, writes=["ident"])
        S.dma("sp", "c0", gv[:], gvec, writes=["gv"])
        S.dma("sp", "c0", cw[:], conv_w, writes=["cw"])
        S.dma("sp", "c0", cb[:], conv_b, writes=["cb"])
        S.dma("sp", "c0", cv[:], cvec, writes=["cv"])
        S.dma("sp", "c0", bada[:], b_ada, writes=["bada"])
        S.seal("c0", ["ident", "gv", "cw", "cb", "cv", "bada"])
        S.op("dve", lambda e: e.memset(ones_bf[:], 1.0), writes=["ones"])
        S.op("dve", lambda e: e.memset(eps_t[:], EPS), writes=["eps"])
        S.op("dve", lambda e: e.tensor_copy(ident_bf[:], ident[:]), reads=["ident"], writes=["identbf"])

        def xk(c0, n):
            return [("x", t) for t in range(c0 // 128, (c0 + n - 1) // 128 + 1)]

        scrA = dict(sq=sq, rt=rt, rstd=rstd, k="")

        def prenorm(src, c0, n, lay, ka, col, dst, doff, srckey, dstkey, scr=None):
            scr = scr or scrA
            sq, rt, rstd, sk = scr["sq"], scr["rt"], scr["rstd"], scr["k"]
            S.op("act", lambda e: e.activation(out=sq[:, :, 0:n], in_=src[:, :, c0:c0 + n], func=AF.Square),
                 reads=srckey, writes=["sq" + sk])
            b = bank("st")

            def f(e):
                for kt in range(8):
                    i = e.matmul(ps[b][:, 0:n], ones_bf[:], sq[:, kt, 0:n], start=(kt == 0), stop=(kt == 7))
                return i
            S.op("pe", f, reads=["sq" + sk, "ones"], writes=[("ps", b)])
            S.op("act", lambda e: e.activation(out=rt[:, 0:n], in_=ps[b][:, 0:n], func=AF.Sqrt, scale=1.0 / D, bias=eps_t[:, 0:1]),
                 reads=[("ps", b), "eps"], writes=["rt" + sk])
            S.op("dve", lambda e: e.reciprocal(out=rstd[:, 0:n], in_=rt[:, 0:n]), reads=["rt" + sk], writes=["rstd" + sk])
            for kt in range(8):
                ts = kt % 2
                S.op("dve", lambda e, kt=kt, ts=ts: e.tensor_tensor(out=tn[ts][:, 0:n], in0=src[:, kt, c0:c0 + n], in1=rstd[:, 0:n], op=ALU.mult),
                     reads=srckey + ["rstd" + sk], writes=[f"tn{ts}"])
                S.op("act", lambda e, kt=kt, ts=ts: e.activation(out=dst[:, kt, doff:doff + n], in_=tn[ts][:, 0:n], func=AF.Identity,
                                                                 scale=coef[:, lay, ka, kt, col:col + 1], bias=coef[:, lay, ka + 1, kt, col:col + 1]),
                     reads=[f"tn{ts}", "coef"], writes=[dstkey])

        def post_update(ysb, n, c0, ykey):
            b = bank("st")

            def f(e):
                for kt in range(8):
                    i = e.matmul(ps[b][:, 0:n], ones_bf[:], sq[:, kt, 0:n], start=(kt == 0), stop=(kt == 7))
                return i
            S.op("pe", f, reads=["sq", "ones"], writes=[("ps", b)])
            S.op("act", lambda e: e.activation(out=rt[:, 0:n], in_=ps[b][:, 0:n], func=AF.Sqrt, scale=1.0 / D, bias=eps_t[:, 0:1]),
                 reads=[("ps", b), "eps"], writes=["rt"])
            S.op("dve", lambda e: e.reciprocal(out=rstd[:, 0:n], in_=rt[:, 0:n]), reads=["rt"], writes=["rstd"])
            S.op("dve", lambda e: e.tensor_tensor(out=ysb[:, :, 0:n], in0=ysb[:, :, 0:n],
                                                  in1=rstd[:, None, 0:n].to_broadcast([128, 8, n]), op=ALU.mult),
                 reads=[ykey, "rstd"], writes=[ykey])
            S.op("dve", lambda e: e.tensor_tensor(out=x_fm[:, :, c0:c0 + n], in0=x_fm[:, :, c0:c0 + n], in1=ysb[:, :, 0:n], op=ALU.add),
                 reads=[ykey] + xk(c0, n), writes=xk(c0, n))

        def out_proj_post(W, nk, rhs, rhskey, wkey, n, c0, lay, kg, ysb, ykey):
            for d in range(8):
                b = bank()

                def f(e, d=d, b=b):
                    for k in range(nk):
                        i = e.matmul(ps[b][:, 0:n], W[:, k, d * 128:(d + 1) * 128], rhs[:, k, 0:n], start=(k == 0), stop=(k == nk - 1))
                    return i
                S.op("pe", f, reads=[rhskey, wkey], writes=[("ps", b)])
                S.op("act", lambda e, d=d, b=b: e.activation(out=sq[:, d, 0:n], in_=ps[b][:, 0:n], func=AF.Square),
                     reads=[("ps", b)], writes=["sq"])
                S.op("act", lambda e, d=d, b=b: e.activation(out=ysb[:, d, 0:n], in_=ps[b][:, 0:n], func=AF.Copy,
                                                             scale=coef[:, lay, kg, d, 0:1]),
                     reads=[("ps", b), "coef"], writes=[ykey])
            post_update(ysb, n, c0, ykey)

        ph01 = contextlib.ExitStack()
        g.ctx_fm = sb("ctx_fm", [128, 8, CTX], stack=ph01)
        with contextlib.ExitStack() as ph:
            cs = sb("cs", [128, 8, 2], BF16, stack=ph)
            wa = [sb(f"wa{i}", [128, 8, 512], BF16, stack=ph) for i in range(3)]
            modsb = sb("modsb", [128, 2, 6, 8, 2], stack=ph)
            xs = [sb(f"xs{i}", [128, D], stack=ph) for i in range(2)]
            S.op("act", lambda e: e.activation(out=cs[:], in_=cv[:], func=AF.Silu), reads=["cv"], writes=["cs"])
            for lay in range(2):
                bm = bank("st")
                for ch in range(12):
                    sl = (lay * 12 + ch) % 3
                    S.dma("pool", f"wa{sl}", wa[sl][:], w_ada[lay, ch], writes=[f"wa{sl}"])

                    def f(e, ch=ch, sl=sl, bm=bm):
                        for ft in range(4):
                            o = (ch * 4 + ft) * 2
                            for kt in range(8):
                                i = e.matmul(ps[bm][:, o:o + 2], wa[sl][:, kt, ft * 128:(ft + 1) * 128], cs[:, kt, :], start=(kt == 0), stop=(kt == 7))
                        return i
                    S.op("pe", f, reads=[f"wa{sl}", "cs"], writes=[("ps", bm)])
                S.op("dve", lambda e, lay=lay, bm=bm: e.tensor_tensor(
                    out=modsb[:, lay].rearrange("p j c t -> p (j c) t"),
                    in0=ps[bm][:, 0:96].rearrange("p (a t) -> p a t", t=2),
                    in1=bada[:, lay, :, None].to_broadcast([128, 48, 2]), op=ALU.add),
                    reads=[("ps", bm), "bada"], writes=["modsb"])
                for (ka, jsc, jsh, jgt, gpre, gpost) in ((0, 1, 0, 2, 0, 1), (3, 4, 3, 5, 2, 3)):
                    S.op("dve", lambda e, lay=lay, ka=ka, jsc=jsc: e.tensor_scalar(out=coef[:, lay, ka], in0=modsb[:, lay, jsc], scalar1=1.0, scalar2=None, op0=ALU.add),
                         reads=["modsb"], writes=["coef"])
                    S.op("dve", lambda e, lay=lay, ka=ka, gpre=gpre: e.tensor_tensor(out=coef[:, lay, ka], in0=coef[:, lay, ka],
                                                                                   in1=gv[:, lay, gpre, :, None].to_broadcast([128, 8, 2]), op=ALU.mult),
                         reads=["coef", "gv"], writes=["coef"])
                    S.op("dve", lambda e, lay=lay, ka=ka, jsh=jsh: e.tensor_copy(coef[:, lay, ka + 1], modsb[:, lay, jsh]),
                         reads=["modsb"], writes=["coef"])
                    S.op("dve", lambda e, lay=lay, ka=ka, jgt=jgt, gpost=gpost: e.tensor_tensor(
                        out=coef[:, lay, ka + 2], in0=modsb[:, lay, jgt], in1=gv[:, lay, gpost, :, None].to_broadcast([128, 8, 2]), op=ALU.mult),
                        reads=["modsb", "gv"], writes=["coef"])

            def load_T(src_ap, dst, t0, i, dkey):
                sl = i % 2
                S.dma("sp", f"xs{sl}", xs[sl][:], src_ap, writes=[f"xs{sl}"])
                for hlf in range(2):
                    b = bank()

                    def f(e, hlf=hlf, b=b, sl=sl):
                        for c in range(4):
                            cc = hlf * 4 + c
                            i2 = e.transpose(ps[b][:, c * 128:(c + 1) * 128], xs[sl][:, cc * 128:(cc + 1) * 128], ident[:])
                        return i2
                    S.op("pe", f, reads=[f"xs{sl}", "ident"], writes=[("ps", b)])
                    eng = "act" if hlf == 0 else "dve"
                    if eng == "act":
                        S.op("act", lambda e, hlf=hlf, b=b: e.activation(out=dst[:, hlf * 4:hlf * 4 + 4, t0:t0 + 128],
                                                                         in_=ps[b][:, :].rearrange("p (c t) -> p c t", t=128), func=AF.Copy),
                             reads=[("ps", b)], writes=[dkey])
                    else:
                        S.op("dve", lambda e, hlf=hlf, b=b: e.tensor_copy(dst[:, hlf * 4:hlf * 4 + 4, t0:t0 + 128],
                                                                          ps[b][:, :].rearrange("p (c t) -> p c t", t=128)),
                             reads=[("ps", b)], writes=[dkey])
            for i in range(NT):
                load_T(x_loc[i], x_fm, i * 128, i, ("x", i))
            for i in range(2):
                load_T(ctx_in[i], g.ctx_fm, i * 128, NT + i, "ctx")
            S.barrier()

        if debug_stage == "p0":
            pass
        if debug_stage not in ("p0",):
            with contextlib.ExitStack() as ph:
                win = sb("win", [128, 8, 1920], BF16, stack=ph)
                hb = [sb(f"hb{i}", [128, 8, 512], BF16, stack=ph) for i in range(2)]
                rcf = sb("rcf", [128, 2, T], stack=ph)
                t1s = [sb(f"t1_{i}", [128, 512], stack=ph) for i in range(2)]
                t2s = [sb(f"t2_{i}", [128, 512], stack=ph) for i in range(2)]
                qst = [sb(f"qst{i}", [128, 5, 512], BF16, stack=ph) for i in range(2)]
                ust = [sb(f"ust{i}", [128, 512], BF16, stack=ph) for i in range(2)]
                vst = [sb(f"vst{i}", [128, 128], BF16, stack=ph) for i in range(2)]
                for c in range(4):
                    S.dma("pool", "win", win[:, :, c * 480:(c + 1) * 480], w_in0[:, :, c * 480:(c + 1) * 480], writes=["win"])
                S.dma("sp", "rcf", rcf[:, 0, :], ropeC, writes=["rcf"])
                S.dma("sp", "rcf", rcf[:, 1, :], ropeS, writes=["rcf"])
                S.op("dve", lambda e: e.memset(vpc[:], 0.0), writes=["vpc"])
                nblk = (T + 511) // 512
                for bi in range(nblk):
                    c0 = bi * 512
                    n = min(512, T - c0)
                    sl = bi % 2
                    h = hb[sl]
                    hk = f"hb{sl}"
                    prenorm(x_fm, c0, n, 0, 0, 0, h, 0, xk(c0, n), hk)
                    for j in range(5):
                        ba, bb = bank(), bank()
                        t1, t2 = t1s[j % 2], t2s[j % 2]
                        k1, k2 = f"t1_{j % 2}", f"t2_{j % 2}"

                        def f(e, j=j, ba=ba, bb=bb, h=h, n=n):
                            for kt in range(8):
                                e.matmul(ps[ba][:, 0:n], win[:, kt, j * 128:(j + 1) * 128], h[:, kt, 0:n], start=(kt == 0), stop=(kt == 7))
                            for kt in range(8):
                                i = e.matmul(ps[bb][:, 0:n], win[:, kt, (5 + j) * 128:(6 + j) * 128], h[:, kt, 0:n], start=(kt == 0), stop=(kt == 7))
                            return i
                        S.op("pe", f, reads=["win", hk], writes=[("ps", ba), ("ps", bb)])
                        S.op("dve", lambda e, ba=ba, c0=c0, n=n, t1=t1: e.tensor_tensor(out=t1[:, 0:n], in0=ps[ba][:, 0:n], in1=rcf[:, 0, c0:c0 + n], op=ALU.mult),
                             reads=[("ps", ba), "rcf"], writes=[k1])
                        S.op("dve", lambda e, bb=bb, c0=c0, n=n, t2=t2: e.tensor_tensor(out=t2[:, 0:n], in0=ps[bb][:, 0:n], in1=rcf[:, 1, c0:c0 + n], op=ALU.mult),
                             reads=[("ps", bb), "rcf"], writes=[k2])
                        S.op("pool", lambda e, j=j, sl=sl, n=n, t1=t1, t2=t2: e.tensor_tensor(out=qst[sl][:, j, 0:n], in0=t1[:, 0:n], in1=t2[:, 0:n], op=ALU.add),
                             reads=[k1, k2], writes=[f"qst{sl}"])
                    S.dma("sp", f"qst{sl}", qk_d[:, :, c0:c0 + n].rearrange("j p t -> p j t"), qst[sl][:, :, 0:n], reads=[f"qst{sl}"], writes=["qk_d"])
                    for tt in range(n // 128):
                        ti = bi * 4 + tt
                        s2 = ti % 2
                        bu, bv = bank(), bank()

                        def f(e, tt=tt, bu=bu, bv=bv, h=h):
                            for kt in range(8):
                                e.matmul(ps[bu][:, :], h[:, kt, tt * 128:(tt + 1) * 128], win[:, kt, 1280:1792], start=(kt == 0), stop=(kt == 7))
                            for kt in range(8):
                                i = e.matmul(ps[bv][:, 0:128], h[:, kt, tt * 128:(tt + 1) * 128], win[:, kt, 1792:1920], start=(kt == 0), stop=(kt == 7))
                            return i
                        S.op("pe", f, reads=["win", hk], writes=[("ps", bu), ("ps", bv)])
                        S.op("act", lambda e, bu=bu, s2=s2: e.activation(out=ust[s2][:], in_=ps[bu][:, :], func=AF.Copy),
                             reads=[("ps", bu)], writes=[f"ust{s2}"])
                        S.op("act", lambda e, bv=bv, s2=s2: e.activation(out=vst[s2][:], in_=ps[bv][:, 0:128], func=AF.Copy),
                             reads=[("ps", bv)], writes=[f"vst{s2}"])
                        S.dma("sp", f"ust{s2}", u_d[ti], ust[s2][:], reads=[f"ust{s2}"], writes=["u_d"])
                        S.dma("sp", f"vst{s2}", v_d[ti], vst[s2][:], reads=[f"vst{s2}"], writes=["v_d"])
                h = hb[0]
                prenorm(g.ctx_fm, 0, CTX, 0, 0, 1, h, 0, ["ctx"], "hb0")
                bk = bank()

                def f(e, bk=bk, h=h):
                    for kt in range(8):
                        i = e.matmul(ps[bk][:, 0:CTX], win[:, kt, 4 * 128:5 * 128], h[:, kt, 0:CTX], start=(kt == 0), stop=(kt == 7))
                    return i
                S.op("pe", f, reads=["win", "hb0"], writes=[("ps", bk)])
                S.op("act", lambda e, bk=bk: e.activation(out=kc_sb[:], in_=ps[bk][:, 0:CTX], func=AF.Copy), reads=[("ps", bk)], writes=["kc"])
                for tt in range(2):
                    bv = bank()

                    def f(e, tt=tt, bv=bv, h=h):
                        for kt in range(8):
                            i = e.matmul(ps[bv][:, 0:128], h[:, kt, tt * 128:(tt + 1) * 128], win[:, kt, 1792:1920], start=(kt == 0), stop=(kt == 7))
                        return i
                    S.op("pe", f, reads=["win", "hb0"], writes=[("ps", bv)])
                    S.op("act", lambda e, tt=tt, bv=bv: e.activation(out=vpc[:, tt, 0, 0:64], in_=ps[bv][:, 0:64], func=AF.Copy),
                         reads=[("ps", bv)], writes=["vpc"])
                    S.op("act", lambda e, tt=tt, bv=bv: e.activation(out=vpc[:, tt, 1, 64:128], in_=ps[bv][:, 64:128], func=AF.Copy),
                         reads=[("ps", bv)], writes=["vpc"])
                S.barrier()
        ph01.close()

        if debug_stage not in ("p0", "p1"):
            with contextlib.ExitStack() as ph:
                wout = sb("wout", [128, 8, D], BF16, stack=ph)
                wpl = sb("wpl", [128, 4, 128], BF16, stack=ph)
                mpl = sb("mpl", [128, 4, 4, 128], BF16, stack=ph)
                nmk = sb("nmk", [128, 2, 128], BF16, stack=ph)
                psc = sb("psc", [128, 4], stack=ph)
                snk = sb("snk", [128, 8], stack=ph)
                es = sb("es", [128, 8], stack=ph)
                ublk = [sb(f"ublk{i}", [128, 6, 512], BF16, stack=ph) for i in range(1)] * 2
                kblk = [sb(f"kblk{i}", [128, 6 * 128], BF16, stack=ph) for i in range(2)]
                qblk = [sb(f"qblk{i}", [128, 4, 512], BF16, stack=ph) for i in range(2)]
                vblk = [sb(f"vblk{i}", [128, 6, 2, 128], BF16, stack=ph) for i in range(2)]
                pooled = sb("pooled", [128, 4, 512], BF16, stack=ph)
                mix = [sb(f"mix{i}", [128, 8, 512], BF16, stack=ph) for i in range(1)] * 2
                pt = [sb(f"pt{i}", [128, 512], BF16, stack=ph) for i in range(3)]
                dsum = sb("dsum", [128, 512], stack=ph)
                rden = sb("rden", [128, 512], stack=ph)
                ysb = sb("ysb", [128, 8, 512], stack=ph)
                for c in range(2):
                    S.dma("pool", "wout", wout[:, :, c * 512:(c + 1) * 512], w_out0[:, :, c * 512:(c + 1) * 512], writes=["wout"])
                S.dma("pool", "wout", wpl[:], w_pool, writes=["wpl"])
                S.dma("pool", "wout", mpl[:], mpool, writes=["mpl"])
                S.dma("pool", "wout", nmk[:], negmask, writes=["nmk"])
                S.dma("sp", "c0", psc[:], pool_sc, writes=["psc"])
                S.dma("sp", "c0", snk[:], sink_bc, writes=["snk"])
                S.seal("c0", ["psc", "snk"])
                S.seal("wout", ["wout", "wpl", "mpl", "nmk"])
                S.op("act", lambda e: e.activation(out=es[:], in_=snk[:], func=AF.Exp), reads=["snk"], writes=["es"])
                for i in range(2):
                    S.op("dve", lambda e, i=i: e.memset(vblk[i][:], 0.0), writes=[f"vblk{i}"])
                nqt = 18
                bcfg["mm"] = bcfg["st"] = [0, 1, 2, 3]
                blocks = [(j0, min(4, nqt - j0)) for j0 in range(0, nqt, 4)]
                for bi, (j0, nj) in enumerate(blocks):
                    sl = bi % 2
                    n = nj * 128
                    lo = max(j0 - 1, 0)
                    hi = j0 + nj
                    off = lo - (j0 - 1)
                    nl = hi - lo + 1
                    S.dma("sp", "ublk0", ublk[sl][:, off:off + nl, :], u_d[lo:hi + 1].rearrange("t p c -> p t c"),
                          reads=["u_d"], writes=["ublk0"])
                    S.dma("sp", f"kblk{sl}", kblk[sl][:, off * 128:(off + nl) * 128], qk_d[4, :, lo * 128:(hi + 1) * 128],
                          reads=["qk_d"], writes=[f"kblk{sl}"])
                    S.dma("sp", f"qblk{sl}", qblk[sl][:, :, 0:n], qk_d[0:4, :, j0 * 128:j0 * 128 + n].rearrange("j p t -> p j t"),
                          reads=["qk_d"], writes=[f"qblk{sl}"])
                    S.dma("sp", f"vblk{sl}", vblk[sl][:, off:off + nl, 0, 0:64], v_d[lo:hi + 1, :, 0:64].rearrange("t p c -> p t c"),
                          reads=["v_d"], writes=[f"vblk{sl}"])
                    S.dma("sp", f"vblk{sl}", vblk[sl][:, off:off + nl, 1, 64:128], v_d[lo:hi + 1, :, 64:128].rearrange("t p c -> p t c"),
                          reads=["v_d"], writes=[f"vblk{sl}"])
                    mx = mix[sl]
                    mk = "mix0"
                    pb = [bank() for _ in range(4)]
                    for gi in range(4):
                        b = pb[gi]

                        def f(e, gi=gi, b=b, j0=j0, nj=nj, sl=sl):
                            for jj in range(nj):
                                j = j0 + jj
                                terms = []
                                if j > 0:
                                    terms.append((jj, 0))
                                terms.append((jj + 1, 3 if j == 0 else 1))
                                terms.append((jj + 2, 2))
                                for ti, (slot, kind) in enumerate(terms):
                                    i = e.matmul(ps[b][:, jj * 128:(jj + 1) * 128], ublk[sl][:, slot, gi * 128:(gi + 1) * 128], mpl[:, gi, kind, :],
                                                 start=(ti == 0), stop=(ti == len(terms) - 1))
                            return i
                        S.op("pe", f, reads=["ublk0", "mpl"], writes=[("ps", b)])
                    for gi in range(4):
                        S.op("act", lambda e, gi=gi, b=pb[gi], n=n: e.activation(out=pooled[:, gi, 0:n], in_=ps[b][:, 0:n], func=AF.Copy),
                             reads=[("ps", pb[gi])], writes=[("pooled", gi)])
                    pb2 = [bank() for _ in range(4)]
                    for gi in range(4):
                        S.op("pe", lambda e, gi=gi, b2=pb2[gi], n=n: e.matmul(ps[b2][:, 0:n], wpl[:, gi, :], pooled[:, gi, 0:n], start=True, stop=True),
                             reads=[("pooled", gi), "wpl"], writes=[("ps", pb2[gi])])
                    for gi in range(4):
                        S.op("act", lambda e, gi=gi, b2=pb2[gi], n=n, mx=mx: e.activation(out=mx[:, gi, 0:n], in_=ps[b2][:, 0:n], func=AF.Copy, scale=psc[:, gi:gi + 1]),
                             reads=[("ps", pb2[gi]), "psc"], writes=[mk])
                    for jj in range(nj):
                        j = j0 + jj
                        for r in range(2):
                            pr = slice(r * 64, (r + 1) * 64)
                            chunks = []
                            if j > 0:
                                chunks.append(("l", jj, 0))
                            chunks.append(("l", jj + 1, None))
                            chunks.append(("l", jj + 2, 1))
                            chunks.append(("c", 0, None))
                            chunks.append(("c", 1, None))
                            bO, bD = ((4, 5), (6, 7))[(jj * 2 + r) % 2]
                            pend = None
                            nch = len(chunks)
                            for ci, (kind, slot, mki) in enumerate(chunks):
                                bS = bank()
                                psl = (bi * 100 + jj * 10 + r * 5 + ci) % 3

                                def f(e, kind=kind, slot=slot, mki=mki, bS=bS, jj=jj, sl=sl, pr=pr):
                                    if kind == "l":
                                        kk = kblk[sl][pr, slot * 128:(slot + 1) * 128]
                                    else:
                                        kk = kc_sb[pr, slot * 128:(slot + 1) * 128]
                                    i = e.matmul(ps[bS][:, :].rearrange("p (h t) -> p h t", t=128), kk, qblk[sl][pr, :, jj * 128:(jj + 1) * 128],
                                                 start=True, stop=(mki is None))
                                    if mki is not None:
                                        i = e.matmul(ps[bS][:, :].rearrange("p (h t) -> p h t", t=128), ident_bf[:],
                                                     nmk[:, mki, None, :].to_broadcast([128, 4, 128]), start=False, stop=True)
                                    return i
                                S.op("pe", f, reads=[f"kblk{sl}", f"qblk{sl}", "kc", "identbf", "nmk"], writes=[("ps", bS)])
                                S.op("act", lambda e, bS=bS, psl=psl: e.activation(out=pt[psl][:], in_=ps[bS][:, :], func=AF.Exp, scale=0.125),
                                     reads=[("ps", bS)], writes=[f"pt{psl}"])
                                cur = (kind, slot, psl, ci)
                                if pend is not None:
                                    _emit_pv(S, ps, pend, vblk[sl], vpc, ones_bf, pt, r, bO, bD, nch, f"vblk{sl}")
                                pend = cur
                            _emit_pv(S, ps, pend, vblk[sl], vpc, ones_bf, pt, r, bO, bD, nch, f"vblk{sl}")
                            S.op("dve", lambda e, pr=pr, r=r, bD=bD: e.tensor_tensor(
                                out=dsum[pr, :].rearrange("p (h t) -> p h t", t=128), in0=ps[bD][pr, :].rearrange("p (h t) -> p h t", t=128),
                                in1=es[pr, r * 4:(r + 1) * 4, None].to_broadcast([64, 4, 128]), op=ALU.add),
                                reads=[("ps", bD), "es"], writes=["dsum"])
                            S.op("dve", lambda e, pr=pr: e.reciprocal(out=rden[pr, :], in_=dsum[pr, :]), reads=["dsum"], writes=["rden"])
                            S.op("dve", lambda e, pr=pr, bO=bO, jj=jj, mx=mx: e.tensor_tensor(
                                out=mx[pr, 4:8, jj * 128:(jj + 1) * 128], in0=ps[bO][pr, :].rearrange("p (h t) -> p h t", t=128),
                                in1=rden[pr, :].rearrange("p (h t) -> p h t", t=128), op=ALU.mult),
                                reads=[("ps", bO), "rden"], writes=[mk])
                    out_proj_post(wout, 8, mx, mk, "wout", n, j0 * 128, 0, 2, ysb, "ysb")
                bcfg["mm"] = [0, 1, 2, 3, 4, 5]
                bcfg["st"] = [6, 7]
                S.barrier()

        def ffn(lay, ntok):
            with contextlib.ExitStack() as ph:
                hfs = [sb(f"hf{lay}_{i}", [128, 8, 512], BF16, stack=ph) for i in range(2)]
                hkeep = sb(f"hkeep{lay}", [128, 8, 2], BF16, stack=ph)
                wu = [sb(f"wu{lay}_{i}", [128, 8, 256], BF16, stack=ph) for i in range(3)]
                wd = [sb(f"wd{lay}_{i}", [128, NF, 128], BF16, stack=ph) for i in range(2)]
                tt_ = [[sb(f"tc{lay}_{i}_{k}", [128, 512], stack=ph) for k in range(2)] for i in range(2)]
                sg = [sb(f"sg{lay}_{i}", [128, 512], stack=ph) for i in range(2)]
                gbuf = sb(f"gb{lay}", [128, NF, 512], BF16, stack=ph)
                ysb = sb(f"ysbf{lay}", [128, 8, 512], stack=ph)
                nb = (ntok + FFN_BLK - 1) // FFN_BLK
                bsz = (ntok + nb - 1) // nb
                scrB = dict(sq=sb(f"sqB{lay}", [128, 8, 512], BF16, stack=ph), rt=sb(f"rtB{lay}", [128, 512], stack=ph),
                            rstd=sb(f"rstdB{lay}", [128, 512], stack=ph), k="B")

                def pre_block(bi):
                    s0 = bi * bsz
                    n = min(bsz, ntok - s0)
                    hf = hfs[bi % 2]
                    hfk = f"hf{bi % 2}"
                    if s0 == 0:
                        S.op("dve", lambda e: e.memset(hf[:, :, 0:1], 0.0), writes=[hfk])
                    else:
                        S.op("dve", lambda e: e.tensor_copy(hf[:, :, 0:1], hkeep[:, :, 0:1]), reads=["hkeep"], writes=[hfk])
                    prenorm(x_fm, s0, n + 1, lay, 3, 0, hf, 1, xk(s0, n + 1), hfk, scrB)
                    if bi < nb - 1:
                        S.op("dve", lambda e, n=n: e.tensor_copy(hkeep[:, :, 0:1], hf[:, :, n:n + 1]), reads=[hfk], writes=["hkeep"])
                wuc = 0
                wdc = 0
                for bi in range(nb):
                    s0 = bi * bsz
                    n = min(bsz, ntok - s0)
                    if bi == 0:
                        pre_block(0)
                    hf = hfs[bi % 2]
                    hfk = f"hf{bi % 2}"
                    def tail(fp):
                        hs = fp % 2
                        S.op("act", lambda e, hs=hs, n=n: e.activation(out=sg[hs][:, 0:n], in_=tt_[hs][0][:, 0:n], func=AF.Silu),
                             reads=[f"tc{hs}0"], writes=[f"sg{hs}"])
                        S.op("dve", lambda e, hs=hs, fp=fp, n=n: e.tensor_tensor(out=gbuf[:, fp, 0:n], in0=sg[hs][:, 0:n], in1=tt_[hs][1][:, 0:n], op=ALU.mult),
                             reads=[f"sg{hs}", f"tc{hs}1"], writes=["gbuf"])

                    for f_ in range(NF):
                        sl = wuc % 3
                        wuc += 1
                        S.dma("pool", f"wu{sl}", wu[sl][:], w_up[lay, f_], writes=[f"wu{sl}"])
                        hs = f_ % 2
                        for k in range(2):
                            b = bank()

                            def f(e, k=k, b=b, sl=sl, n=n, hf=hf):
                                for kt in range(8):
                                    i = e.matmul(ps[b][:, 0:n + 2], wu[sl][:, kt, k * 128:(k + 1) * 128], hf[:, kt, 0:n + 2], start=(kt == 0), stop=(kt == 7))
                                return i
                            S.op("pe", f, reads=[f"wu{sl}", hfk], writes=[("ps", b)])
                            hk = f"hu{hs}{k}"
                            tk = f"tc{hs}{k}"
                            tc = tt_[hs][k]
                            ft = f_ + k * NF
                            S.op("act", lambda e, b=b, tc=tc, ft=ft, n=n: e.activation(out=tc[:, 0:n], in_=ps[b][:, 1:n + 1], func=AF.Identity,
                                                                                     scale=cw[:, lay, ft, 1:2], bias=cb[:, lay, ft:ft + 1]),
                                 reads=[("ps", b), "cw", "cb"], writes=[tk])
                            S.op("dve", lambda e, b=b, tc=tc, ft=ft, n=n: e.scalar_tensor_tensor(
                                out=tc[:, 0:n], in0=ps[b][:, 0:n], scalar=cw[:, lay, ft, 0:1], in1=tc[:, 0:n], op0=ALU.mult, op1=ALU.add),
                                reads=[("ps", b), tk, "cw"], writes=[tk])
                            S.op("dve", lambda e, b=b, tc=tc, ft=ft, n=n: e.scalar_tensor_tensor(
                                out=tc[:, 0:n], in0=ps[b][:, 2:n + 2], scalar=cw[:, lay, ft, 2:3], in1=tc[:, 0:n], op0=ALU.mult, op1=ALU.add),
                                reads=[("ps", b), tk, "cw"], writes=[tk])
                        if f_ > 0:
                            tail(f_ - 1)
                        if f_ == 8 and bi + 1 < nb:
                            pre_block(bi + 1)
                    tail(NF - 1)
                    for d in range(8):
                        sl = wdc % 2
                        wdc += 1
                        S.dma("pool", f"wd{sl}", wd[sl][:], w_down[lay, d], writes=[f"wd{sl}"])
                        b = bank()

                        def f(e, b=b, sl=sl, n=n):
                            for k in range(NF):
                                i = e.matmul(ps[b][:, 0:n], wd[sl][:, k, :], gbuf[:, k, 0:n], start=(k == 0), stop=(k == NF - 1))
                            return i
                        S.op("pe", f, reads=[f"wd{sl}", "gbuf"], writes=[("ps", b)])
                        S.op("act", lambda e, d=d, b=b, n=n: e.activation(out=sq[:, d, 0:n], in_=ps[b][:, 0:n], func=AF.Square),
                             reads=[("ps", b)], writes=["sq"])
                        S.op("act", lambda e, d=d, b=b, n=n: e.activation(out=ysb[:, d, 0:n], in_=ps[b][:, 0:n], func=AF.Copy, scale=coef[:, lay, 5, d, 0:1]),
                             reads=[("ps", b), "coef"], writes=["ysbf"])
                    post_update(ysb, n, s0, "ysbf")
                S.barrier()

        if debug_stage not in ("p0", "p1", "p2"):
            ffn(0, N_L0_FFN)

        if debug_stage not in ("p0", "p1", "p2", "p3"):
            with contextlib.ExitStack() as ph:
                wio = sb("wio", [128, 8, 2048], BF16, stack=ph)
                wo1 = sb("wo1", [128, 8, D], BF16, stack=ph)
                lngb = sb("lngb", [128, 2, D], stack=ph)
                wst = sb("wst", [128, 8, 128], BF16, stack=ph)
                bsb = sb("bsb", [128, 8, 128], stack=ph)
                h1 = sb("h1", [128, 8, 512], BF16, stack=ph)
                usb = sb("usb", [128, 8, 512], BF16, stack=ph)
                vg = [sb(f"vg{i}", [128, D], stack=ph) for i in range(2)]
                vn = [sb(f"vn{i}", [128, D], BF16, stack=ph) for i in range(2)]
                stat = [sb(f"stat{i}", [128, 8], stack=ph) for i in range(2)]
                s_sb = [sb(f"s_sb{i}", [128, 512], stack=ph) for i in range(2)]
                gated = usb
                ysb = sb("ysb1", [128, 8, 512], stack=ph)
                for c in range(4):
                    S.dma("pool", "wio", wio[:, :, c * 512:(c + 1) * 512], w_in1[:, :, c * 512:(c + 1) * 512], writes=["wio"])
                for c in range(2):
                    S.dma("pool", "wio", wo1[:, :, c * 512:(c + 1) * 512], w_out1[:, :, c * 512:(c + 1) * 512], writes=["wo1"])
                S.dma("pool", "wio", wst[:], wsT, writes=["wst"])
                S.dma("sp", "c0", lngb[:], ln_gb, writes=["lngb"])
                S.dma("sp", "c0", bsb[:], bs_bc, writes=["bsb"])
                S.seal("c0", ["lngb", "bsb"])
                S.seal("wio", ["wio", "wo1", "wst"])
                ntile = N_L1_MIX // 128
                for j0 in range(0, ntile, 4):
                    nj = min(4, ntile - j0)
                    n = nj * 128
                    prenorm(x_fm, j0 * 128, n, 1, 0, 0, h1, 0, xk(j0 * 128, n), "h1")
                    for ft in range(8):
                        b = bank()

                        def f(e, ft=ft, b=b, n=n):
                            for kt in range(8):
                                i = e.matmul(ps[b][:, 0:n], wio[:, kt, ft * 128:(ft + 1) * 128], h1[:, kt, 0:n], start=(kt == 0), stop=(kt == 7))
                            return i
                        S.op("pe", f, reads=["wio", "h1"], writes=[("ps", b)])
                        S.op("act", lambda e, ft=ft, b=b, n=n: e.activation(out=usb[:, ft, 0:n], in_=ps[b][:, 0:n], func=AF.Gelu_apprx_tanh),
                             reads=[("ps", b)], writes=["usb"])
                    def stages(jj):
                        p = jj % 2
                        vg_, vn_, st_, ss_ = vg[p], vn[p], stat[p], s_sb[p]
                        kvg, kvn, kst, kss = f"vg{p}", f"vn{p}", f"stat{p}", f"s_sb{p}"
                        yield lambda: S.op("dve", lambda e: e.memset(st_[:], 0.0), writes=[kst])

                        def vproj():
                            for hh in range(2):
                                b = bank()

                                def f(e, hh=hh, b=b):
                                    for kt in range(8):
                                        i = e.matmul(ps[b][:, :], h1[:, kt, jj * 128:(jj + 1) * 128], wio[:, kt, 1024 + hh * 512:1536 + hh * 512], start=(kt == 0), stop=(kt == 7))
                                    return i
                                S.op("pe", f, reads=["wio", "h1"], writes=[("ps", b)])
                                S.op("act", lambda e, hh=hh, b=b: e.activation(out=vg_[:, hh * 512:(hh + 1) * 512], in_=ps[b][:, :], func=AF.Gelu_apprx_tanh,
                                                                               accum_out=st_[:, hh:hh + 1]),
                                     reads=[("ps", b)], writes=[kvg, kst])
                        yield vproj
                        yield lambda: S.op("dve", lambda e: e.tensor_tensor(out=st_[:, 2:3], in0=st_[:, 0:1], in1=st_[:, 1:2], op=ALU.add), reads=[kst], writes=[kst])
                        yield lambda: S.op("dve", lambda e: e.tensor_scalar(out=st_[:, 3:4], in0=st_[:, 2:3], scalar1=-1.0 / D, scalar2=None, op0=ALU.mult), reads=[kst], writes=[kst])
                        yield lambda: S.op("dve", lambda e: e.tensor_scalar(out=vg_[:], in0=vg_[:], scalar1=st_[:, 3:4], scalar2=None, op0=ALU.add),
                                           reads=[kvg, kst], writes=[kvg])
                        yield lambda: S.op("act", lambda e: e.activation(out=vn_[:], in_=vg_[:], func=AF.Square, accum_out=st_[:, 4:5]), reads=[kvg], writes=[kvn, kst])
                        yield lambda: S.op("act", lambda e: e.activation(out=st_[:, 5:6], in_=st_[:, 4:5], func=AF.Sqrt, scale=1.0 / D, bias=eps_t[:, 0:1]),
                                           reads=[kst, "eps"], writes=[kst])
                        yield lambda: S.op("dve", lambda e: e.reciprocal(out=st_[:, 6:7], in_=st_[:, 5:6]), reads=[kst], writes=[kst])
                        yield lambda: S.op("dve", lambda e: e.scalar_tensor_tensor(out=vg_[:], in0=vg_[:], scalar=st_[:, 6:7], in1=lngb[:, 0, :], op0=ALU.mult, op1=ALU.mult),
                                           reads=[kvg, kst, "lngb"], writes=[kvg])
                        yield lambda: S.op("pool", lambda e: e.tensor_tensor(out=vn_[:], in0=vg_[:], in1=lngb[:, 1, :], op=ALU.add), reads=[kvg, "lngb"], writes=[kvn])
                        for g4 in range(2):
                            def spat(g4=g4):
                                b = bank()

                                def f(e, b=b):
                                    for gg in range(4):
                                        gi = g4 * 4 + gg
                                        i = e.matmul(ps[b][:, gg * 128:(gg + 1) * 128], vn_[:, gi * 128:(gi + 1) * 128], wst[:, gi, :], start=True, stop=True)
                                    return i
                                S.op("pe", f, reads=[kvn, "wst"], writes=[("ps", b)])
                                S.op("dve", lambda e, b=b: e.tensor_tensor(out=ss_[:, :], in0=ps[b][:, :], in1=bsb[:, g4 * 4:(g4 + 1) * 4, :].rearrange("p g t -> p (g t)"), op=ALU.add),
                                     reads=[("ps", b), "bsb"], writes=[kss])
                            yield spat
                            yield lambda g4=g4: S.op("dve", lambda e: e.tensor_tensor(out=gated[:, g4 * 4:(g4 + 1) * 4, jj * 128:(jj + 1) * 128],
                                                                                    in0=ss_[:, :].rearrange("p (g t) -> p g t", t=128),
                                                                                    in1=usb[:, g4 * 4:(g4 + 1) * 4, jj * 128:(jj + 1) * 128], op=ALU.mult),
                                                     reads=[kss, "usb"], writes=["usb"])

                    for ja in range(0, nj, 2):
                        gens = [list(stages(jj)) for jj in range(ja, min(ja + 2, nj))]
                        for si in range(len(gens[0])):
                            for gl in gens:
                                gl[si]()
                    out_proj_post(wo1, 8, gated, "usb", "wo1", n, j0 * 128, 1, 2, ysb, "ysb1")
                S.barrier()

        if debug_stage not in ("p0", "p1", "p2", "p3", "p4"):
            ffn(1, TOWN)

        with contextlib.ExitStack() as ph:
            ost = [sb(f"ost{i}", [128, D], stack=ph) for i in range(2)]
            for i in range(TOWN // 128):
                sl = i % 2
                for hlf in range(2):
                    b = bank()

                    def f(e, hlf=hlf, b=b, i=i):
                        for c in range(4):
                            cc = hlf * 4 + c
                            i2 = e.transpose(ps[b][:, c * 128:(c + 1) * 128], x_fm[:, cc, i * 128:(i + 1) * 128], ident[:])
                        return i2
                    S.op("pe", f, reads=[("x", i), "ident"], writes=[("ps", b)])
                    if hlf == 0:
                        S.op("act", lambda e, b=b, sl=sl: e.activation(out=ost[sl][:, 0:512], in_=ps[b][:, :], func=AF.Copy), reads=[("ps", b)], writes=[f"ost{sl}"])
                    else:
                        S.op("dve", lambda e, b=b, sl=sl: e.tensor_copy(ost[sl][:, 512:1024], ps[b][:, :]), reads=[("ps", b)], writes=[f"ost{sl}"])
                S.dma("sp", f"ost{sl}", out_loc[i], ost[sl][:], reads=[f"ost{sl}"], writes=[f"out{i}"])
            S.barrier()
    return nc


def _emit_pv(S, ps, pend, vb, vpc, ones_bf, pt, r, bO, bD, nch, vkey):
    kind, slot, psl, ci = pend

    def f(e):
        vv = vb[:, slot, r, :] if kind == "l" else vpc[:, slot, r, :]
        e.matmul(ps[bO][:, :], vv, pt[psl][:], start=(ci == 0), stop=(ci == nch - 1))
        return e.matmul(ps[bD][:, :], ones_bf[:], pt[psl][:], start=(ci == 0), stop=(ci == nch - 1))
    S.op("pe", f, reads=[f"pt{psl}", vkey, "vpc", "ones"], writes=[("ps", bO), ("ps", bD)])


def _fm(v):
    v = np.asarray(v, np.float32)
    sh = v.shape
    v = v.reshape(sh[:-1] + (sh[-1] // 128, 128))
    return np.ascontiguousarray(np.moveaxis(v, -1, 0))


def _kt(w):
    K, C = w.shape
    return np.ascontiguousarray(w.reshape(K // 128, 128, C).transpose(1, 0, 2))


def _positions(hf):
    i = np.arange(T)
    return i if hf == 0 else (L - 1 - i)


def _rope_tables(pos):
    inv = (10000.0 ** (-np.arange(16, dtype=np.float32) / 16)).astype(np.float32)
    row = (pos // 64).astype(np.float32)
    col = (pos % 64).astype(np.float32)
    C = np.zeros((128, T), np.float32)
    Sg = np.zeros((128, T), np.float32)
    for rr_ in range(128):
        i = rr_ % 64
        axis, half, f = i // 32, (i % 32) // 16, i % 16
        ang = (row if axis == 0 else col) * inv[f]
        C[rr_] = np.cos(ang)
        Sg[rr_] = np.sin(ang) * (-1.0 if half == 0 else 1.0)
    return C, Sg


def _pool_mats(pos):
    M = np.zeros((128, 4, 4, 128), np.float32)
    for gi, w in enumerate(POOL_W):
        hw = w // 2
        for kind, (jo, dj) in enumerate(((2, -1), (2, 0), (2, 1), (0, 0))):
            ji = jo + dj
            for to in range(128):
                p = pos[jo * 128 + to]
                st_, en = max(p - hw, 0), min(p + hw, L)
                cnt = en - st_
                for ti in range(128):
                    q = pos[ji * 128 + ti]
                    v = 0.0
                    if st_ <= q < en:
                        v += 1.0 / cnt
                    if dj == 0 and ti == to:
                        v -= 1.0
                    M[ti, gi, kind, to] = v
    return M


_PROG = {}


def kernel(x, c, ctx, c_ctx, w_ada, b_ada, g_mix_pre, g_mix_post, g_ffn_pre, g_ffn_post,
           w_in_even, w_pool, pool_scale, attn_sink, w_out_even,
           w_in_odd, sgu_ln_g, sgu_ln_b, sgu_w, sgu_b, w_out_odd,
           w_ffn_up, ffn_conv_w, ffn_conv_b, w_ffn_down, _debug_stage=None):
    f32 = np.float32
    x = np.asarray(x, f32)
    ctx = np.asarray(ctx, f32)
    w_ada_l = np.ascontiguousarray(np.asarray(w_ada, f32).reshape(2, 8, 128, 12, 512).transpose(0, 3, 2, 1, 4))
    b_ada_l = np.ascontiguousarray(np.asarray(b_ada, f32).reshape(2, 48, 128).transpose(2, 0, 1))
    gvec = np.stack([_fm(np.asarray(a, f32)) for a in (g_mix_pre, g_mix_post, g_ffn_pre, g_ffn_post)], axis=0)
    gvec = np.ascontiguousarray(gvec.transpose(1, 2, 0, 3))
    wi = np.asarray(w_in_even, f32)[0]
    perm64 = np.array([(i // 32) * 32 + (1 - (i % 32) // 16) * 16 + (i % 16) for i in range(64)])
    qcols, qpcols = [], []
    for j in range(4):
        for h in (j, 4 + j):
            base = 512 + h * 64
            qcols += list(base + np.arange(64))
            qpcols += list(base + perm64)
    kcols = list(1024 + np.arange(128))
    kpcols = [1024 + hh * 64 + p for hh in range(2) for p in perm64]
    cols = qcols + kcols + qpcols + kpcols + list(range(0, 512)) + list(range(1152, 1280))
    w_in0 = _kt(wi[:, cols])
    w_pool_l = np.ascontiguousarray(np.asarray(w_pool, f32)[0].transpose(1, 0, 2))
    pool_sc = _fm(np.asarray(pool_scale, f32)[0])
    sink_bc = np.ascontiguousarray(np.broadcast_to(np.asarray(attn_sink, f32)[0][None, :], (128, 8)))
    nm = np.zeros((128, 2, 128), f32)
    kk = np.arange(128)[:, None]
    qq = np.arange(128)[None, :]
    nm[:, 0, :] = np.where(kk >= qq, 0.0, -30000.0)
    nm[:, 1, :] = np.where(kk <= qq, 0.0, -30000.0)
    rows = list(range(512))
    for j in range(4):
        rows += list(512 + j * 64 + np.arange(64)) + list(512 + (4 + j) * 64 + np.arange(64))
    w_out0 = _kt(np.asarray(w_out_even, f32)[0][rows, :])
    w_in1 = _kt(np.asarray(w_in_odd, f32)[0])
    ln_gb = np.ascontiguousarray(np.broadcast_to(np.stack([np.asarray(sgu_ln_g, f32)[0], np.asarray(sgu_ln_b, f32)[0]], 0)[None], (128, 2, D)))
    w_out1 = _kt(np.asarray(w_out_odd, f32)[0])
    wu = np.asarray(w_ffn_up, f32).reshape(2, 8, 128, 2, NF, 128)
    w_up_l = np.ascontiguousarray(wu.transpose(0, 4, 2, 1, 3, 5)).reshape(2, NF, 128, 8, 256)
    wd = np.asarray(w_ffn_down, f32).reshape(2, NF, 128, 8, 128)
    w_down_l = np.ascontiguousarray(wd.transpose(0, 3, 2, 1, 4))
    cwf = np.asarray(ffn_conv_w, f32).reshape(2, 3, 44, 128)
    conv_b_l = np.ascontiguousarray(np.asarray(ffn_conv_b, f32).reshape(2, 44, 128).transpose(2, 0, 1))
    sw = np.asarray(sgu_w, f32)[0]
    sbias = np.asarray(sgu_b, f32)[0]
    ident = np.eye(128, dtype=f32)

    per_half = []
    for hf in range(2):
        pos = _positions(hf)
        C, Sg = _rope_tables(pos)
        mp = _pool_mats(pos)
        cwl = cwf if hf == 0 else cwf[:, ::-1]
        conv_w_l = np.ascontiguousarray(cwl.transpose(3, 0, 2, 1))
        swl = sw if hf == 0 else sw[:, ::-1, ::-1]
        wsT = np.ascontiguousarray(swl.transpose(2, 0, 1))
        sbl = sbias if hf == 0 else sbias[:, ::-1]
        bs_bc = np.ascontiguousarray(np.broadcast_to(sbl[None], (128, 8, 128)))
        per_half.append(dict(pos=pos, ropeC=C, ropeS=Sg, mpool=mp, conv_w=conv_w_l, wsT=wsT, bs_bc=bs_bc))

    in_maps = []
    for core in range(8):
        b, hf = core // 2, core % 2
        ph = per_half[hf]
        cv = np.stack([np.asarray(c, f32)[b], np.asarray(c_ctx, f32)], axis=-1)
        in_maps.append(dict(
            x_loc=np.ascontiguousarray(x[b][ph["pos"]].reshape(NT, 128, D)),
            ctx_in=np.ascontiguousarray(ctx[b].reshape(2, 128, D)),
            cvec=np.ascontiguousarray(cv.reshape(8, 128, 2).transpose(1, 0, 2)),
            w_ada=w_ada_l, b_ada=b_ada_l, gvec=gvec, w_in0=w_in0, ropeC=ph["ropeC"], ropeS=ph["ropeS"],
            w_pool=w_pool_l, pool_sc=pool_sc, mpool=ph["mpool"], sink_bc=sink_bc, negmask=nm, ident_in=ident,
            w_out0=w_out0, w_in1=w_in1, ln_gb=ln_gb, wsT=ph["wsT"], bs_bc=ph["bs_bc"], w_out1=w_out1,
            w_up=w_up_l, w_down=w_down_l, conv_w=ph["conv_w"], conv_b=conv_b_l,
        ))
    key = _debug_stage
    if key not in _PROG:
        _PROG[key] = build_program(_debug_stage)
    res = run_bass_kernel_spmd(_PROG[key], in_maps, core_ids=list(range(8)))
    out = np.empty((4, L, D), f32)
    for core in range(8):
        b, hf = core // 2, core % 2
        o = np.asarray(res.results[core]["out_loc"], f32).reshape(TOWN, D)
        out[b, per_half[hf]["pos"][:TOWN]] = o
    return out
```

```python
import contextlib
import numpy as np
import concourse.bass as bass
import concourse.mybir as mybir
from concourse.bass_utils import run_bass_kernel_spmd

F32 = mybir.dt.float32
BF16 = mybir.dt.bfloat16
AF = mybir.ActivationFunctionType
ALU = mybir.AluOpType

D = 1024
L = 4096
NT = 19
T = NT * 128
TOWN = 2048
CTX = 256
DFF = 2816
NF = 22
EPS = 1e-6
POOL_W = (2, 4, 8, 16)
N_L0_MIX = 18 * 128
N_L0_FFN = 17 * 128
N_L1_MIX = 17 * 128
FFN_BLK = 510


class Sched:
    def __init__(self, nc, st):
        self.nc = nc
        self.st = st
        self.eng = dict(pe=nc.tensor, act=nc.scalar, dve=nc.vector, pool=nc.gpsimd, sp=nc.sync)
        self.sem = {k: st.enter_context(nc.semaphore("sem_" + k)) for k in ("pe", "act", "dve", "pool")}
        self.cnt = {k: 0 for k in self.sem}
        self.dsem = {}
        self.dcnt = {}
        self.waited = {k: {} for k in self.eng}
        self.lastw = {}
        self.readers = {}

    def _deps(self, reads, writes):
        deps = {}

        def add(k, v):
            if deps.get(k, 0) < v:
                deps[k] = v

        for r in reads:
            t = self.lastw.get(r)
            if t:
                add(*t)
        for w in writes:
            t = self.lastw.get(w)
            if t:
                add(*t)
            for k, v in self.readers.get(w, {}).items():
                add(k, v)
        return deps

    def _semof(self, k):
        return self.sem[k] if k in self.sem else self.dsem[k]

    def _wait(self, eng, deps):
        e = self.eng[eng]
        for k, v in deps.items():
            if eng == "pe" and k == "pe":
                continue
            if self.waited[eng].get(k, 0) >= v:
                continue
            e.wait_ge(self._semof(k), v)
            self.waited[eng][k] = v

    def _commit(self, tok, reads, writes):
        k, v = tok
        for r in reads:
            d = self.readers.setdefault(r, {})
            if d.get(k, 0) < v:
                d[k] = v
        for w in writes:
            self.lastw[w] = tok
            self.readers[w] = {}

    def op(self, eng, fn, reads=(), writes=()):
        self._wait(eng, self._deps(reads, writes))
        inst = fn(self.eng[eng])
        self.cnt[eng] += 1
        inst.then_inc(self.sem[eng], 1)
        self._commit((eng, self.cnt[eng]), reads, writes)

    def dma(self, q, key, out, in_, reads=(), writes=()):
        if key not in self.dsem:
            self.dsem[key] = self.st.enter_context(self.nc.semaphore("dsem_" + key))
            self.dcnt[key] = 0
        self._wait(q, self._deps(reads, writes))
        self.eng[q].dma_start(out=out, in_=in_).then_inc(self.dsem[key], 16)
        self.dcnt[key] += 16
        self._commit((key, self.dcnt[key]), reads, writes)

    def seal(self, key, resources):
        for r in resources:
            self.lastw[r] = (key, self.dcnt[key])

    def barrier(self, engines=("pe", "act", "dve", "pool", "sp")):
        for e in engines:
            deps = {k: v for k, v in self.cnt.items() if v > 0}
            deps.update({k: v for k, v in self.dcnt.items() if v > 0})
            deps.pop(e, None) if e == "pe" else None
            self._wait(e, deps)


class Ctx:
    pass


def build_program(debug_stage=None):
    nc = bass.Bass("TRN2", target_bir_lowering=False)
    g = Ctx()

    def din(name, shape, dt=F32):
        return nc.dram_tensor(name, list(shape), dt, kind="ExternalInput").ap()

    x_loc = din("x_loc", [NT, 128, D])
    ctx_in = din("ctx_in", [2, 128, D])
    cvec = din("cvec", [128, 8, 2])
    w_ada = din("w_ada", [2, 12, 128, 8, 512])
    b_ada = din("b_ada", [128, 2, 48])
    gvec = din("gvec", [128, 2, 4, 8])
    w_in0 = din("w_in0", [128, 8, 1920])
    ropeC = din("ropeC", [128, T])
    ropeS = din("ropeS", [128, T])
    w_pool = din("w_pool", [128, 4, 128])
    pool_sc = din("pool_sc", [128, 4])
    mpool = din("mpool", [128, 4, 4, 128])
    sink_bc = din("sink_bc", [128, 8])
    negmask = din("negmask", [128, 2, 128])
    ident_in = din("ident_in", [128, 128])
    w_out0 = din("w_out0", [128, 8, D])
    w_in1 = din("w_in1", [128, 8, 2048])
    ln_gb = din("ln_gb", [128, 2, D])
    wsT = din("wsT", [128, 8, 128])
    bs_bc = din("bs_bc", [128, 8, 128])
    w_out1 = din("w_out1", [128, 8, D])
    w_up = din("w_up", [2, NF, 128, 8, 256])
    w_down = din("w_down", [2, 8, 128, NF, 128])
    conv_w = din("conv_w", [128, 2, 44, 3])
    conv_b = din("conv_b", [128, 2, 44])
    out_loc = nc.dram_tensor("out_loc", [TOWN // 128, 128, D], F32, kind="ExternalOutput").ap()
    qk_d = nc.dram_tensor("qk_d", [5, 128, T], BF16).ap()
    u_d = nc.dram_tensor("u_d", [NT, 128, 512], BF16).ap()
    v_d = nc.dram_tensor("v_d", [NT, 128, 128], BF16).ap()

    with contextlib.ExitStack() as st:
        E = st.enter_context
        S = Sched(nc, st)

        def sb(name, shape, dt=F32, stack=None):
            return (stack or st).enter_context(nc.sbuf_tensor(name, list(shape), dt))

        ps = [E(nc.psum_tensor(f"ps{i}", [128, 512], F32)) for i in range(8)]
        rr = {"mm": 0, "st": 0}

        bcfg = {"mm": [0, 1, 2, 3, 4, 5], "st": [6, 7]}

        def bank(pool="mm"):
            lst = bcfg[pool]
            ctr = "mm" if bcfg["st"] is bcfg["mm"] else pool
            i = lst[rr[ctr] % len(lst)]
            rr[ctr] += 1
            return i

        x_fm = sb("x_fm", [128, 8, T])
        ident = sb("ident", [128, 128])
        ident_bf = sb("ident_bf", [128, 128], BF16)
        ones_bf = sb("ones_bf", [128, 128], BF16)
        coef = sb("coef", [128, 2, 6, 8, 2])
        gv = sb("gv", [128, 2, 4, 8])
        cw = sb("cw", [128, 2, 44, 3])
        cb = sb("cb", [128, 2, 44])
        eps_t = sb("eps_t", [128, 1])
        sq = sb("sq", [128, 8, 512], BF16)
        rt = sb("rt", [128, 512])
        rstd = sb("rstd", [128, 512])
        tn = [sb(f"tn{i}", [128, 512]) for i in range(2)]
        kc_sb = sb("kc_sb", [128, CTX], BF16)
        vpc = sb("vpc", [128, 2, 2, 128], BF16)

        cv = sb("cv", [128, 8, 2])
        bada = sb("bada", [128, 2, 48])
        S.dma("sp", "c0", ident[:], ident_in, writes=["ident"])
        S.dma("sp", "c0", gv[:], gvec, writes=["gv"])
        S.dma("sp", "c0", cw[:], conv_w, writes=["cw"])
        S.dma("sp", "c0", cb[:], conv_b, writes=["cb"])
        S.dma("sp", "c0", cv[:], cvec, writes=["cv"])
        S.dma("sp", "c0", bada[:], b_ada, writes=["bada"])
        S.seal("c0", ["ident", "gv", "cw", "cb", "cv", "bada"])
        S.op("dve", lambda e: e.memset(ones_bf[:], 1.0), writes=["ones"])
        S.op("dve", lambda e: e.memset(eps_t[:], EPS), writes=["eps"])
        S.op("dve", lambda e: e.tensor_copy(ident_bf[:], ident[:]), reads=["ident"], writes=["identbf"])

        def xk(c0, n):
            return [("x", t) for t in range(c0 // 128, (c0 + n - 1) // 128 + 1)]

        scrA = dict(sq=sq, rt=rt, rstd=rstd, k="")

        def prenorm(src, c0, n, lay, ka, col, dst, doff, srckey, dstkey, scr=None):
            scr = scr or scrA
            sq, rt, rstd, sk = scr["sq"], scr["rt"], scr["rstd"], scr["k"]
            S.op("act", lambda e: e.activation(out=sq[:, :, 0:n], in_=src[:, :, c0:c0 + n], func=AF.Square),
                 reads=srckey, writes=["sq" + sk])
            b = bank("st")

            def f(e):
                for kt in range(8):
                    i = e.matmul(ps[b][:, 0:n], ones_bf[:], sq[:, kt, 0:n], start=(kt == 0), stop=(kt == 7))
                return i
            S.op("pe", f, reads=["sq" + sk, "ones"], writes=[("ps", b)])
            S.op("act", lambda e: e.activation(out=rt[:, 0:n], in_=ps[b][:, 0:n], func=AF.Sqrt, scale=1.0 / D, bias=eps_t[:, 0:1]),
                 reads=[("ps", b), "eps"], writes=["rt" + sk])
            S.op("dve", lambda e: e.reciprocal(out=rstd[:, 0:n], in_=rt[:, 0:n]), reads=["rt" + sk], writes=["rstd" + sk])
            for kt in range(8):
                ts = kt % 2
                S.op("dve", lambda e, kt=kt, ts=ts: e.tensor_tensor(out=tn[ts][:, 0:n], in0=src[:, kt, c0:c0 + n], in1=rstd[:, 0:n], op=ALU.mult),
                     reads=srckey + ["rstd" + sk], writes=[f"tn{ts}"])
                S.op("act", lambda e, kt=kt, ts=ts: e.activation(out=dst[:, kt, doff:doff + n], in_=tn[ts][:, 0:n], func=AF.Identity,
                                                                 scale=coef[:, lay, ka, kt, col:col + 1], bias=coef[:, lay, ka + 1, kt, col:col + 1]),
                     reads=[f"tn{ts}", "coef"], writes=[dstkey])

        def post_update(ysb, n, c0, ykey):
            b = bank("st")

            def f(e):
                for kt in range(8):
                    i = e.matmul(ps[b][:, 0:n], ones_bf[:], sq[:, kt, 0:n], start=(kt == 0), stop=(kt == 7))
                return i
            S.op("pe", f, reads=["sq", "ones"], writes=[("ps", b)])
            S.op("act", lambda e: e.activation(out=rt[:, 0:n], in_=ps[b][:, 0:n], func=AF.Sqrt, scale=1.0 / D, bias=eps_t[:, 0:1]),
                 reads=[("ps", b), "eps"], writes=["rt"])
            S.op("dve", lambda e: e.reciprocal(out=rstd[:, 0:n], in_=rt[:, 0:n]), reads=["rt"], writes=["rstd"])
            S.op("dve", lambda e: e.tensor_tensor(out=ysb[:, :, 0:n], in0=ysb[:, :, 0:n],
                                                  in1=rstd[:, None, 0:n].to_broadcast([128, 8, n]), op=ALU.mult),
                 reads=[ykey, "rstd"], writes=[ykey])
            S.op("dve", lambda e: e.tensor_tensor(out=x_fm[:, :, c0:c0 + n], in0=x_fm[:, :, c0:c0 + n], in1=ysb[:, :, 0:n], op=ALU.add),
                 reads=[ykey] + xk(c0, n), writes=xk(c0, n))

        def out_proj_post(W, nk, rhs, rhskey, wkey, n, c0, lay, kg, ysb, ykey):
            for d in range(8):
                b = bank()

                def f(e, d=d, b=b):
                    for k in range(nk):
                        i = e.matmul(ps[b][:, 0:n], W[:, k, d * 128:(d + 1) * 128], rhs[:, k, 0:n], start=(k == 0), stop=(k == nk - 1))
                    return i
                S.op("pe", f, reads=[rhskey, wkey], writes=[("ps", b)])
                S.op("act", lambda e, d=d, b=b: e.activation(out=sq[:, d, 0:n], in_=ps[b][:, 0:n], func=AF.Square),
                     reads=[("ps", b)], writes=["sq"])
                S.op("act", lambda e, d=d, b=b: e.activation(out=ysb[:, d, 0:n], in_=ps[b][:, 0:n], func=AF.Copy,
                                                             scale=coef[:, lay, kg, d, 0:1]),
                     reads=[("ps", b), "coef"], writes=[ykey])
            post_update(ysb, n, c0, ykey)

        ph01 = contextlib.ExitStack()
        g.ctx_fm = sb("ctx_fm", [128, 8, CTX], stack=ph01)
        wout = sb("wout", [128, 8, D], BF16, stack=ph01)
        wpl = sb("wpl", [128, 4, 128], BF16, stack=ph01)
        mpl = sb("mpl", [128, 4, 4, 128], BF16, stack=ph01)
        nmk = sb("nmk", [128, 2, 128], BF16, stack=ph01)
        with contextlib.ExitStack() as ph:
            cs = sb("cs", [128, 8, 2], BF16, stack=ph)
            wa = [sb(f"wa{i}", [128, 8, 512], BF16, stack=ph) for i in range(3)]
            modsb = sb("modsb", [128, 2, 6, 8, 2], stack=ph)
            xs = [sb(f"xs{i}", [128, D], stack=ph) for i in range(2)]
            S.op("act", lambda e: e.activation(out=cs[:], in_=cv[:], func=AF.Silu), reads=["cv"], writes=["cs"])
            for lay in range(2):
                bm = bank("st")
                for ch in range(12):
                    sl = (lay * 12 + ch) % 3
                    S.dma("pool", f"wa{sl}", wa[sl][:], w_ada[lay, ch], writes=[f"wa{sl}"])

                    def f(e, ch=ch, sl=sl, bm=bm):
                        for ft in range(4):
                            o = (ch * 4 + ft) * 2
                            for kt in range(8):
                                i = e.matmul(ps[bm][:, o:o + 2], wa[sl][:, kt, ft * 128:(ft + 1) * 128], cs[:, kt, :], start=(kt == 0), stop=(kt == 7))
                        return i
                    S.op("pe", f, reads=[f"wa{sl}", "cs"], writes=[("ps", bm)])
                S.op("dve", lambda e, lay=lay, bm=bm: e.tensor_tensor(
                    out=modsb[:, lay].rearrange("p j c t -> p (j c) t"),
                    in0=ps[bm][:, 0:96].rearrange("p (a t) -> p a t", t=2),
                    in1=bada[:, lay, :, None].to_broadcast([128, 48, 2]), op=ALU.add),
                    reads=[("ps", bm), "bada"], writes=["modsb"])
                for (ka, jsc, jsh, jgt, gpre, gpost) in ((0, 1, 0, 2, 0, 1), (3, 4, 3, 5, 2, 3)):
                    S.op("dve", lambda e, lay=lay, ka=ka, jsc=jsc: e.tensor_scalar(out=coef[:, lay, ka], in0=modsb[:, lay, jsc], scalar1=1.0, scalar2=None, op0=ALU.add),
                         reads=["modsb"], writes=["coef"])
                    S.op("dve", lambda e, lay=lay, ka=ka, gpre=gpre: e.tensor_tensor(out=coef[:, lay, ka], in0=coef[:, lay, ka],
                                                                                   in1=gv[:, lay, gpre, :, None].to_broadcast([128, 8, 2]), op=ALU.mult),
                         reads=["coef", "gv"], writes=["coef"])
                    S.op("dve", lambda e, lay=lay, ka=ka, jsh=jsh: e.tensor_copy(coef[:, lay, ka + 1], modsb[:, lay, jsh]),
                         reads=["modsb"], writes=["coef"])
                    S.op("dve", lambda e, lay=lay, ka=ka, jgt=jgt, gpost=gpost: e.tensor_tensor(
                        out=coef[:, lay, ka + 2], in0=modsb[:, lay, jgt], in1=gv[:, lay, gpost, :, None].to_broadcast([128, 8, 2]), op=ALU.mult),
                        reads=["modsb", "gv"], writes=["coef"])

            def load_T(src_ap, dst, t0, i, dkey):
                sl = i % 2
                S.dma("sp", f"xs{sl}", xs[sl][:], src_ap, writes=[f"xs{sl}"])
                for hlf in range(2):
                    b = bank()

                    def f(e, hlf=hlf, b=b, sl=sl):
                        for c in range(4):
                            cc = hlf * 4 + c
                            i2 = e.transpose(ps[b][:, c * 128:(c + 1) * 128], xs[sl][:, cc * 128:(cc + 1) * 128], ident[:])
                        return i2
                    S.op("pe", f, reads=[f"xs{sl}", "ident"], writes=[("ps", b)])
                    eng = "act" if hlf == 0 else "dve"
                    if eng == "act":
                        S.op("act", lambda e, hlf=hlf, b=b: e.activation(out=dst[:, hlf * 4:hlf * 4 + 4, t0:t0 + 128],
                                                                         in_=ps[b][:, :].rearrange("p (c t) -> p c t", t=128), func=AF.Copy),
                             reads=[("ps", b)], writes=[dkey])
                    else:
                        S.op("dve", lambda e, hlf=hlf, b=b: e.tensor_copy(dst[:, hlf * 4:hlf * 4 + 4, t0:t0 + 128],
                                                                          ps[b][:, :].rearrange("p (c t) -> p c t", t=128)),
                             reads=[("ps", b)], writes=[dkey])
            for i in range(NT):
                load_T(x_loc[i], x_fm, i * 128, i, ("x", i))
            for i in range(2):
                load_T(ctx_in[i], g.ctx_fm, i * 128, NT + i, "ctx")
            S.barrier()

        if debug_stage == "p0":
            pass
        if debug_stage not in ("p0",):
            with contextlib.ExitStack() as ph:
                win = sb("win", [128, 8, 1920], BF16, stack=ph)
                hb = [sb(f"hb{i}", [128, 8, 512], BF16, stack=ph) for i in range(2)]
                rcf = sb("rcf", [128, 2, T], stack=ph)
                t1s = [sb(f"t1_{i}", [128, 512], stack=ph) for i in range(2)]
                t2s = [sb(f"t2_{i}", [128, 512], stack=ph) for i in range(2)]
                qst = [sb(f"qst{i}", [128, 5, 512], BF16, stack=ph) for i in range(1)] * 2
                ust = [sb(f"ust{i}", [128, 512], BF16, stack=ph) for i in range(2)]
                vst = [sb(f"vst{i}", [128, 128], BF16, stack=ph) for i in range(2)]
                for c in range(4):
                    S.dma("pool", "win", win[:, :, c * 480:(c + 1) * 480], w_in0[:, :, c * 480:(c + 1) * 480], writes=["win"])
                for c in range(2):
                    S.dma("pool", "wout", wout[:, :, c * 512:(c + 1) * 512], w_out0[:, :, c * 512:(c + 1) * 512], writes=["wout"])
                S.dma("pool", "wout", wpl[:], w_pool, writes=["wpl"])
                S.dma("pool", "wout", mpl[:], mpool, writes=["mpl"])
                S.dma("pool", "wout", nmk[:], negmask, writes=["nmk"])
                S.seal("wout", ["wout", "wpl", "mpl", "nmk"])
                S.dma("sp", "rcf", rcf[:, 0, :], ropeC, writes=["rcf"])
                S.dma("sp", "rcf", rcf[:, 1, :], ropeS, writes=["rcf"])
                S.op("dve", lambda e: e.memset(vpc[:], 0.0), writes=["vpc"])
                nblk = (T + 511) // 512
                for bi in range(nblk):
                    c0 = bi * 512
                    n = min(512, T - c0)
                    sl = bi % 2
                    h = hb[sl]
                    hk = f"hb{sl}"
                    if bi == 0:
                        prenorm(x_fm, c0, n, 0, 0, 0, h, 0, xk(c0, n), hk)
                    for j in range(5):
                        if j == 2 and bi + 1 < nblk:
                            c1 = (bi + 1) * 512
                            n1 = min(512, T - c1)
                            prenorm(x_fm, c1, n1, 0, 0, 0, hb[1 - sl], 0, xk(c1, n1), f"hb{1 - sl}")
                        ba, bb = bank(), bank()
                        t1, t2 = t1s[j % 2], t2s[j % 2]
                        k1, k2 = f"t1_{j % 2}", f"t2_{j % 2}"

                        def f(e, j=j, ba=ba, bb=bb, h=h, n=n):
                            for kt in range(8):
                                e.matmul(ps[ba][:, 0:n], win[:, kt, j * 128:(j + 1) * 128], h[:, kt, 0:n], start=(kt == 0), stop=(kt == 7))
                            for kt in range(8):
                                i = e.matmul(ps[bb][:, 0:n], win[:, kt, (5 + j) * 128:(6 + j) * 128], h[:, kt, 0:n], start=(kt == 0), stop=(kt == 7))
                            return i
                        S.op("pe", f, reads=["win", hk], writes=[("ps", ba), ("ps", bb)])
                        S.op("dve", lambda e, ba=ba, c0=c0, n=n, t1=t1: e.tensor_tensor(out=t1[:, 0:n], in0=ps[ba][:, 0:n], in1=rcf[:, 0, c0:c0 + n], op=ALU.mult),
                             reads=[("ps", ba), "rcf"], writes=[k1])
                        S.op("dve", lambda e, bb=bb, c0=c0, n=n, t2=t2: e.tensor_tensor(out=t2[:, 0:n], in0=ps[bb][:, 0:n], in1=rcf[:, 1, c0:c0 + n], op=ALU.mult),
                             reads=[("ps", bb), "rcf"], writes=[k2])
                        S.op("pool", lambda e, j=j, sl=sl, n=n, t1=t1, t2=t2: e.tensor_tensor(out=qst[sl][:, j, 0:n], in0=t1[:, 0:n], in1=t2[:, 0:n], op=ALU.add),
                             reads=[k1, k2], writes=["qst0"])
                    S.dma("sp", "qst0", qk_d[:, :, c0:c0 + n].rearrange("j p t -> p j t"), qst[sl][:, :, 0:n], reads=["qst0"], writes=["qk_d"])
                    for tt in range(n // 128):
                        ti = bi * 4 + tt
                        s2 = ti % 2
                        bu, bv = bank(), bank()

                        def f(e, tt=tt, bu=bu, bv=bv, h=h):
                            for kt in range(8):
                                e.matmul(ps[bu][:, :], h[:, kt, tt * 128:(tt + 1) * 128], win[:, kt, 1280:1792], start=(kt == 0), stop=(kt == 7))
                            for kt in range(8):
                                i = e.matmul(ps[bv][:, 0:128], h[:, kt, tt * 128:(tt + 1) * 128], win[:, kt, 1792:1920], start=(kt == 0), stop=(kt == 7))
                            return i
                        S.op("pe", f, reads=["win", hk], writes=[("ps", bu), ("ps", bv)])
                        S.op("act", lambda e, bu=bu, s2=s2: e.activation(out=ust[s2][:], in_=ps[bu][:, :], func=AF.Copy),
                             reads=[("ps", bu)], writes=[f"ust{s2}"])
                        S.op("act", lambda e, bv=bv, s2=s2: e.activation(out=vst[s2][:], in_=ps[bv][:, 0:128], func=AF.Copy),
                             reads=[("ps", bv)], writes=[f"vst{s2}"])
                        S.dma("sp", f"ust{s2}", u_d[ti], ust[s2][:], reads=[f"ust{s2}"], writes=["u_d"])
                        S.dma("sp", f"vst{s2}", v_d[ti], vst[s2][:], reads=[f"vst{s2}"], writes=["v_d"])
                h = hb[0]
                prenorm(g.ctx_fm, 0, CTX, 0, 0, 1, h, 0, ["ctx"], "hb0")
                bk = bank()

                def f(e, bk=bk, h=h):
                    for kt in range(8):
                        i = e.matmul(ps[bk][:, 0:CTX], win[:, kt, 4 * 128:5 * 128], h[:, kt, 0:CTX], start=(kt == 0), stop=(kt == 7))
                    return i
                S.op("pe", f, reads=["win", "hb0"], writes=[("ps", bk)])
                S.op("act", lambda e, bk=bk: e.activation(out=kc_sb[:], in_=ps[bk][:, 0:CTX], func=AF.Copy), reads=[("ps", bk)], writes=["kc"])
                for tt in range(2):
                    bv = bank()

                    def f(e, tt=tt, bv=bv, h=h):
                        for kt in range(8):
                            i = e.matmul(ps[bv][:, 0:128], h[:, kt, tt * 128:(tt + 1) * 128], win[:, kt, 1792:1920], start=(kt == 0), stop=(kt == 7))
                        return i
                    S.op("pe", f, reads=["win", "hb0"], writes=[("ps", bv)])
                    S.op("act", lambda e, tt=tt, bv=bv: e.activation(out=vpc[:, tt, 0, 0:64], in_=ps[bv][:, 0:64], func=AF.Copy),
                         reads=[("ps", bv)], writes=["vpc"])
                    S.op("act", lambda e, tt=tt, bv=bv: e.activation(out=vpc[:, tt, 1, 64:128], in_=ps[bv][:, 64:128], func=AF.Copy),
                         reads=[("ps", bv)], writes=["vpc"])
                S.barrier()

        if debug_stage not in ("p0", "p1"):
            with contextlib.ExitStack() as ph:
                psc = sb("psc", [128, 4], stack=ph)
                snk = sb("snk", [128, 8], stack=ph)
                es = sb("es", [128, 8], stack=ph)
                ublk = [sb(f"ublk{i}", [128, 6, 512], BF16, stack=ph) for i in range(1)] * 2
                kblk = [sb(f"kblk{i}", [128, 6 * 128], BF16, stack=ph) for i in range(2)]
                qblk = [sb(f"qblk{i}", [128, 4, 512], BF16, stack=ph) for i in range(2)]
                vblk = [sb(f"vblk{i}", [128, 6, 2, 128], BF16, stack=ph) for i in range(2)]
                pooled = sb("pooled", [128, 4, 512], BF16, stack=ph)
                mix = [sb(f"mix{i}", [128, 8, 512], BF16, stack=ph) for i in range(1)] * 2
                pt = [sb(f"pt{i}", [128, 512], BF16, stack=ph) for i in range(3)]
                dsum = sb("dsum", [128, 512], stack=ph)
                rden = sb("rden", [128, 512], stack=ph)
                ysb = sb("ysb", [128, 8, 512], stack=ph)
                S.dma("sp", "c0", psc[:], pool_sc, writes=["psc"])
                S.dma("sp", "c0", snk[:], sink_bc, writes=["snk"])
                S.seal("c0", ["psc", "snk"])
                S.op("act", lambda e: e.activation(out=es[:], in_=snk[:], func=AF.Exp), reads=["snk"], writes=["es"])
                for i in range(2):
                    S.op("dve", lambda e, i=i: e.memset(vblk[i][:], 0.0), writes=[f"vblk{i}"])
                nqt = 18
                bcfg["mm"] = bcfg["st"] = [0, 1, 2, 3]
                blocks = [(j0, min(4, nqt - j0)) for j0 in range(0, nqt, 4)]
                for bi, (j0, nj) in enumerate(blocks):
                    sl = bi % 2
                    n = nj * 128
                    lo = max(j0 - 1, 0)
                    hi = j0 + nj
                    off = lo - (j0 - 1)
                    nl = hi - lo + 1
                    S.dma("sp", "ublk0", ublk[sl][:, off:off + nl, :], u_d[lo:hi + 1].rearrange("t p c -> p t c"),
                          reads=["u_d"], writes=["ublk0"])
                    S.dma("sp", f"kblk{sl}", kblk[sl][:, off * 128:(off + nl) * 128], qk_d[4, :, lo * 128:(hi + 1) * 128],
                          reads=["qk_d"], writes=[f"kblk{sl}"])
                    S.dma("sp", f"qblk{sl}", qblk[sl][:, :, 0:n], qk_d[0:4, :, j0 * 128:j0 * 128 + n].rearrange("j p t -> p j t"),
                          reads=["qk_d"], writes=[f"qblk{sl}"])
                    S.dma("sp", f"vblk{sl}", vblk[sl][:, off:off + nl, 0, 0:64], v_d[lo:hi + 1, :, 0:64].rearrange("t p c -> p t c"),
                          reads=["v_d"], writes=[f"vblk{sl}"])
                    S.dma("sp", f"vblk{sl}", vblk[sl][:, off:off + nl, 1, 64:128], v_d[lo:hi + 1, :, 64:128].rearrange("t p c -> p t c"),
                          reads=["v_d"], writes=[f"vblk{sl}"])
                    mx = mix[sl]
                    mk = "mix0"
                    pb = [bank() for _ in range(4)]
                    for gi in range(4):
                        b = pb[gi]

                        def f(e, gi=gi, b=b, j0=j0, nj=nj, sl=sl):
                            for jj in range(nj):
                                j = j0 + jj
                                terms = []
                                if j > 0:
                                    terms.append((jj, 0))
                                terms.append((jj + 1, 3 if j == 0 else 1))
                                terms.append((jj + 2, 2))
                                for ti, (slot, kind) in enumerate(terms):
                                    i = e.matmul(ps[b][:, jj * 128:(jj + 1) * 128], ublk[sl][:, slot, gi * 128:(gi + 1) * 128], mpl[:, gi, kind, :],
                                                 start=(ti == 0), stop=(ti == len(terms) - 1))
                            return i
                        S.op("pe", f, reads=["ublk0", "mpl"], writes=[("ps", b)])
                    for gi in range(4):
                        S.op("act", lambda e, gi=gi, b=pb[gi], n=n: e.activation(out=pooled[:, gi, 0:n], in_=ps[b][:, 0:n], func=AF.Copy),
                             reads=[("ps", pb[gi])], writes=[("pooled", gi)])
                    pb2 = [bank() for _ in range(4)]
                    for gi in range(4):
                        S.op("pe", lambda e, gi=gi, b2=pb2[gi], n=n: e.matmul(ps[b2][:, 0:n], wpl[:, gi, :], pooled[:, gi, 0:n], start=True, stop=True),
                             reads=[("pooled", gi), "wpl"], writes=[("ps", pb2[gi])])
                    for gi in range(4):
                        S.op("act", lambda e, gi=gi, b2=pb2[gi], n=n, mx=mx: e.activation(out=mx[:, gi, 0:n], in_=ps[b2][:, 0:n], func=AF.Copy, scale=psc[:, gi:gi + 1]),
                             reads=[("ps", pb2[gi]), "psc"], writes=[mk])
                    for jj in range(nj):
                        j = j0 + jj
                        for r in range(2):
                            pr = slice(r * 64, (r + 1) * 64)
                            chunks = []
                            if j > 0:
                                chunks.append(("l", jj, 0))
                            chunks.append(("l", jj + 1, None))
                            chunks.append(("l", jj + 2, 1))
                            chunks.append(("c", 0, None))
                            chunks.append(("c", 1, None))
                            bO, bD = ((4, 5), (6, 7))[(jj * 2 + r) % 2]
                            pend = None
                            nch = len(chunks)
                            for ci, (kind, slot, mki) in enumerate(chunks):
                                bS = bank()
                                psl = (bi * 100 + jj * 10 + r * 5 + ci) % 3

                                def f(e, kind=kind, slot=slot, mki=mki, bS=bS, jj=jj, sl=sl, pr=pr):
                                    if kind == "l":
                                        kk = kblk[sl][pr, slot * 128:(slot + 1) * 128]
                                    else:
                                        kk = kc_sb[pr, slot * 128:(slot + 1) * 128]
                                    i = e.matmul(ps[bS][:, :].rearrange("p (h t) -> p h t", t=128), kk, qblk[sl][pr, :, jj * 128:(jj + 1) * 128],
                                                 start=True, stop=(mki is None))
                                    if mki is not None:
                                        i = e.matmul(ps[bS][:, :].rearrange("p (h t) -> p h t", t=128), ident_bf[:],
                                                     nmk[:, mki, None, :].to_broadcast([128, 4, 128]), start=False, stop=True)
                                    return i
                                S.op("pe", f, reads=[f"kblk{sl}", f"qblk{sl}", "kc", "identbf", "nmk"], writes=[("ps", bS)])
                                S.op("act", lambda e, bS=bS, psl=psl: e.activation(out=pt[psl][:], in_=ps[bS][:, :], func=AF.Exp, scale=0.125),
                                     reads=[("ps", bS)], writes=[f"pt{psl}"])
                                cur = (kind, slot, psl, ci)
                                if pend is not None:
                                    _emit_pv(S, ps, pend, vblk[sl], vpc, ones_bf, pt, r, bO, bD, nch, f"vblk{sl}")
                                pend = cur
                            _emit_pv(S, ps, pend, vblk[sl], vpc, ones_bf, pt, r, bO, bD, nch, f"vblk{sl}")
                            S.op("dve", lambda e, pr=pr, r=r, bD=bD: e.tensor_tensor(
                                out=dsum[pr, :].rearrange("p (h t) -> p h t", t=128), in0=ps[bD][pr, :].rearrange("p (h t) -> p h t", t=128),
                                in1=es[pr, r * 4:(r + 1) * 4, None].to_broadcast([64, 4, 128]), op=ALU.add),
                                reads=[("ps", bD), "es"], writes=["dsum"])
                            S.op("dve", lambda e, pr=pr: e.reciprocal(out=rden[pr, :], in_=dsum[pr, :]), reads=["dsum"], writes=["rden"])
                            S.op("dve", lambda e, pr=pr, bO=bO, jj=jj, mx=mx: e.tensor_tensor(
                                out=mx[pr, 4:8, jj * 128:(jj + 1) * 128], in0=ps[bO][pr, :].rearrange("p (h t) -> p h t", t=128),
                                in1=rden[pr, :].rearrange("p (h t) -> p h t", t=128), op=ALU.mult),
                                reads=[("ps", bO), "rden"], writes=[mk])
                    out_proj_post(wout, 8, mx, mk, "wout", n, j0 * 128, 0, 2, ysb, "ysb")
                bcfg["mm"] = [0, 1, 2, 3, 4, 5]
                bcfg["st"] = [6, 7]
                S.barrier()
        ph01.close()

        def ffn(lay, ntok):
            with contextlib.ExitStack() as ph:
                hfs = [sb(f"hf{lay}_{i}", [128, 8, 512], BF16, stack=ph) for i in range(2)]
                hkeep = sb(f"hkeep{lay}", [128, 8, 2], BF16, stack=ph)
                wu = [sb(f"wu{lay}_{i}", [128, 8, 256], BF16, stack=ph) for i in range(3)]
                wd = [sb(f"wd{lay}_{i}", [128, NF, 128], BF16, stack=ph) for i in range(2)]
                tt_ = [[sb(f"tc{lay}_{i}_{k}", [128, 512], stack=ph) for k in range(2)] for i in range(2)]
                sg = [sb(f"sg{lay}_{i}", [128, 512], stack=ph) for i in range(2)]
                gbuf = sb(f"gb{lay}", [128, NF, 512], BF16, stack=ph)
                ysb = sb(f"ysbf{lay}", [128, 8, 512], stack=ph)
                nb = (ntok + FFN_BLK - 1) // FFN_BLK
                bsz = (ntok + nb - 1) // nb
                scrB = dict(sq=sb(f"sqB{lay}", [128, 8, 512], BF16, stack=ph), rt=sb(f"rtB{lay}", [128, 512], stack=ph),
                            rstd=sb(f"rstdB{lay}", [128, 512], stack=ph), k="B")

                def pre_block(bi):
                    s0 = bi * bsz
                    n = min(bsz, ntok - s0)
                    hf = hfs[bi % 2]
                    hfk = f"hf{bi % 2}"
                    if s0 == 0:
                        S.op("dve", lambda e: e.memset(hf[:, :, 0:1], 0.0), writes=[hfk])
                    else:
                        S.op("dve", lambda e: e.tensor_copy(hf[:, :, 0:1], hkeep[:, :, 0:1]), reads=["hkeep"], writes=[hfk])
                    prenorm(x_fm, s0, n + 1, lay, 3, 0, hf, 1, xk(s0, n + 1), hfk, scrB)
                    if bi < nb - 1:
                        S.op("dve", lambda e, n=n: e.tensor_copy(hkeep[:, :, 0:1], hf[:, :, n:n + 1]), reads=[hfk], writes=["hkeep"])
                wuc = 0
                wdc = 0
                for bi in range(nb):
                    s0 = bi * bsz
                    n = min(bsz, ntok - s0)
                    if bi == 0:
                        pre_block(0)
                    hf = hfs[bi % 2]
                    hfk = f"hf{bi % 2}"
                    def tail(fp):
                        hs = fp % 2
                        S.op("act", lambda e, hs=hs, n=n: e.activation(out=sg[hs][:, 0:n], in_=tt_[hs][0][:, 0:n], func=AF.Silu),
                             reads=[f"tc{hs}0"], writes=[f"sg{hs}"])
                        S.op("dve", lambda e, hs=hs, fp=fp, n=n: e.tensor_tensor(out=gbuf[:, fp, 0:n], in0=sg[hs][:, 0:n], in1=tt_[hs][1][:, 0:n], op=ALU.mult),
                             reads=[f"sg{hs}", f"tc{hs}1"], writes=["gbuf"])

                    for f_ in range(NF):
                        sl = wuc % 3
                        wuc += 1
                        S.dma("pool", f"wu{sl}", wu[sl][:], w_up[lay, f_], writes=[f"wu{sl}"])
                        hs = f_ % 2
                        for k in range(2):
                            b = bank()

                            def f(e, k=k, b=b, sl=sl, n=n, hf=hf):
                                for kt in range(8):
                                    i = e.matmul(ps[b][:, 0:n + 2], wu[sl][:, kt, k * 128:(k + 1) * 128], hf[:, kt, 0:n + 2], start=(kt == 0), stop=(kt == 7))
                                return i
                            S.op("pe", f, reads=[f"wu{sl}", hfk], writes=[("ps", b)])
                            hk = f"hu{hs}{k}"
                            tk = f"tc{hs}{k}"
                            tc = tt_[hs][k]
                            ft = f_ + k * NF
                            S.op("act", lambda e, b=b, tc=tc, ft=ft, n=n: e.activation(out=tc[:, 0:n], in_=ps[b][:, 1:n + 1], func=AF.Identity,
                                                                                     scale=cw[:, lay, ft, 1:2], bias=cb[:, lay, ft:ft + 1]),
                                 reads=[("ps", b), "cw", "cb"], writes=[tk])
                            S.op("dve", lambda e, b=b, tc=tc, ft=ft, n=n: e.scalar_tensor_tensor(
                                out=tc[:, 0:n], in0=ps[b][:, 0:n], scalar=cw[:, lay, ft, 0:1], in1=tc[:, 0:n], op0=ALU.mult, op1=ALU.add),
                                reads=[("ps", b), tk, "cw"], writes=[tk])
                            S.op("dve", lambda e, b=b, tc=tc, ft=ft, n=n: e.scalar_tensor_tensor(
                                out=tc[:, 0:n], in0=ps[b][:, 2:n + 2], scalar=cw[:, lay, ft, 2:3], in1=tc[:, 0:n], op0=ALU.mult, op1=ALU.add),
                                reads=[("ps", b), tk, "cw"], writes=[tk])
                        if f_ > 0:
                            tail(f_ - 1)
                        if f_ == 8 and bi + 1 < nb:
                            pre_block(bi + 1)
                    tail(NF - 1)
                    for d in range(8):
                        sl = wdc % 2
                        wdc += 1
                        S.dma("pool", f"wd{sl}", wd[sl][:], w_down[lay, d], writes=[f"wd{sl}"])
                        b = bank()

                        def f(e, b=b, sl=sl, n=n):
                            for k in range(NF):
                                i = e.matmul(ps[b][:, 0:n], wd[sl][:, k, :], gbuf[:, k, 0:n], start=(k == 0), stop=(k == NF - 1))
                            return i
                        S.op("pe", f, reads=[f"wd{sl}", "gbuf"], writes=[("ps", b)])
                        S.op("act", lambda e, d=d, b=b, n=n: e.activation(out=sq[:, d, 0:n], in_=ps[b][:, 0:n], func=AF.Square),
                             reads=[("ps", b)], writes=["sq"])
                        S.op("act", lambda e, d=d, b=b, n=n: e.activation(out=ysb[:, d, 0:n], in_=ps[b][:, 0:n], func=AF.Copy, scale=coef[:, lay, 5, d, 0:1]),
                             reads=[("ps", b), "coef"], writes=["ysbf"])
                    post_update(ysb, n, s0, "ysbf")
                S.barrier()

        if debug_stage not in ("p0", "p1", "p2"):
            ffn(0, N_L0_FFN)

        if debug_stage not in ("p0", "p1", "p2", "p3"):
            with contextlib.ExitStack() as ph:
                wio = sb("wio", [128, 8, 2048], BF16, stack=ph)
                wo1 = sb("wo1", [128, 8, D], BF16, stack=ph)
                lngb = sb("lngb", [128, 2, D], stack=ph)
                wst = sb("wst", [128, 8, 128], BF16, stack=ph)
                bsb = sb("bsb", [128, 8, 128], stack=ph)
                h1 = sb("h1", [128, 8, 512], BF16, stack=ph)
                usb = sb("usb", [128, 8, 512], BF16, stack=ph)
                vg = [sb(f"vg{i}", [128, D], stack=ph) for i in range(2)]
                vn = [sb(f"vn{i}", [128, D], BF16, stack=ph) for i in range(2)]
                stat = [sb(f"stat{i}", [128, 8], stack=ph) for i in range(2)]
                s_sb = [sb(f"s_sb{i}", [128, 512], stack=ph) for i in range(2)]
                gated = usb
                ysb = sb("ysb1", [128, 8, 512], stack=ph)
                for c in range(4):
                    S.dma("pool", "wio", wio[:, :, c * 512:(c + 1) * 512], w_in1[:, :, c * 512:(c + 1) * 512], writes=["wio"])
                for c in range(2):
                    S.dma("pool", "wio", wo1[:, :, c * 512:(c + 1) * 512], w_out1[:, :, c * 512:(c + 1) * 512], writes=["wo1"])
                S.dma("pool", "wio", wst[:], wsT, writes=["wst"])
                S.dma("sp", "c0", lngb[:], ln_gb, writes=["lngb"])
                S.dma("sp", "c0", bsb[:], bs_bc, writes=["bsb"])
                S.seal("c0", ["lngb", "bsb"])
                S.seal("wio", ["wio", "wo1", "wst"])
                ntile = N_L1_MIX // 128
                for j0 in range(0, ntile, 4):
                    nj = min(4, ntile - j0)
                    n = nj * 128
                    prenorm(x_fm, j0 * 128, n, 1, 0, 0, h1, 0, xk(j0 * 128, n), "h1")
                    for ft in range(8):
                        b = bank()

                        def f(e, ft=ft, b=b, n=n):
                            for kt in range(8):
                                i = e.matmul(ps[b][:, 0:n], wio[:, kt, ft * 128:(ft + 1) * 128], h1[:, kt, 0:n], start=(kt == 0), stop=(kt == 7))
                            return i
                        S.op("pe", f, reads=["wio", "h1"], writes=[("ps", b)])
                        S.op("act", lambda e, ft=ft, b=b, n=n: e.activation(out=usb[:, ft, 0:n], in_=ps[b][:, 0:n], func=AF.Gelu_apprx_tanh),
                             reads=[("ps", b)], writes=["usb"])
                    def stages(jj):
                        p = jj % 2
                        vg_, vn_, st_, ss_ = vg[p], vn[p], stat[p], s_sb[p]
                        kvg, kvn, kst, kss = f"vg{p}", f"vn{p}", f"stat{p}", f"s_sb{p}"
                        yield lambda: S.op("dve", lambda e: e.memset(st_[:], 0.0), writes=[kst])

                        def vproj():
                            for hh in range(2):
                                b = bank()

                                def f(e, hh=hh, b=b):
                                    for kt in range(8):
                                        i = e.matmul(ps[b][:, :], h1[:, kt, jj * 128:(jj + 1) * 128], wio[:, kt, 1024 + hh * 512:1536 + hh * 512], start=(kt == 0), stop=(kt == 7))
                                    return i
                                S.op("pe", f, reads=["wio", "h1"], writes=[("ps", b)])
                                S.op("act", lambda e, hh=hh, b=b: e.activation(out=vg_[:, hh * 512:(hh + 1) * 512], in_=ps[b][:, :], func=AF.Gelu_apprx_tanh,
                                                                               accum_out=st_[:, hh:hh + 1]),
                                     reads=[("ps", b)], writes=[kvg, kst])
                        yield vproj
                        yield lambda: S.op("dve", lambda e: e.tensor_tensor(out=st_[:, 2:3], in0=st_[:, 0:1], in1=st_[:, 1:2], op=ALU.add), reads=[kst], writes=[kst])
                        yield lambda: S.op("dve", lambda e: e.tensor_scalar(out=st_[:, 3:4], in0=st_[:, 2:3], scalar1=-1.0 / D, scalar2=None, op0=ALU.mult), reads=[kst], writes=[kst])
                        yield lambda: S.op("dve", lambda e: e.tensor_scalar(out=vg_[:], in0=vg_[:], scalar1=st_[:, 3:4], scalar2=None, op0=ALU.add),
                                           reads=[kvg, kst], writes=[kvg])
                        yield lambda: S.op("act", lambda e: e.activation(out=vn_[:], in_=vg_[:], func=AF.Square, accum_out=st_[:, 4:5]), reads=[kvg], writes=[kvn, kst])
                        yield lambda: S.op("act", lambda e: e.activation(out=st_[:, 5:6], in_=st_[:, 4:5], func=AF.Sqrt, scale=1.0 / D, bias=eps_t[:, 0:1]),
                                           reads=[kst, "eps"], writes=[kst])
                        yield lambda: S.op("dve", lambda e: e.reciprocal(out=st_[:, 6:7], in_=st_[:, 5:6]), reads=[kst], writes=[kst])
                        yield lambda: S.op("dve", lambda e: e.scalar_tensor_tensor(out=vg_[:], in0=vg_[:], scalar=st_[:, 6:7], in1=lngb[:, 0, :], op0=ALU.mult, op1=ALU.mult),
                                           reads=[kvg, kst, "lngb"], writes=[kvg])
                        yield lambda: S.op("pool", lambda e: e.tensor_tensor(out=vn_[:], in0=vg_[:], in1=lngb[:, 1, :], op=ALU.add), reads=[kvg, "lngb"], writes=[kvn])
                        for g4 in range(2):
                            def spat(g4=g4):
                                b = bank()

                                def f(e, b=b):
                                    for gg in range(4):
                                        gi = g4 * 4 + gg
                                        i = e.matmul(ps[b][:, gg * 128:(gg + 1) * 128], vn_[:, gi * 128:(gi + 1) * 128], wst[:, gi, :], start=True, stop=True)
                                    return i
                                S.op("pe", f, reads=[kvn, "wst"], writes=[("ps", b)])
                                S.op("dve", lambda e, b=b: e.tensor_tensor(out=ss_[:, :], in0=ps[b][:, :], in1=bsb[:, g4 * 4:(g4 + 1) * 4, :].rearrange("p g t -> p (g t)"), op=ALU.add),
                                     reads=[("ps", b), "bsb"], writes=[kss])
                            yield spat
                            yield lambda g4=g4: S.op("dve", lambda e: e.tensor_tensor(out=gated[:, g4 * 4:(g4 + 1) * 4, jj * 128:(jj + 1) * 128],
                                                                                    in0=ss_[:, :].rearrange("p (g t) -> p g t", t=128),
                                                                                    in1=usb[:, g4 * 4:(g4 + 1) * 4, jj * 128:(jj + 1) * 128], op=ALU.mult),
                                                     reads=[kss, "usb"], writes=["usb"])

                    for ja in range(0, nj, 2):
                        gens = [list(stages(jj)) for jj in range(ja, min(ja + 2, nj))]
                        for si in range(len(gens[0])):
                            for gl in gens:
                                gl[si]()
                    out_proj_post(wo1, 8, gated, "usb", "wo1", n, j0 * 128, 1, 2, ysb, "ysb1")
                S.barrier()

        if debug_stage not in ("p0", "p1", "p2", "p3", "p4"):
            ffn(1, TOWN)

        with contextlib.ExitStack() as ph:
            ost = [sb(f"ost{i}", [128, D], stack=ph) for i in range(2)]
            for i in range(TOWN // 128):
                sl = i % 2
                for hlf in range(2):
                    b = bank()

                    def f(e, hlf=hlf, b=b, i=i):
                        for c in range(4):
                            cc = hlf * 4 + c
                            i2 = e.transpose(ps[b][:, c * 128:(c + 1) * 128], x_fm[:, cc, i * 128:(i + 1) * 128], ident[:])
                        return i2
                    S.op("pe", f, reads=[("x", i), "ident"], writes=[("ps", b)])
                    if hlf == 0:
                        S.op("act", lambda e, b=b, sl=sl: e.activation(out=ost[sl][:, 0:512], in_=ps[b][:, :], func=AF.Copy), reads=[("ps", b)], writes=[f"ost{sl}"])
                    else:
                        S.op("dve", lambda e, b=b, sl=sl: e.tensor_copy(ost[sl][:, 512:1024], ps[b][:, :]), reads=[("ps", b)], writes=[f"ost{sl}"])
                S.dma("sp", f"ost{sl}", out_loc[i], ost[sl][:], reads=[f"ost{sl}"], writes=[f"out{i}"])
            S.barrier()
    return nc


def _emit_pv(S, ps, pend, vb, vpc, ones_bf, pt, r, bO, bD, nch, vkey):
    kind, slot, psl, ci = pend

    def f(e):
        vv = vb[:, slot, r, :] if kind == "l" else vpc[:, slot, r, :]
        e.matmul(ps[bO][:, :], vv, pt[psl][:], start=(ci == 0), stop=(ci == nch - 1))
        return e.matmul(ps[bD][:, :], ones_bf[:], pt[psl][:], start=(ci == 0), stop=(ci == nch - 1))
    S.op("pe", f, reads=[f"pt{psl}", vkey, "vpc", "ones"], writes=[("ps", bO), ("ps", bD)])


def _fm(v):
    v = np.asarray(v, np.float32)
    sh = v.shape
    v = v.reshape(sh[:-1] + (sh[-1] // 128, 128))
    return np.ascontiguousarray(np.moveaxis(v, -1, 0))


def _kt(w):
    K, C = w.shape
    return np.ascontiguousarray(w.reshape(K // 128, 128, C).transpose(1, 0, 2))


def _positions(hf):
    i = np.arange(T)
    return i if hf == 0 else (L - 1 - i)


def _rope_tables(pos):
    inv = (10000.0 ** (-np.arange(16, dtype=np.float32) / 16)).astype(np.float32)
    row = (pos // 64).astype(np.float32)
    col = (pos % 64).astype(np.float32)
    C = np.zeros((128, T), np.float32)
    Sg = np.zeros((128, T), np.float32)
    for rr_ in range(128):
        i = rr_ % 64
        axis, half, f = i // 32, (i % 32) // 16, i % 16
        ang = (row if axis == 0 else col) * inv[f]
        C[rr_] = np.cos(ang)
        Sg[rr_] = np.sin(ang) * (-1.0 if half == 0 else 1.0)
    return C, Sg


def _pool_mats(pos):
    M = np.zeros((128, 4, 4, 128), np.float32)
    for gi, w in enumerate(POOL_W):
        hw = w // 2
        for kind, (jo, dj) in enumerate(((2, -1), (2, 0), (2, 1), (0, 0))):
            ji = jo + dj
            for to in range(128):
                p = pos[jo * 128 + to]
                st_, en = max(p - hw, 0), min(p + hw, L)
                cnt = en - st_
                for ti in range(128):
                    q = pos[ji * 128 + ti]
                    v = 0.0
                    if st_ <= q < en:
                        v += 1.0 / cnt
                    if dj == 0 and ti == to:
                        v -= 1.0
                    M[ti, gi, kind, to] = v
    return M


_PROG = {}


def kernel(x, c, ctx, c_ctx, w_ada, b_ada, g_mix_pre, g_mix_post, g_ffn_pre, g_ffn_post,
           w_in_even, w_pool, pool_scale, attn_sink, w_out_even,
           w_in_odd, sgu_ln_g, sgu_ln_b, sgu_w, sgu_b, w_out_odd,
           w_ffn_up, ffn_conv_w, ffn_conv_b, w_ffn_down, _debug_stage=None):
    f32 = np.float32
    x = np.asarray(x, f32)
    ctx = np.asarray(ctx, f32)
    w_ada_l = np.ascontiguousarray(np.asarray(w_ada, f32).reshape(2, 8, 128, 12, 512).transpose(0, 3, 2, 1, 4))
    b_ada_l = np.ascontiguousarray(np.asarray(b_ada, f32).reshape(2, 48, 128).transpose(2, 0, 1))
    gvec = np.stack([_fm(np.asarray(a, f32)) for a in (g_mix_pre, g_mix_post, g_ffn_pre, g_ffn_post)], axis=0)
    gvec = np.ascontiguousarray(gvec.transpose(1, 2, 0, 3))
    wi = np.asarray(w_in_even, f32)[0]
    perm64 = np.array([(i // 32) * 32 + (1 - (i % 32) // 16) * 16 + (i % 16) for i in range(64)])
    qcols, qpcols = [], []
    for j in range(4):
        for h in (j, 4 + j):
            base = 512 + h * 64
            qcols += list(base + np.arange(64))
            qpcols += list(base + perm64)
    kcols = list(1024 + np.arange(128))
    kpcols = [1024 + hh * 64 + p for hh in range(2) for p in perm64]
    cols = qcols + kcols + qpcols + kpcols + list(range(0, 512)) + list(range(1152, 1280))
    w_in0 = _kt(wi[:, cols])
    w_pool_l = np.ascontiguousarray(np.asarray(w_pool, f32)[0].transpose(1, 0, 2))
    pool_sc = _fm(np.asarray(pool_scale, f32)[0])
    sink_bc = np.ascontiguousarray(np.broadcast_to(np.asarray(attn_sink, f32)[0][None, :], (128, 8)))
    nm = np.zeros((128, 2, 128), f32)
    kk = np.arange(128)[:, None]
    qq = np.arange(128)[None, :]
    nm[:, 0, :] = np.where(kk >= qq, 0.0, -30000.0)
    nm[:, 1, :] = np.where(kk <= qq, 0.0, -30000.0)
    rows = list(range(512))
    for j in range(4):
        rows += list(512 + j * 64 + np.arange(64)) + list(512 + (4 + j) * 64 + np.arange(64))
    w_out0 = _kt(np.asarray(w_out_even, f32)[0][rows, :])
    w_in1 = _kt(np.asarray(w_in_odd, f32)[0])
    ln_gb = np.ascontiguousarray(np.broadcast_to(np.stack([np.asarray(sgu_ln_g, f32)[0], np.asarray(sgu_ln_b, f32)[0]], 0)[None], (128, 2, D)))
    w_out1 = _kt(np.asarray(w_out_odd, f32)[0])
    wu = np.asarray(w_ffn_up, f32).reshape(2, 8, 128, 2, NF, 128)
    w_up_l = np.ascontiguousarray(wu.transpose(0, 4, 2, 1, 3, 5)).reshape(2, NF, 128, 8, 256)
    wd = np.asarray(w_ffn_down, f32).reshape(2, NF, 128, 8, 128)
    w_down_l = np.ascontiguousarray(wd.transpose(0, 3, 2, 1, 4))
    cwf = np.asarray(ffn_conv_w, f32).reshape(2, 3, 44, 128)
    conv_b_l = np.ascontiguousarray(np.asarray(ffn_conv_b, f32).reshape(2, 44, 128).transpose(2, 0, 1))
    sw = np.asarray(sgu_w, f32)[0]
    sbias = np.asarray(sgu_b, f32)[0]
    ident = np.eye(128, dtype=f32)

    per_half = []
    for hf in range(2):
        pos = _positions(hf)
        C, Sg = _rope_tables(pos)
        mp = _pool_mats(pos)
        cwl = cwf if hf == 0 else cwf[:, ::-1]
        conv_w_l = np.ascontiguousarray(cwl.transpose(3, 0, 2, 1))
        swl = sw if hf == 0 else sw[:, ::-1, ::-1]
        wsT = np.ascontiguousarray(swl.transpose(2, 0, 1))
        sbl = sbias if hf == 0 else sbias[:, ::-1]
        bs_bc = np.ascontiguousarray(np.broadcast_to(sbl[None], (128, 8, 128)))
        per_half.append(dict(pos=pos, ropeC=C, ropeS=Sg, mpool=mp, conv_w=conv_w_l, wsT=wsT, bs_bc=bs_bc))

    in_maps = []
    for core in range(8):
        b, hf = core // 2, core % 2
        ph = per_half[hf]
        cv = np.stack([np.asarray(c, f32)[b], np.asarray(c_ctx, f32)], axis=-1)
        in_maps.append(dict(
            x_loc=np.ascontiguousarray(x[b][ph["pos"]].reshape(NT, 128, D)),
            ctx_in=np.ascontiguousarray(ctx[b].reshape(2, 128, D)),
            cvec=np.ascontiguousarray(cv.reshape(8, 128, 2).transpose(1, 0, 2)),
            w_ada=w_ada_l, b_ada=b_ada_l, gvec=gvec, w_in0=w_in0, ropeC=ph["ropeC"], ropeS=ph["ropeS"],
            w_pool=w_pool_l, pool_sc=pool_sc, mpool=ph["mpool"], sink_bc=sink_bc, negmask=nm, ident_in=ident,
            w_out0=w_out0, w_in1=w_in1, ln_gb=ln_gb, wsT=ph["wsT"], bs_bc=ph["bs_bc"], w_out1=w_out1,
            w_up=w_up_l, w_down=w_down_l, conv_w=ph["conv_w"], conv_b=conv_b_l,
        ))
    key = _debug_stage
    if key not in _PROG:
        _PROG[key] = build_program(_debug_stage)
    res = run_bass_kernel_spmd(_PROG[key], in_maps, core_ids=list(range(8)))
    out = np.empty((4, L, D), f32)
    for core in range(8):
        b, hf = core // 2, core % 2
        o = np.asarray(res.results[core]["out_loc"], f32).reshape(TOWN, D)
        out[b, per_half[hf]["pos"][:TOWN]] = o
    return out
```

```python
import contextlib
import numpy as np
import concourse.bass as bass
import concourse.mybir as mybir
from concourse.bass_utils import run_bass_kernel_spmd

F32 = mybir.dt.float32
BF16 = mybir.dt.bfloat16
AF = mybir.ActivationFunctionType
ALU = mybir.AluOpType

D = 1024
L = 4096
NT = 19
T = NT * 128
TOWN = 2048
CTX = 256
DFF = 2816
NF = 22
EPS = 1e-6
POOL_W = (2, 4, 8, 16)
N_L0_MIX = 18 * 128
N_L0_FFN = 17 * 128
N_L1_MIX = 17 * 128
FFN_BLK = 510


class Sched:
    def __init__(self, nc, st):
        self.nc = nc
        self.st = st
        self.eng = dict(pe=nc.tensor, act=nc.scalar, dve=nc.vector, pool=nc.gpsimd, sp=nc.sync)
        self.sem = {k: st.enter_context(nc.semaphore("sem_" + k)) for k in ("pe", "act", "dve", "pool")}
        self.cnt = {k: 0 for k in self.sem}
        self.dsem = {}
        self.dcnt = {}
        self.waited = {k: {} for k in self.eng}
        self.lastw = {}
        self.readers = {}

    def _deps(self, reads, writes):
        deps = {}

        def add(k, v):
            if deps.get(k, 0) < v:
                deps[k] = v

        for r in reads:
            t = self.lastw.get(r)
            if t:
                add(*t)
        for w in writes:
            t = self.lastw.get(w)
            if t:
                add(*t)
            for k, v in self.readers.get(w, {}).items():
                add(k, v)
        return deps

    def _semof(self, k):
        return self.sem[k] if k in self.sem else self.dsem[k]

    def _wait(self, eng, deps):
        e = self.eng[eng]
        for k, v in deps.items():
            if eng == "pe" and k == "pe":
                continue
            if self.waited[eng].get(k, 0) >= v:
                continue
            e.wait_ge(self._semof(k), v)
            self.waited[eng][k] = v

    def _commit(self, tok, reads, writes):
        k, v = tok
        for r in reads:
            d = self.readers.setdefault(r, {})
            if d.get(k, 0) < v:
                d[k] = v
        for w in writes:
            self.lastw[w] = tok
            self.readers[w] = {}

    def op(self, eng, fn, reads=(), writes=()):
        self._wait(eng, self._deps(reads, writes))
        inst = fn(self.eng[eng])
        self.cnt[eng] += 1
        inst.then_inc(self.sem[eng], 1)
        self._commit((eng, self.cnt[eng]), reads, writes)

    def dma(self, q, key, out, in_, reads=(), writes=()):
        if key not in self.dsem:
            self.dsem[key] = self.st.enter_context(self.nc.semaphore("dsem_" + key))
            self.dcnt[key] = 0
        self._wait(q, self._deps(reads, writes))
        self.eng[q].dma_start(out=out, in_=in_).then_inc(self.dsem[key], 16)
        self.dcnt[key] += 16
        self._commit((key, self.dcnt[key]), reads, writes)

    def seal(self, key, resources):
        for r in resources:
            self.lastw[r] = (key, self.dcnt[key])

    def barrier(self, engines=("pe", "act", "dve", "pool", "sp")):
        for e in engines:
            deps = {k: v for k, v in self.cnt.items() if v > 0}
            deps.update({k: v for k, v in self.dcnt.items() if v > 0})
            deps.pop(e, None) if e == "pe" else None
            self._wait(e, deps)


class Ctx:
    pass


def build_program(debug_stage=None):
    nc = bass.Bass("TRN2", target_bir_lowering=False)
    g = Ctx()

    def din(name, shape, dt=F32):
        return nc.dram_tensor(name, list(shape), dt, kind="ExternalInput").ap()

    x_loc = din("x_loc", [NT, 128, D])
    ctx_in = din("ctx_in", [2, 128, D])
    cvec = din("cvec", [128, 8, 2])
    w_ada = din("w_ada", [2, 12, 128, 8, 512])
    b_ada = din("b_ada", [128, 2, 48])
    gvec = din("gvec", [128, 2, 4, 8])
    w_in0 = din("w_in0", [128, 8, 1920])
    ropeC = din("ropeC", [128, T])
    ropeS = din("ropeS", [128, T])
    w_pool = din("w_pool", [128, 4, 128])
    pool_sc = din("pool_sc", [128, 4])
    mpool = din("mpool", [128, 4, 4, 128])
    sink_bc = din("sink_bc", [128, 8])
    negmask = din("negmask", [128, 2, 128])
    ident_in = din("ident_in", [128, 128])
    w_out0 = din("w_out0", [128, 8, D])
    w_in1 = din("w_in1", [128, 8, 2048])
    ln_gb = din("ln_gb", [128, 2, D])
    wsT = din("wsT", [128, 8, 128])
    bs_bc = din("bs_bc", [128, 8, 128])
    w_out1 = din("w_out1", [128, 8, D])
    w_up = din("w_up", [2, NF, 128, 8, 256])
    w_down = din("w_down", [2, 8, 128, NF, 128])
    conv_w = din("conv_w", [128, 2, 44, 3])
    conv_b = din("conv_b", [128, 2, 44])
    out_loc = nc.dram_tensor("out_loc", [TOWN // 128, 128, D], F32, kind="ExternalOutput").ap()
    qk_d = nc.dram_tensor("qk_d", [5, 128, T], BF16).ap()
    u_d = nc.dram_tensor("u_d", [NT, 128, 512], BF16).ap()
    v_d = nc.dram_tensor("v_d", [NT, 128, 128], BF16).ap()
    wub_d = nc.dram_tensor("wub_d", [2, NF, 128, 8, 256], BF16).ap()
    wdb_d = nc.dram_tensor("wdb_d", [2, 8, 128, NF, 128], BF16).ap()

    with contextlib.ExitStack() as st:
        E = st.enter_context
        S = Sched(nc, st)

        def sb(name, shape, dt=F32, stack=None):
            return (stack or st).enter_context(nc.sbuf_tensor(name, list(shape), dt))

        ps = [E(nc.psum_tensor(f"ps{i}", [128, 512], F32)) for i in range(8)]
        rr = {"mm": 0, "st": 0}

        bcfg = {"mm": [0, 1, 2, 3, 4, 5], "st": [6, 7]}

        def bank(pool="mm"):
            lst = bcfg[pool]
            ctr = "mm" if bcfg["st"] is bcfg["mm"] else pool
            i = lst[rr[ctr] % len(lst)]
            rr[ctr] += 1
            return i

        x_fm = sb("x_fm", [128, 8, T])
        ident = sb("ident", [128, 128])
        ident_bf = sb("ident_bf", [128, 128], BF16)
        ones_bf = sb("ones_bf", [128, 128], BF16)
        coef = sb("coef", [128, 2, 6, 8, 2])
        gv = sb("gv", [128, 2, 4, 8])
        cw = sb("cw", [128, 2, 44, 3])
        cb = sb("cb", [128, 2, 44])
        eps_t = sb("eps_t", [128, 1])
        sq = sb("sq", [128, 8, 512], BF16)
        rt = sb("rt", [128, 512])
        rstd = sb("rstd", [128, 512])
        tn = [sb(f"tn{i}", [128, 512]) for i in range(2)]
        kc_sb = sb("kc_sb", [128, CTX], BF16)
        vpc = sb("vpc", [128, 2, 2, 128], BF16)

        cv = sb("cv", [128, 8, 2])
        bada = sb("bada", [128, 2, 48])
        S.dma("sp", "c0", ident[:], ident_in, writes=["ident"])
        S.dma("sp", "c0", gv[:], gvec, writes=["gv"])
        S.dma("sp", "c0", cw[:], conv_w, writes=["cw"])
        S.dma("sp", "c0", cb[:], conv_b, writes=["cb"])
        S.dma("sp", "c0", cv[:], cvec, writes=["cv"])
        S.dma("sp", "c0", bada[:], b_ada, writes=["bada"])
        S.seal("c0", ["ident", "gv", "cw", "cb", "cv", "bada"])
        S.op("dve", lambda e: e.memset(ones_bf[:], 1.0), writes=["ones"])
        S.op("dve", lambda e: e.memset(eps_t[:], EPS), writes=["eps"])
        S.op("dve", lambda e: e.tensor_copy(ident_bf[:], ident[:]), reads=["ident"], writes=["identbf"])

        def xk(c0, n):
            return [("x", t) for t in range(c0 // 128, (c0 + n - 1) // 128 + 1)]

        scrA = dict(sq=sq, rt=rt, rstd=rstd, k="")

        def prenorm(src, c0, n, lay, ka, col, dst, doff, srckey, dstkey, scr=None):
            scr = scr or scrA
            sq, rt, rstd, sk = scr["sq"], scr["rt"], scr["rstd"], scr["k"]
            S.op("act", lambda e: e.activation(out=sq[:, :, 0:n], in_=src[:, :, c0:c0 + n], func=AF.Square),
                 reads=srckey, writes=["sq" + sk])
            b = bank("st")

            def f(e):
                for kt in range(8):
                    i = e.matmul(ps[b][:, 0:n], ones_bf[:], sq[:, kt, 0:n], start=(kt == 0), stop=(kt == 7))
                return i
            S.op("pe", f, reads=["sq" + sk, "ones"], writes=[("ps", b)])
            S.op("act", lambda e: e.activation(out=rt[:, 0:n], in_=ps[b][:, 0:n], func=AF.Sqrt, scale=1.0 / D, bias=eps_t[:, 0:1]),
                 reads=[("ps", b), "eps"], writes=["rt" + sk])
            S.op("dve", lambda e: e.reciprocal(out=rstd[:, 0:n], in_=rt[:, 0:n]), reads=["rt" + sk], writes=["rstd" + sk])
            for kt in range(8):
                ts = kt % 2
                S.op("dve", lambda e, kt=kt, ts=ts: e.tensor_tensor(out=tn[ts][:, 0:n], in0=src[:, kt, c0:c0 + n], in1=rstd[:, 0:n], op=ALU.mult),
                     reads=srckey + ["rstd" + sk], writes=[f"tn{ts}"])
                S.op("act", lambda e, kt=kt, ts=ts: e.activation(out=dst[:, kt, doff:doff + n], in_=tn[ts][:, 0:n], func=AF.Identity,
                                                                 scale=coef[:, lay, ka, kt, col:col + 1], bias=coef[:, lay, ka + 1, kt, col:col + 1]),
                     reads=[f"tn{ts}", "coef"], writes=[dstkey])

        def post_update(ysb, n, c0, ykey):
            b = bank("st")

            def f(e):
                for kt in range(8):
                    i = e.matmul(ps[b][:, 0:n], ones_bf[:], sq[:, kt, 0:n], start=(kt == 0), stop=(kt == 7))
                return i
            S.op("pe", f, reads=["sq", "ones"], writes=[("ps", b)])
            S.op("act", lambda e: e.activation(out=rt[:, 0:n], in_=ps[b][:, 0:n], func=AF.Sqrt, scale=1.0 / D, bias=eps_t[:, 0:1]),
                 reads=[("ps", b), "eps"], writes=["rt"])
            S.op("dve", lambda e: e.reciprocal(out=rstd[:, 0:n], in_=rt[:, 0:n]), reads=["rt"], writes=["rstd"])
            S.op("dve", lambda e: e.tensor_tensor(out=ysb[:, :, 0:n], in0=ysb[:, :, 0:n],
                                                  in1=rstd[:, None, 0:n].to_broadcast([128, 8, n]), op=ALU.mult),
                 reads=[ykey, "rstd"], writes=[ykey])
            S.op("dve", lambda e: e.tensor_tensor(out=x_fm[:, :, c0:c0 + n], in0=x_fm[:, :, c0:c0 + n], in1=ysb[:, :, 0:n], op=ALU.add),
                 reads=[ykey] + xk(c0, n), writes=xk(c0, n))

        def out_proj_post(W, nk, rhs, rhskey, wkey, n, c0, lay, kg, ysb, ykey):
            for d in range(8):
                b = bank()

                def f(e, d=d, b=b):
                    for k in range(nk):
                        i = e.matmul(ps[b][:, 0:n], W[:, k, d * 128:(d + 1) * 128], rhs[:, k, 0:n], start=(k == 0), stop=(k == nk - 1))
                    return i
                S.op("pe", f, reads=[rhskey, wkey], writes=[("ps", b)])
                S.op("act", lambda e, d=d, b=b: e.activation(out=sq[:, d, 0:n], in_=ps[b][:, 0:n], func=AF.Square),
                     reads=[("ps", b)], writes=["sq"])
                S.op("act", lambda e, d=d, b=b: e.activation(out=ysb[:, d, 0:n], in_=ps[b][:, 0:n], func=AF.Copy,
                                                             scale=coef[:, lay, kg, d, 0:1]),
                     reads=[("ps", b), "coef"], writes=[ykey])
            post_update(ysb, n, c0, ykey)

        ph01 = contextlib.ExitStack()
        g.ctx_fm = sb("ctx_fm", [128, 8, CTX], stack=ph01)
        wout = sb("wout", [128, 8, D], BF16, stack=ph01)
        wpl = sb("wpl", [128, 4, 128], BF16, stack=ph01)
        mpl = sb("mpl", [128, 4, 4, 128], BF16, stack=ph01)
        nmk = sb("nmk", [128, 2, 128], BF16, stack=ph01)
        with contextlib.ExitStack() as ph:
            cs = sb("cs", [128, 8, 2], BF16, stack=ph)
            wa = [sb(f"wa{i}", [128, 8, 512], BF16, stack=ph) for i in range(3)]
            modsb = sb("modsb", [128, 2, 6, 8, 2], stack=ph)
            xs = [sb(f"xs{i}", [128, D], stack=ph) for i in range(2)]
            S.op("act", lambda e: e.activation(out=cs[:], in_=cv[:], func=AF.Silu), reads=["cv"], writes=["cs"])
            for lay in range(2):
                bm = bank("st")
                for ch in range(12):
                    sl = (lay * 12 + ch) % 3
                    S.dma("pool", f"wa{sl}", wa[sl][:], w_ada[lay, ch], writes=[f"wa{sl}"])

                    def f(e, ch=ch, sl=sl, bm=bm):
                        for ft in range(4):
                            o = (ch * 4 + ft) * 2
                            for kt in range(8):
                                i = e.matmul(ps[bm][:, o:o + 2], wa[sl][:, kt, ft * 128:(ft + 1) * 128], cs[:, kt, :], start=(kt == 0), stop=(kt == 7))
                        return i
                    S.op("pe", f, reads=[f"wa{sl}", "cs"], writes=[("ps", bm)])
                S.op("dve", lambda e, lay=lay, bm=bm: e.tensor_tensor(
                    out=modsb[:, lay].rearrange("p j c t -> p (j c) t"),
                    in0=ps[bm][:, 0:96].rearrange("p (a t) -> p a t", t=2),
                    in1=bada[:, lay, :, None].to_broadcast([128, 48, 2]), op=ALU.add),
                    reads=[("ps", bm), "bada"], writes=["modsb"])
                for (ka, jsc, jsh, jgt, gpre, gpost) in ((0, 1, 0, 2, 0, 1), (3, 4, 3, 5, 2, 3)):
                    S.op("dve", lambda e, lay=lay, ka=ka, jsc=jsc: e.tensor_scalar(out=coef[:, lay, ka], in0=modsb[:, lay, jsc], scalar1=1.0, scalar2=None, op0=ALU.add),
                         reads=["modsb"], writes=["coef"])
                    S.op("dve", lambda e, lay=lay, ka=ka, gpre=gpre: e.tensor_tensor(out=coef[:, lay, ka], in0=coef[:, lay, ka],
                                                                                   in1=gv[:, lay, gpre, :, None].to_broadcast([128, 8, 2]), op=ALU.mult),
                         reads=["coef", "gv"], writes=["coef"])
                    S.op("dve", lambda e, lay=lay, ka=ka, jsh=jsh: e.tensor_copy(coef[:, lay, ka + 1], modsb[:, lay, jsh]),
                         reads=["modsb"], writes=["coef"])
                    S.op("dve", lambda e, lay=lay, ka=ka, jgt=jgt, gpost=gpost: e.tensor_tensor(
                        out=coef[:, lay, ka + 2], in0=modsb[:, lay, jgt], in1=gv[:, lay, gpost, :, None].to_broadcast([128, 8, 2]), op=ALU.mult),
                        reads=["modsb", "gv"], writes=["coef"])

            def load_T(src_ap, dst, t0, i, dkey):
                sl = i % 2
                S.dma("sp", f"xs{sl}", xs[sl][:], src_ap, writes=[f"xs{sl}"])
                for hlf in range(2):
                    b = bank()

                    def f(e, hlf=hlf, b=b, sl=sl):
                        for c in range(4):
                            cc = hlf * 4 + c
                            i2 = e.transpose(ps[b][:, c * 128:(c + 1) * 128], xs[sl][:, cc * 128:(cc + 1) * 128], ident[:])
                        return i2
                    S.op("pe", f, reads=[f"xs{sl}", "ident"], writes=[("ps", b)])
                    eng = "act" if hlf == 0 else "dve"
                    if eng == "act":
                        S.op("act", lambda e, hlf=hlf, b=b: e.activation(out=dst[:, hlf * 4:hlf * 4 + 4, t0:t0 + 128],
                                                                         in_=ps[b][:, :].rearrange("p (c t) -> p c t", t=128), func=AF.Copy),
                             reads=[("ps", b)], writes=[dkey])
                    else:
                        S.op("dve", lambda e, hlf=hlf, b=b: e.tensor_copy(dst[:, hlf * 4:hlf * 4 + 4, t0:t0 + 128],
                                                                          ps[b][:, :].rearrange("p (c t) -> p c t", t=128)),
                             reads=[("ps", b)], writes=[dkey])
            for i in range(NT):
                load_T(x_loc[i], x_fm, i * 128, i, ("x", i))
            for i in range(2):
                load_T(ctx_in[i], g.ctx_fm, i * 128, NT + i, "ctx")
            S.barrier()

        if debug_stage == "p0":
            pass
        if debug_stage not in ("p0",):
            with contextlib.ExitStack() as ph:
                win = sb("win", [128, 8, 1920], BF16, stack=ph)
                hb = [sb(f"hb{i}", [128, 8, 512], BF16, stack=ph) for i in range(2)]
                rcf = sb("rcf", [128, 2, T], stack=ph)
                t1s = [sb(f"t1_{i}", [128, 512], stack=ph) for i in range(2)]
                t2s = [sb(f"t2_{i}", [128, 512], stack=ph) for i in range(2)]
                qst = [sb(f"qst{i}", [128, 5, 512], BF16, stack=ph) for i in range(1)] * 2
                ust = [sb(f"ust{i}", [128, 512], BF16, stack=ph) for i in range(2)]
                vst = [sb(f"vst{i}", [128, 128], BF16, stack=ph) for i in range(2)]
                for c in range(4):
                    S.dma("pool", "win", win[:, :, c * 480:(c + 1) * 480], w_in0[:, :, c * 480:(c + 1) * 480], writes=["win"])
                for c in range(2):
                    S.dma("pool", "wout", wout[:, :, c * 512:(c + 1) * 512], w_out0[:, :, c * 512:(c + 1) * 512], writes=["wout"])
                S.dma("pool", "wout", wpl[:], w_pool, writes=["wpl"])
                S.dma("pool", "wout", mpl[:], mpool, writes=["mpl"])
                S.dma("pool", "wout", nmk[:], negmask, writes=["nmk"])
                S.seal("wout", ["wout", "wpl", "mpl", "nmk"])
                S.dma("sp", "rcf", rcf[:, 0, :], ropeC, writes=["rcf"])
                S.dma("sp", "rcf", rcf[:, 1, :], ropeS, writes=["rcf"])
                S.op("dve", lambda e: e.memset(vpc[:], 0.0), writes=["vpc"])
                nblk = (T + 511) // 512
                for bi in range(nblk):
                    c0 = bi * 512
                    n = min(512, T - c0)
                    sl = bi % 2
                    h = hb[sl]
                    hk = f"hb{sl}"
                    if bi == 0:
                        prenorm(x_fm, c0, n, 0, 0, 0, h, 0, xk(c0, n), hk)
                    for j in range(5):
                        if j == 2 and bi + 1 < nblk:
                            c1 = (bi + 1) * 512
                            n1 = min(512, T - c1)
                            prenorm(x_fm, c1, n1, 0, 0, 0, hb[1 - sl], 0, xk(c1, n1), f"hb{1 - sl}")
                        ba, bb = bank(), bank()
                        t1, t2 = t1s[j % 2], t2s[j % 2]
                        k1, k2 = f"t1_{j % 2}", f"t2_{j % 2}"

                        def f(e, j=j, ba=ba, bb=bb, h=h, n=n):
                            for kt in range(8):
                                e.matmul(ps[ba][:, 0:n], win[:, kt, j * 128:(j + 1) * 128], h[:, kt, 0:n], start=(kt == 0), stop=(kt == 7))
                            for kt in range(8):
                                i = e.matmul(ps[bb][:, 0:n], win[:, kt, (5 + j) * 128:(6 + j) * 128], h[:, kt, 0:n], start=(kt == 0), stop=(kt == 7))
                            return i
                        S.op("pe", f, reads=["win", hk], writes=[("ps", ba), ("ps", bb)])
                        S.op("dve", lambda e, ba=ba, c0=c0, n=n, t1=t1: e.tensor_tensor(out=t1[:, 0:n], in0=ps[ba][:, 0:n], in1=rcf[:, 0, c0:c0 + n], op=ALU.mult),
                             reads=[("ps", ba), "rcf"], writes=[k1])
                        S.op("dve", lambda e, bb=bb, c0=c0, n=n, t2=t2: e.tensor_tensor(out=t2[:, 0:n], in0=ps[bb][:, 0:n], in1=rcf[:, 1, c0:c0 + n], op=ALU.mult),
                             reads=[("ps", bb), "rcf"], writes=[k2])
                        S.op("pool", lambda e, j=j, sl=sl, n=n, t1=t1, t2=t2: e.tensor_tensor(out=qst[sl][:, j, 0:n], in0=t1[:, 0:n], in1=t2[:, 0:n], op=ALU.add),
                             reads=[k1, k2], writes=["qst0"])
                    S.dma("sp", "qst0", qk_d[:, :, c0:c0 + n].rearrange("j p t -> p j t"), qst[sl][:, :, 0:n], reads=["qst0"], writes=["qk_d"])
                    for tt in range(n // 128):
                        ti = bi * 4 + tt
                        s2 = ti % 2
                        bu, bv = bank(), bank()

                        def f(e, tt=tt, bu=bu, bv=bv, h=h):
                            for kt in range(8):
                                e.matmul(ps[bu][:, :], h[:, kt, tt * 128:(tt + 1) * 128], win[:, kt, 1280:1792], start=(kt == 0), stop=(kt == 7))
                            for kt in range(8):
                                i = e.matmul(ps[bv][:, 0:128], h[:, kt, tt * 128:(tt + 1) * 128], win[:, kt, 1792:1920], start=(kt == 0), stop=(kt == 7))
                            return i
                        S.op("pe", f, reads=["win", hk], writes=[("ps", bu), ("ps", bv)])
                        S.op("act", lambda e, bu=bu, s2=s2: e.activation(out=ust[s2][:], in_=ps[bu][:, :], func=AF.Copy),
                             reads=[("ps", bu)], writes=[f"ust{s2}"])
                        S.op("act", lambda e, bv=bv, s2=s2: e.activation(out=vst[s2][:], in_=ps[bv][:, 0:128], func=AF.Copy),
                             reads=[("ps", bv)], writes=[f"vst{s2}"])
                        S.dma("sp", f"ust{s2}", u_d[ti], ust[s2][:], reads=[f"ust{s2}"], writes=["u_d"])
                        S.dma("sp", f"vst{s2}", v_d[ti], vst[s2][:], reads=[f"vst{s2}"], writes=["v_d"])
                h = hb[0]
                prenorm(g.ctx_fm, 0, CTX, 0, 0, 1, h, 0, ["ctx"], "hb0")
                bk = bank()

                def f(e, bk=bk, h=h):
                    for kt in range(8):
                        i = e.matmul(ps[bk][:, 0:CTX], win[:, kt, 4 * 128:5 * 128], h[:, kt, 0:CTX], start=(kt == 0), stop=(kt == 7))
                    return i
                S.op("pe", f, reads=["win", "hb0"], writes=[("ps", bk)])
                S.op("act", lambda e, bk=bk: e.activation(out=kc_sb[:], in_=ps[bk][:, 0:CTX], func=AF.Copy), reads=[("ps", bk)], writes=["kc"])
                for tt in range(2):
                    bv = bank()

                    def f(e, tt=tt, bv=bv, h=h):
                        for kt in range(8):
                            i = e.matmul(ps[bv][:, 0:128], h[:, kt, tt * 128:(tt + 1) * 128], win[:, kt, 1792:1920], start=(kt == 0), stop=(kt == 7))
                        return i
                    S.op("pe", f, reads=["win", "hb0"], writes=[("ps", bv)])
                    S.op("act", lambda e, tt=tt, bv=bv: e.activation(out=vpc[:, tt, 0, 0:64], in_=ps[bv][:, 0:64], func=AF.Copy),
                         reads=[("ps", bv)], writes=["vpc"])
                    S.op("act", lambda e, tt=tt, bv=bv: e.activation(out=vpc[:, tt, 1, 64:128], in_=ps[bv][:, 64:128], func=AF.Copy),
                         reads=[("ps", bv)], writes=["vpc"])
                S.barrier()

        if debug_stage not in ("p0", "p1"):
            with contextlib.ExitStack() as ph:
                psc = sb("psc", [128, 4], stack=ph)
                snk = sb("snk", [128, 8], stack=ph)
                es = sb("es", [128, 8], stack=ph)
                ublk = [sb(f"ublk{i}", [128, 6, 512], BF16, stack=ph) for i in range(1)] * 2
                kblk = [sb(f"kblk{i}", [128, 6 * 128], BF16, stack=ph) for i in range(2)]
                qblk = [sb(f"qblk{i}", [128, 4, 512], BF16, stack=ph) for i in range(2)]
                vblk = [sb(f"vblk{i}", [128, 6, 2, 128], BF16, stack=ph) for i in range(2)]
                pooled = sb("pooled", [128, 4, 512], BF16, stack=ph)
                mix = [sb(f"mix{i}", [128, 8, 512], BF16, stack=ph) for i in range(1)] * 2
                pt = [sb(f"pt{i}", [128, 512], BF16, stack=ph) for i in range(3)]
                dsum = sb("dsum", [128, 512], stack=ph)
                rden = sb("rden", [128, 512], stack=ph)
                ysb = sb("ysb", [128, 8, 512], stack=ph)
                S.dma("sp", "c0", psc[:], pool_sc, writes=["psc"])
                S.dma("sp", "c0", snk[:], sink_bc, writes=["snk"])
                S.seal("c0", ["psc", "snk"])
                S.op("act", lambda e: e.activation(out=es[:], in_=snk[:], func=AF.Exp), reads=["snk"], writes=["es"])
                for i in range(2):
                    S.op("dve", lambda e, i=i: e.memset(vblk[i][:], 0.0), writes=[f"vblk{i}"])
                nqt = 18
                bcfg["mm"] = bcfg["st"] = [0, 1, 2, 3]
                blocks = [(j0, min(4, nqt - j0)) for j0 in range(0, nqt, 4)]
                for bi, (j0, nj) in enumerate(blocks):
                    sl = bi % 2
                    n = nj * 128
                    lo = max(j0 - 1, 0)
                    hi = j0 + nj
                    off = lo - (j0 - 1)
                    nl = hi - lo + 1
                    S.dma("sp", "ublk0", ublk[sl][:, off:off + nl, :], u_d[lo:hi + 1].rearrange("t p c -> p t c"),
                          reads=["u_d"], writes=["ublk0"])
                    S.dma("sp", f"kblk{sl}", kblk[sl][:, off * 128:(off + nl) * 128], qk_d[4, :, lo * 128:(hi + 1) * 128],
                          reads=["qk_d"], writes=[f"kblk{sl}"])
                    S.dma("sp", f"qblk{sl}", qblk[sl][:, :, 0:n], qk_d[0:4, :, j0 * 128:j0 * 128 + n].rearrange("j p t -> p j t"),
                          reads=["qk_d"], writes=[f"qblk{sl}"])
                    S.dma("sp", f"vblk{sl}", vblk[sl][:, off:off + nl, 0, 0:64], v_d[lo:hi + 1, :, 0:64].rearrange("t p c -> p t c"),
                          reads=["v_d"], writes=[f"vblk{sl}"])
                    S.dma("sp", f"vblk{sl}", vblk[sl][:, off:off + nl, 1, 64:128], v_d[lo:hi + 1, :, 64:128].rearrange("t p c -> p t c"),
                          reads=["v_d"], writes=[f"vblk{sl}"])
                    mx = mix[sl]
                    mk = "mix0"
                    pb = [bank() for _ in range(4)]
                    for gi in range(4):
                        b = pb[gi]

                        def f(e, gi=gi, b=b, j0=j0, nj=nj, sl=sl):
                            for jj in range(nj):
                                j = j0 + jj
                                terms = []
                                if j > 0:
                                    terms.append((jj, 0))
                                terms.append((jj + 1, 3 if j == 0 else 1))
                                terms.append((jj + 2, 2))
                                for ti, (slot, kind) in enumerate(terms):
                                    i = e.matmul(ps[b][:, jj * 128:(jj + 1) * 128], ublk[sl][:, slot, gi * 128:(gi + 1) * 128], mpl[:, gi, kind, :],
                                                 start=(ti == 0), stop=(ti == len(terms) - 1))
                            return i
                        S.op("pe", f, reads=["ublk0", "mpl"], writes=[("ps", b)])
                    for gi in range(4):
                        S.op("act", lambda e, gi=gi, b=pb[gi], n=n: e.activation(out=pooled[:, gi, 0:n], in_=ps[b][:, 0:n], func=AF.Copy),
                             reads=[("ps", pb[gi])], writes=[("pooled", gi)])
                    pb2 = [bank() for _ in range(4)]
                    for gi in range(4):
                        S.op("pe", lambda e, gi=gi, b2=pb2[gi], n=n: e.matmul(ps[b2][:, 0:n], wpl[:, gi, :], pooled[:, gi, 0:n], start=True, stop=True),
                             reads=[("pooled", gi), "wpl"], writes=[("ps", pb2[gi])])
                    for gi in range(4):
                        S.op("act", lambda e, gi=gi, b2=pb2[gi], n=n, mx=mx: e.activation(out=mx[:, gi, 0:n], in_=ps[b2][:, 0:n], func=AF.Copy, scale=psc[:, gi:gi + 1]),
                             reads=[("ps", pb2[gi]), "psc"], writes=[mk])
                    for jj in range(nj):
                        j = j0 + jj
                        for r in range(2):
                            pr = slice(r * 64, (r + 1) * 64)
                            chunks = []
                            if j > 0:
                                chunks.append(("l", jj, 0))
                            chunks.append(("l", jj + 1, None))
                            chunks.append(("l", jj + 2, 1))
                            chunks.append(("c", 0, None))
                            chunks.append(("c", 1, None))
                            bO, bD = ((4, 5), (6, 7))[(jj * 2 + r) % 2]
                            pend = None
                            nch = len(chunks)
                            for ci, (kind, slot, mki) in enumerate(chunks):
                                bS = bank()
                                psl = (bi * 100 + jj * 10 + r * 5 + ci) % 3

                                def f(e, kind=kind, slot=slot, mki=mki, bS=bS, jj=jj, sl=sl, pr=pr):
                                    if kind == "l":
                                        kk = kblk[sl][pr, slot * 128:(slot + 1) * 128]
                                    else:
                                        kk = kc_sb[pr, slot * 128:(slot + 1) * 128]
                                    i = e.matmul(ps[bS][:, :].rearrange("p (h t) -> p h t", t=128), kk, qblk[sl][pr, :, jj * 128:(jj + 1) * 128],
                                                 start=True, stop=(mki is None))
                                    if mki is not None:
                                        i = e.matmul(ps[bS][:, :].rearrange("p (h t) -> p h t", t=128), ident_bf[:],
                                                     nmk[:, mki, None, :].to_broadcast([128, 4, 128]), start=False, stop=True)
                                    return i
                                S.op("pe", f, reads=[f"kblk{sl}", f"qblk{sl}", "kc", "identbf", "nmk"], writes=[("ps", bS)])
                                S.op("act", lambda e, bS=bS, psl=psl: e.activation(out=pt[psl][:], in_=ps[bS][:, :], func=AF.Exp, scale=0.125),
                                     reads=[("ps", bS)], writes=[f"pt{psl}"])
                                cur = (kind, slot, psl, ci)
                                if pend is not None:
                                    _emit_pv(S, ps, pend, vblk[sl], vpc, ones_bf, pt, r, bO, bD, nch, f"vblk{sl}")
                                pend = cur
                            _emit_pv(S, ps, pend, vblk[sl], vpc, ones_bf, pt, r, bO, bD, nch, f"vblk{sl}")
                            S.op("dve", lambda e, pr=pr, r=r, bD=bD: e.tensor_tensor(
                                out=dsum[pr, :].rearrange("p (h t) -> p h t", t=128), in0=ps[bD][pr, :].rearrange("p (h t) -> p h t", t=128),
                                in1=es[pr, r * 4:(r + 1) * 4, None].to_broadcast([64, 4, 128]), op=ALU.add),
                                reads=[("ps", bD), "es"], writes=["dsum"])
                            S.op("dve", lambda e, pr=pr: e.reciprocal(out=rden[pr, :], in_=dsum[pr, :]), reads=["dsum"], writes=["rden"])
                            S.op("dve", lambda e, pr=pr, bO=bO, jj=jj, mx=mx: e.tensor_tensor(
                                out=mx[pr, 4:8, jj * 128:(jj + 1) * 128], in0=ps[bO][pr, :].rearrange("p (h t) -> p h t", t=128),
                                in1=rden[pr, :].rearrange("p (h t) -> p h t", t=128), op=ALU.mult),
                                reads=[("ps", bO), "rden"], writes=[mk])
                    out_proj_post(wout, 8, mx, mk, "wout", n, j0 * 128, 0, 2, ysb, "ysb")
                bcfg["mm"] = [0, 1, 2, 3, 4, 5]
                bcfg["st"] = [6, 7]
                S.barrier()
        ph01.close()

        def ffn(lay, ntok):
            with contextlib.ExitStack() as ph:
                hfs = [sb(f"hf{lay}_{i}", [128, 8, 512], BF16, stack=ph) for i in range(2)]
                hkeep = sb(f"hkeep{lay}", [128, 8, 2], BF16, stack=ph)
                wu = [sb(f"wu{lay}_{i}", [128, 8, 256], BF16, stack=ph) for i in range(3)]
                wd = [sb(f"wd{lay}_{i}", [128, NF, 128], BF16, stack=ph) for i in range(2)]
                tt_ = [[sb(f"tc{lay}_{i}_{k}", [128, 512], stack=ph) for k in range(2)] for i in range(2)]
                sg = [sb(f"sg{lay}_{i}", [128, 512], stack=ph) for i in range(2)]
                gbuf = sb(f"gb{lay}", [128, NF, 512], BF16, stack=ph)
                ysb = sb(f"ysbf{lay}", [128, 8, 512], stack=ph)
                nb = (ntok + FFN_BLK - 1) // FFN_BLK
                bsz = (ntok + nb - 1) // nb
                scrB = dict(sq=sb(f"sqB{lay}", [128, 8, 512], BF16, stack=ph), rt=sb(f"rtB{lay}", [128, 512], stack=ph),
                            rstd=sb(f"rstdB{lay}", [128, 512], stack=ph), k="B")

                def pre_block(bi):
                    s0 = bi * bsz
                    n = min(bsz, ntok - s0)
                    hf = hfs[bi % 2]
                    hfk = f"hf{bi % 2}"
                    if s0 == 0:
                        S.op("dve", lambda e: e.memset(hf[:, :, 0:1], 0.0), writes=[hfk])
                    else:
                        S.op("dve", lambda e: e.tensor_copy(hf[:, :, 0:1], hkeep[:, :, 0:1]), reads=["hkeep"], writes=[hfk])
                    prenorm(x_fm, s0, n + 1, lay, 3, 0, hf, 1, xk(s0, n + 1), hfk, scrB)
                    if bi < nb - 1:
                        S.op("dve", lambda e, n=n: e.tensor_copy(hkeep[:, :, 0:1], hf[:, :, n:n + 1]), reads=[hfk], writes=["hkeep"])
                wuc = 0
                wdc = 0
                for bi in range(nb):
                    s0 = bi * bsz
                    n = min(bsz, ntok - s0)
                    if bi == 0:
                        pre_block(0)
                    hf = hfs[bi % 2]
                    hfk = f"hf{bi % 2}"
                    def tail(fp):
                        hs = fp % 2
                        S.op("act", lambda e, hs=hs, n=n: e.activation(out=sg[hs][:, 0:n], in_=tt_[hs][0][:, 0:n], func=AF.Silu),
                             reads=[f"tc{hs}0"], writes=[f"sg{hs}"])
                        S.op("dve", lambda e, hs=hs, fp=fp, n=n: e.tensor_tensor(out=gbuf[:, fp, 0:n], in0=sg[hs][:, 0:n], in1=tt_[hs][1][:, 0:n], op=ALU.mult),
                             reads=[f"sg{hs}", f"tc{hs}1"], writes=[("gbuf", fp)])

                    for f_ in range(NF):
                        sl = wuc % 3
                        wuc += 1
                        if bi == 0:
                            S.dma("pool", f"wu{sl}", wu[sl][:], w_up[lay, f_], writes=[f"wu{sl}"])
                            S.dma("sp", f"wus{sl}", wub_d[lay, f_], wu[sl][:], reads=[f"wu{sl}"], writes=[("wub", f_)])
                        else:
                            S.dma("sp", f"wu{sl}", wu[sl][:], wub_d[lay, f_], reads=[("wub", f_)], writes=[f"wu{sl}"])
                        hs = f_ % 2
                        for k in range(2):
                            b = bank()

                            def f(e, k=k, b=b, sl=sl, n=n, hf=hf):
                                for kt in range(8):
                                    i = e.matmul(ps[b][:, 0:n + 2], wu[sl][:, kt, k * 128:(k + 1) * 128], hf[:, kt, 0:n + 2], start=(kt == 0), stop=(kt == 7))
                                return i
                            S.op("pe", f, reads=[f"wu{sl}", hfk], writes=[("ps", b)])
                            hk = f"hu{hs}{k}"
                            tk = f"tc{hs}{k}"
                            tc = tt_[hs][k]
                            ft = f_ + k * NF
                            S.op("act", lambda e, b=b, tc=tc, ft=ft, n=n: e.activation(out=tc[:, 0:n], in_=ps[b][:, 1:n + 1], func=AF.Identity,
                                                                                     scale=cw[:, lay, ft, 1:2], bias=cb[:, lay, ft:ft + 1]),
                                 reads=[("ps", b), "cw", "cb"], writes=[tk])
                            S.op("dve", lambda e, b=b, tc=tc, ft=ft, n=n: e.scalar_tensor_tensor(
                                out=tc[:, 0:n], in0=ps[b][:, 0:n], scalar=cw[:, lay, ft, 0:1], in1=tc[:, 0:n], op0=ALU.mult, op1=ALU.add),
                                reads=[("ps", b), tk, "cw"], writes=[tk])
                            S.op("dve", lambda e, b=b, tc=tc, ft=ft, n=n: e.scalar_tensor_tensor(
                                out=tc[:, 0:n], in0=ps[b][:, 2:n + 2], scalar=cw[:, lay, ft, 2:3], in1=tc[:, 0:n], op0=ALU.mult, op1=ALU.add),
                                reads=[("ps", b), tk, "cw"], writes=[tk])
                        if f_ > 0:
                            tail(f_ - 1)
                        if f_ == 8 and bi + 1 < nb:
                            pre_block(bi + 1)
                    tail(NF - 1)
                    for d in range(8):
                        sl = wdc % 2
                        wdc += 1
                        if bi == 0:
                            S.dma("pool", f"wd{sl}", wd[sl][:], w_down[lay, d], writes=[f"wd{sl}"])
                            S.dma("sp", f"wds{sl}", wdb_d[lay, d], wd[sl][:], reads=[f"wd{sl}"], writes=[("wdb", d)])
                        else:
                            S.dma("sp", f"wd{sl}", wd[sl][:], wdb_d[lay, d], reads=[("wdb", d)], writes=[f"wd{sl}"])
                        b = bank()

                        KS = 17

                        def f1(e, b=b, sl=sl, n=n):
                            for k in range(KS):
                                i = e.matmul(ps[b][:, 0:n], wd[sl][:, k, :], gbuf[:, k, 0:n], start=(k == 0), stop=False)
                            return i

                        def f2(e, b=b, sl=sl, n=n):
                            for k in range(KS, NF):
                                i = e.matmul(ps[b][:, 0:n], wd[sl][:, k, :], gbuf[:, k, 0:n], start=False, stop=(k == NF - 1))
                            return i
                        S.op("pe", f1, reads=[f"wd{sl}"] + [("gbuf", k) for k in range(KS)], writes=[("ps", b)])
                        S.op("pe", f2, reads=[f"wd{sl}"] + [("gbuf", k) for k in range(KS, NF)], writes=[("ps", b)])
                        S.op("act", lambda e, d=d, b=b, n=n: e.activation(out=sq[:, d, 0:n], in_=ps[b][:, 0:n], func=AF.Square),
                             reads=[("ps", b)], writes=["sq"])
                        S.op("act", lambda e, d=d, b=b, n=n: e.activation(out=ysb[:, d, 0:n], in_=ps[b][:, 0:n], func=AF.Copy, scale=coef[:, lay, 5, d, 0:1]),
                             reads=[("ps", b), "coef"], writes=["ysbf"])
                    post_update(ysb, n, s0, "ysbf")
                S.barrier()

        if debug_stage not in ("p0", "p1", "p2"):
            ffn(0, N_L0_FFN)

        if debug_stage not in ("p0", "p1", "p2", "p3"):
            with contextlib.ExitStack() as ph:
                wio = sb("wio", [128, 8, 2048], BF16, stack=ph)
                wo1 = sb("wo1", [128, 8, D], BF16, stack=ph)
                lngb = sb("lngb", [128, 2, D], stack=ph)
                wst = sb("wst", [128, 8, 128], BF16, stack=ph)
                bsb = sb("bsb", [128, 8, 128], stack=ph)
                h1 = sb("h1", [128, 8, 512], BF16, stack=ph)
                usb = sb("usb", [128, 8, 512], BF16, stack=ph)
                vg = [sb(f"vg{i}", [128, D], stack=ph) for i in range(2)]
                vn = [sb(f"vn{i}", [128, D], BF16, stack=ph) for i in range(2)]
                stat = [sb(f"stat{i}", [128, 8], stack=ph) for i in range(2)]
                s_sb = [sb(f"s_sb{i}", [128, 512], stack=ph) for i in range(2)]
                gated = usb
                ysb = sb("ysb1", [128, 8, 512], stack=ph)
                for c in range(4):
                    S.dma("pool", "wio", wio[:, :, c * 512:(c + 1) * 512], w_in1[:, :, c * 512:(c + 1) * 512], writes=["wio"])
                for c in range(2):
                    S.dma("pool", "wio", wo1[:, :, c * 512:(c + 1) * 512], w_out1[:, :, c * 512:(c + 1) * 512], writes=["wo1"])
                S.dma("pool", "wio", wst[:], wsT, writes=["wst"])
                S.dma("sp", "c0", lngb[:], ln_gb, writes=["lngb"])
                S.dma("sp", "c0", bsb[:], bs_bc, writes=["bsb"])
                S.seal("c0", ["lngb", "bsb"])
                S.seal("wio", ["wio", "wo1", "wst"])
                ntile = N_L1_MIX // 128
                for j0 in range(0, ntile, 4):
                    nj = min(4, ntile - j0)
                    n = nj * 128
                    prenorm(x_fm, j0 * 128, n, 1, 0, 0, h1, 0, xk(j0 * 128, n), "h1")
                    for ft in range(8):
                        b = bank()

                        def f(e, ft=ft, b=b, n=n):
                            for kt in range(8):
                                i = e.matmul(ps[b][:, 0:n], wio[:, kt, ft * 128:(ft + 1) * 128], h1[:, kt, 0:n], start=(kt == 0), stop=(kt == 7))
                            return i
                        S.op("pe", f, reads=["wio", "h1"], writes=[("ps", b)])
                        S.op("act", lambda e, ft=ft, b=b, n=n: e.activation(out=usb[:, ft, 0:n], in_=ps[b][:, 0:n], func=AF.Gelu_apprx_tanh),
                             reads=[("ps", b)], writes=["usb"])
                    def stages(jj):
                        p = jj % 2
                        vg_, vn_, st_, ss_ = vg[p], vn[p], stat[p], s_sb[p]
                        kvg, kvn, kst, kss = f"vg{p}", f"vn{p}", f"stat{p}", f"s_sb{p}"
                        yield lambda: S.op("dve", lambda e: e.memset(st_[:], 0.0), writes=[kst])

                        def vproj():
                            for hh in range(2):
                                b = bank()

                                def f(e, hh=hh, b=b):
                                    for kt in range(8):
                                        i = e.matmul(ps[b][:, :], h1[:, kt, jj * 128:(jj + 1) * 128], wio[:, kt, 1024 + hh * 512:1536 + hh * 512], start=(kt == 0), stop=(kt == 7))
                                    return i
                                S.op("pe", f, reads=["wio", "h1"], writes=[("ps", b)])
                                S.op("act", lambda e, hh=hh, b=b: e.activation(out=vg_[:, hh * 512:(hh + 1) * 512], in_=ps[b][:, :], func=AF.Gelu_apprx_tanh,
                                                                               accum_out=st_[:, hh:hh + 1]),
                                     reads=[("ps", b)], writes=[kvg, kst])
                        yield vproj
                        yield lambda: S.op("dve", lambda e: e.tensor_tensor(out=st_[:, 2:3], in0=st_[:, 0:1], in1=st_[:, 1:2], op=ALU.add), reads=[kst], writes=[kst])
                        yield lambda: S.op("dve", lambda e: e.tensor_scalar(out=st_[:, 3:4], in0=st_[:, 2:3], scalar1=-1.0 / D, scalar2=None, op0=ALU.mult), reads=[kst], writes=[kst])
                        yield lambda: S.op("dve", lambda e: e.tensor_scalar(out=vg_[:], in0=vg_[:], scalar1=st_[:, 3:4], scalar2=None, op0=ALU.add),
                                           reads=[kvg, kst], writes=[kvg])
                        yield lambda: S.op("act", lambda e: e.activation(out=vn_[:], in_=vg_[:], func=AF.Square, accum_out=st_[:, 4:5]), reads=[kvg], writes=[kvn, kst])
                        yield lambda: S.op("act", lambda e: e.activation(out=st_[:, 5:6], in_=st_[:, 4:5], func=AF.Sqrt, scale=1.0 / D, bias=eps_t[:, 0:1]),
                                           reads=[kst, "eps"], writes=[kst])
                        yield lambda: S.op("dve", lambda e: e.reciprocal(out=st_[:, 6:7], in_=st_[:, 5:6]), reads=[kst], writes=[kst])
                        yield lambda: S.op("dve", lambda e: e.scalar_tensor_tensor(out=vg_[:], in0=vg_[:], scalar=st_[:, 6:7], in1=lngb[:, 0, :], op0=ALU.mult, op1=ALU.mult),
                                           reads=[kvg, kst, "lngb"], writes=[kvg])
                        yield lambda: S.op("pool", lambda e: e.tensor_tensor(out=vn_[:], in0=vg_[:], in1=lngb[:, 1, :], op=ALU.add), reads=[kvg, "lngb"], writes=[kvn])
                        for g4 in range(2):
                            def spat(g4=g4):
                                b = bank()

                                def f(e, b=b):
                                    for gg in range(4):
                                        gi = g4 * 4 + gg
                                        i = e.matmul(ps[b][:, gg * 128:(gg + 1) * 128], vn_[:, gi * 128:(gi + 1) * 128], wst[:, gi, :], start=True, stop=True)
                                    return i
                                S.op("pe", f, reads=[kvn, "wst"], writes=[("ps", b)])
                                S.op("dve", lambda e, b=b: e.tensor_tensor(out=ss_[:, :], in0=ps[b][:, :], in1=bsb[:, g4 * 4:(g4 + 1) * 4, :].rearrange("p g t -> p (g t)"), op=ALU.add),
                                     reads=[("ps", b), "bsb"], writes=[kss])
                            yield spat
                            yield lambda g4=g4: S.op("dve", lambda e: e.tensor_tensor(out=gated[:, g4 * 4:(g4 + 1) * 4, jj * 128:(jj + 1) * 128],
                                                                                    in0=ss_[:, :].rearrange("p (g t) -> p g t", t=128),
                                                                                    in1=usb[:, g4 * 4:(g4 + 1) * 4, jj * 128:(jj + 1) * 128], op=ALU.mult),
                                                     reads=[kss, "usb"], writes=["usb"])

                    for ja in range(0, nj, 2):
                        gens = [list(stages(jj)) for jj in range(ja, min(ja + 2, nj))]
                        for si in range(len(gens[0])):
                            for gl in gens:
                                gl[si]()
                    out_proj_post(wo1, 8, gated, "usb", "wo1", n, j0 * 128, 1, 2, ysb, "ysb1")
                S.barrier()

        if debug_stage not in ("p0", "p1", "p2", "p3", "p4"):
            ffn(1, TOWN)

        with contextlib.ExitStack() as ph:
            ost = [sb(f"ost{i}", [128, D], stack=ph) for i in range(2)]
            for i in range(TOWN // 128):
                sl = i % 2
                for hlf in range(2):
                    b = bank()

                    def f(e, hlf=hlf, b=b, i=i):
                        for c in range(4):
                            cc = hlf * 4 + c
                            i2 = e.transpose(ps[b][:, c * 128:(c + 1) * 128], x_fm[:, cc, i * 128:(i + 1) * 128], ident[:])
                        return i2
                    S.op("pe", f, reads=[("x", i), "ident"], writes=[("ps", b)])
                    if hlf == 0:
                        S.op("act", lambda e, b=b, sl=sl: e.activation(out=ost[sl][:, 0:512], in_=ps[b][:, :], func=AF.Copy), reads=[("ps", b)], writes=[f"ost{sl}"])
                    else:
                        S.op("dve", lambda e, b=b, sl=sl: e.tensor_copy(ost[sl][:, 512:1024], ps[b][:, :]), reads=[("ps", b)], writes=[f"ost{sl}"])
                S.dma("sp", f"ost{sl}", out_loc[i], ost[sl][:], reads=[f"ost{sl}"], writes=[f"out{i}"])
            S.barrier()
    return nc


def _emit_pv(S, ps, pend, vb, vpc, ones_bf, pt, r, bO, bD, nch, vkey):
    kind, slot, psl, ci = pend

    def f(e):
        vv = vb[:, slot, r, :] if kind == "l" else vpc[:, slot, r, :]
        e.matmul(ps[bO][:, :], vv, pt[psl][:], start=(ci == 0), stop=(ci == nch - 1))
        return e.matmul(ps[bD][:, :], ones_bf[:], pt[psl][:], start=(ci == 0), stop=(ci == nch - 1))
    S.op("pe", f, reads=[f"pt{psl}", vkey, "vpc", "ones"], writes=[("ps", bO), ("ps", bD)])


def _fm(v):
    v = np.asarray(v, np.float32)
    sh = v.shape
    v = v.reshape(sh[:-1] + (sh[-1] // 128, 128))
    return np.ascontiguousarray(np.moveaxis(v, -1, 0))


def _kt(w):
    K, C = w.shape
    return np.ascontiguousarray(w.reshape(K // 128, 128, C).transpose(1, 0, 2))


def _positions(hf):
    i = np.arange(T)
    return i if hf == 0 else (L - 1 - i)


def _rope_tables(pos):
    inv = (10000.0 ** (-np.arange(16, dtype=np.float32) / 16)).astype(np.float32)
    row = (pos // 64).astype(np.float32)
    col = (pos % 64).astype(np.float32)
    C = np.zeros((128, T), np.float32)
    Sg = np.zeros((128, T), np.float32)
    for rr_ in range(128):
        i = rr_ % 64
        axis, half, f = i // 32, (i % 32) // 16, i % 16
        ang = (row if axis == 0 else col) * inv[f]
        C[rr_] = np.cos(ang)
        Sg[rr_] = np.sin(ang) * (-1.0 if half == 0 else 1.0)
    return C, Sg


def _pool_mats(pos):
    M = np.zeros((128, 4, 4, 128), np.float32)
    for gi, w in enumerate(POOL_W):
        hw = w // 2
        for kind, (jo, dj) in enumerate(((2, -1), (2, 0), (2, 1), (0, 0))):
            ji = jo + dj
            for to in range(128):
                p = pos[jo * 128 + to]
                st_, en = max(p - hw, 0), min(p + hw, L)
                cnt = en - st_
                for ti in range(128):
                    q = pos[ji * 128 + ti]
                    v = 0.0
                    if st_ <= q < en:
                        v += 1.0 / cnt
                    if dj == 0 and ti == to:
                        v -= 1.0
                    M[ti, gi, kind, to] = v
    return M


_PROG = {}


def kernel(x, c, ctx, c_ctx, w_ada, b_ada, g_mix_pre, g_mix_post, g_ffn_pre, g_ffn_post,
           w_in_even, w_pool, pool_scale, attn_sink, w_out_even,
           w_in_odd, sgu_ln_g, sgu_ln_b, sgu_w, sgu_b, w_out_odd,
           w_ffn_up, ffn_conv_w, ffn_conv_b, w_ffn_down, _debug_stage=None):
    f32 = np.float32
    x = np.asarray(x, f32)
    ctx = np.asarray(ctx, f32)
    w_ada_l = np.ascontiguousarray(np.asarray(w_ada, f32).reshape(2, 8, 128, 12, 512).transpose(0, 3, 2, 1, 4))
    b_ada_l = np.ascontiguousarray(np.asarray(b_ada, f32).reshape(2, 48, 128).transpose(2, 0, 1))
    gvec = np.stack([_fm(np.asarray(a, f32)) for a in (g_mix_pre, g_mix_post, g_ffn_pre, g_ffn_post)], axis=0)
    gvec = np.ascontiguousarray(gvec.transpose(1, 2, 0, 3))
    wi = np.asarray(w_in_even, f32)[0]
    perm64 = np.array([(i // 32) * 32 + (1 - (i % 32) // 16) * 16 + (i % 16) for i in range(64)])
    qcols, qpcols = [], []
    for j in range(4):
        for h in (j, 4 + j):
            base = 512 + h * 64
            qcols += list(base + np.arange(64))
            qpcols += list(base + perm64)
    kcols = list(1024 + np.arange(128))
    kpcols = [1024 + hh * 64 + p for hh in range(2) for p in perm64]
    cols = qcols + kcols + qpcols + kpcols + list(range(0, 512)) + list(range(1152, 1280))
    w_in0 = _kt(wi[:, cols])
    w_pool_l = np.ascontiguousarray(np.asarray(w_pool, f32)[0].transpose(1, 0, 2))
    pool_sc = _fm(np.asarray(pool_scale, f32)[0])
    sink_bc = np.ascontiguousarray(np.broadcast_to(np.asarray(attn_sink, f32)[0][None, :], (128, 8)))
    nm = np.zeros((128, 2, 128), f32)
    kk = np.arange(128)[:, None]
    qq = np.arange(128)[None, :]
    nm[:, 0, :] = np.where(kk >= qq, 0.0, -30000.0)
    nm[:, 1, :] = np.where(kk <= qq, 0.0, -30000.0)
    rows = list(range(512))
    for j in range(4):
        rows += list(512 + j * 64 + np.arange(64)) + list(512 + (4 + j) * 64 + np.arange(64))
    w_out0 = _kt(np.asarray(w_out_even, f32)[0][rows, :])
    w_in1 = _kt(np.asarray(w_in_odd, f32)[0])
    ln_gb = np.ascontiguousarray(np.broadcast_to(np.stack([np.asarray(sgu_ln_g, f32)[0], np.asarray(sgu_ln_b, f32)[0]], 0)[None], (128, 2, D)))
    w_out1 = _kt(np.asarray(w_out_odd, f32)[0])
    wu = np.asarray(w_ffn_up, f32).reshape(2, 8, 128, 2, NF, 128)
    w_up_l = np.ascontiguousarray(wu.transpose(0, 4, 2, 1, 3, 5)).reshape(2, NF, 128, 8, 256)
    wd = np.asarray(w_ffn_down, f32).reshape(2, NF, 128, 8, 128)
    w_down_l = np.ascontiguousarray(wd.transpose(0, 3, 2, 1, 4))
    cwf = np.asarray(ffn_conv_w, f32).reshape(2, 3, 44, 128)
    conv_b_l = np.ascontiguousarray(np.asarray(ffn_conv_b, f32).reshape(2, 44, 128).transpose(2, 0, 1))
    sw = np.asarray(sgu_w, f32)[0]
    sbias = np.asarray(sgu_b, f32)[0]
    ident = np.eye(128, dtype=f32)

    per_half = []
    for hf in range(2):
        pos = _positions(hf)
        C, Sg = _rope_tables(pos)
        mp = _pool_mats(pos)
        cwl = cwf if hf == 0 else cwf[:, ::-1]
        conv_w_l = np.ascontiguousarray(cwl.transpose(3, 0, 2, 1))
        swl = sw if hf == 0 else sw[:, ::-1, ::-1]
        wsT = np.ascontiguousarray(swl.transpose(2, 0, 1))
        sbl = sbias if hf == 0 else sbias[:, ::-1]
        bs_bc = np.ascontiguousarray(np.broadcast_to(sbl[None], (128, 8, 128)))
        per_half.append(dict(pos=pos, ropeC=C, ropeS=Sg, mpool=mp, conv_w=conv_w_l, wsT=wsT, bs_bc=bs_bc))

    in_maps = []
    for core in range(8):
        b, hf = core // 2, core % 2
        ph = per_half[hf]
        cv = np.stack([np.asarray(c, f32)[b], np.asarray(c_ctx, f32)], axis=-1)
        in_maps.append(dict(
            x_loc=np.ascontiguousarray(x[b][ph["pos"]].reshape(NT, 128, D)),
            ctx_in=np.ascontiguousarray(ctx[b].reshape(2, 128, D)),
            cvec=np.ascontiguousarray(cv.reshape(8, 128, 2).transpose(1, 0, 2)),
            w_ada=w_ada_l, b_ada=b_ada_l, gvec=gvec, w_in0=w_in0, ropeC=ph["ropeC"], ropeS=ph["ropeS"],
            w_pool=w_pool_l, pool_sc=pool_sc, mpool=ph["mpool"], sink_bc=sink_bc, negmask=nm, ident_in=ident,
            w_out0=w_out0, w_in1=w_in1, ln_gb=ln_gb, wsT=ph["wsT"], bs_bc=ph["bs_bc"], w_out1=w_out1,
            w_up=w_up_l, w_down=w_down_l, conv_w=ph["conv_w"], conv_b=conv_b_l,
        ))
    key = _debug_stage
    if key not in _PROG:
        _PROG[key] = build_program(_debug_stage)
    res = run_bass_kernel_spmd(_PROG[key], in_maps, core_ids=list(range(8)))
    out = np.empty((4, L, D), f32)
    for core in range(8):
        b, hf = core // 2, core % 2
        o = np.asarray(res.results[core]["out_loc"], f32).reshape(TOWN, D)
        out[b, per_half[hf]["pos"][:TOWN]] = o
    return out
```

```python
import contextlib
import numpy as np
import concourse.bass as bass
import concourse.mybir as mybir
from concourse.bass_utils import run_bass_kernel_spmd

F32 = mybir.dt.float32
BF16 = mybir.dt.bfloat16
AF = mybir.ActivationFunctionType
ALU = mybir.AluOpType

D = 1024
L = 4096
NT = 19
T = NT * 128
TOWN = 2048
CTX = 256
DFF = 2816
NF = 22
EPS = 1e-6
POOL_W = (2, 4, 8, 16)
N_L0_MIX = 18 * 128
N_L0_FFN = 17 * 128
N_L1_MIX = 17 * 128
FFN_BLK = 510


class Sched:
    def __init__(self, nc, st):
        self.nc = nc
        self.st = st
        self.eng = dict(pe=nc.tensor, act=nc.scalar, dve=nc.vector, pool=nc.gpsimd, sp=nc.sync)
        self.sem = {k: st.enter_context(nc.semaphore("sem_" + k)) for k in ("pe", "act", "dve", "pool")}
        self.cnt = {k: 0 for k in self.sem}
        self.dsem = {}
        self.dcnt = {}
        self.waited = {k: {} for k in self.eng}
        self.lastw = {}
        self.readers = {}

    def _deps(self, reads, writes):
        deps = {}

        def add(k, v):
            if deps.get(k, 0) < v:
                deps[k] = v

        for r in reads:
            t = self.lastw.get(r)
            if t:
                add(*t)
        for w in writes:
            t = self.lastw.get(w)
            if t:
                add(*t)
            for k, v in self.readers.get(w, {}).items():
                add(k, v)
        return deps

    def _semof(self, k):
        return self.sem[k] if k in self.sem else self.dsem[k]

    def _wait(self, eng, deps):
        e = self.eng[eng]
        for k, v in deps.items():
            if eng == "pe" and k == "pe":
                continue
            if self.waited[eng].get(k, 0) >= v:
                continue
            e.wait_ge(self._semof(k), v)
            self.waited[eng][k] = v

    def _commit(self, tok, reads, writes):
        k, v = tok
        for r in reads:
            d = self.readers.setdefault(r, {})
            if d.get(k, 0) < v:
                d[k] = v
        for w in writes:
            self.lastw[w] = tok
            self.readers[w] = {}

    def op(self, eng, fn, reads=(), writes=()):
        self._wait(eng, self._deps(reads, writes))
        inst = fn(self.eng[eng])
        self.cnt[eng] += 1
        inst.then_inc(self.sem[eng], 1)
        self._commit((eng, self.cnt[eng]), reads, writes)

    def dma(self, q, key, out, in_, reads=(), writes=()):
        if key not in self.dsem:
            self.dsem[key] = self.st.enter_context(self.nc.semaphore("dsem_" + key))
            self.dcnt[key] = 0
        self._wait(q, self._deps(reads, writes))
        self.eng[q].dma_start(out=out, in_=in_).then_inc(self.dsem[key], 16)
        self.dcnt[key] += 16
        self._commit((key, self.dcnt[key]), reads, writes)

    def seal(self, key, resources):
        for r in resources:
            self.lastw[r] = (key, self.dcnt[key])

    def barrier(self, engines=("pe", "act", "dve", "pool", "sp")):
        for e in engines:
            deps = {k: v for k, v in self.cnt.items() if v > 0}
            deps.update({k: v for k, v in self.dcnt.items() if v > 0})
            deps.pop(e, None) if e == "pe" else None
            self._wait(e, deps)


class Ctx:
    pass


def build_program(debug_stage=None):
    nc = bass.Bass("TRN2", target_bir_lowering=False)
    g = Ctx()

    def din(name, shape, dt=F32):
        return nc.dram_tensor(name, list(shape), dt, kind="ExternalInput").ap()

    x_loc = din("x_loc", [NT, 128, D])
    ctx_in = din("ctx_in", [2, 128, D])
    cvec = din("cvec", [128, 8, 2])
    w_ada = din("w_ada", [2, 12, 128, 8, 512])
    b_ada = din("b_ada", [128, 2, 48])
    gvec = din("gvec", [128, 2, 4, 8])
    w_in0 = din("w_in0", [128, 8, 1920])
    ropeC = din("ropeC", [128, T])
    ropeS = din("ropeS", [128, T])
    w_pool = din("w_pool", [128, 4, 128])
    pool_sc = din("pool_sc", [128, 4])
    mpool = din("mpool", [128, 4, 4, 128])
    sink_bc = din("sink_bc", [128, 8])
    negmask = din("negmask", [128, 2, 128])
    ident_in = din("ident_in", [128, 128])
    w_out0 = din("w_out0", [128, 8, D])
    w_in1 = din("w_in1", [128, 8, 2048])
    ln_gb = din("ln_gb", [128, 2, D])
    wsT = din("wsT", [128, 8, 128])
    bs_bc = din("bs_bc", [128, 8, 128])
    w_out1 = din("w_out1", [128, 8, D])
    w_up = din("w_up", [2, NF, 128, 8, 256])
    w_down = din("w_down", [2, 8, 128, NF, 128])
    conv_w = din("conv_w", [128, 2, 44, 3])
    conv_b = din("conv_b", [128, 2, 44])
    out_loc = nc.dram_tensor("out_loc", [TOWN // 128, 128, D], F32, kind="ExternalOutput").ap()
    qk_d = nc.dram_tensor("qk_d", [5, 128, T], BF16).ap()
    u_d = nc.dram_tensor("u_d", [NT, 128, 512], BF16).ap()
    v_d = nc.dram_tensor("v_d", [NT, 128, 128], BF16).ap()
    wub_d = nc.dram_tensor("wub_d", [2, NF, 128, 8, 256], BF16).ap()
    wdb_d = nc.dram_tensor("wdb_d", [2, 8, 128, NF, 128], BF16).ap()

    with contextlib.ExitStack() as st:
        E = st.enter_context
        S = Sched(nc, st)

        def sb(name, shape, dt=F32, stack=None):
            return (stack or st).enter_context(nc.sbuf_tensor(name, list(shape), dt))

        ps = [E(nc.psum_tensor(f"ps{i}", [128, 512], F32)) for i in range(8)]
        rr = {"mm": 0, "st": 0}

        bcfg = {"mm": [0, 1, 2, 3, 4, 5], "st": [6, 7]}

        def bank(pool="mm"):
            lst = bcfg[pool]
            ctr = "mm" if bcfg["st"] is bcfg["mm"] else pool
            i = lst[rr[ctr] % len(lst)]
            rr[ctr] += 1
            return i

        x_fm = sb("x_fm", [128, 8, T])
        ident = sb("ident", [128, 128])
        ident_bf = sb("ident_bf", [128, 128], BF16)
        ones_bf = sb("ones_bf", [128, 128], BF16)
        coef = sb("coef", [128, 2, 6, 8, 2])
        gv = sb("gv", [128, 2, 4, 8])
        cw = sb("cw", [128, 2, 44, 3])
        cb = sb("cb", [128, 2, 44])
        eps_t = sb("eps_t", [128, 1])
        sq = sb("sq", [128, 8, 512], BF16)
        rt = sb("rt", [128, 512])
        rstd = sb("rstd", [128, 512])
        tn = [sb(f"tn{i}", [128, 512]) for i in range(2)]
        kc_sb = sb("kc_sb", [128, CTX], BF16)
        vpc = sb("vpc", [128, 2, 2, 128], BF16)

        cv = sb("cv", [128, 8, 2])
        bada = sb("bada", [128, 2, 48])
        S.dma("sp", "c0", ident[:], ident_in, writes=["ident"])
        S.dma("sp", "c0", gv[:], gvec, writes=["gv"])
        S.dma("sp", "c0", cw[:], conv_w, writes=["cw"])
        S.dma("sp", "c0", cb[:], conv_b, writes=["cb"])
        S.dma("sp", "c0", cv[:], cvec, writes=["cv"])
        S.dma("sp", "c0", bada[:], b_ada, writes=["bada"])
        S.seal("c0", ["ident", "gv", "cw", "cb", "cv", "bada"])
        S.op("dve", lambda e: e.memset(ones_bf[:], 1.0), writes=["ones"])
        S.op("dve", lambda e: e.memset(eps_t[:], EPS), writes=["eps"])
        S.op("dve", lambda e: e.tensor_copy(ident_bf[:], ident[:]), reads=["ident"], writes=["identbf"])

        def xk(c0, n):
            return [("x", t) for t in range(c0 // 128, (c0 + n - 1) // 128 + 1)]

        scrA = dict(sq=sq, rt=rt, rstd=rstd, k="")

        def prenorm(src, c0, n, lay, ka, col, dst, doff, srckey, dstkey, scr=None):
            scr = scr or scrA
            sq, rt, rstd, sk = scr["sq"], scr["rt"], scr["rstd"], scr["k"]
            S.op("act", lambda e: e.activation(out=sq[:, :, 0:n], in_=src[:, :, c0:c0 + n], func=AF.Square),
                 reads=srckey, writes=["sq" + sk])
            b = bank("st")

            def f(e):
                for kt in range(8):
                    i = e.matmul(ps[b][:, 0:n], ones_bf[:], sq[:, kt, 0:n], start=(kt == 0), stop=(kt == 7))
                return i
            S.op("pe", f, reads=["sq" + sk, "ones"], writes=[("ps", b)])
            S.op("act", lambda e: e.activation(out=rt[:, 0:n], in_=ps[b][:, 0:n], func=AF.Sqrt, scale=1.0 / D, bias=eps_t[:, 0:1]),
                 reads=[("ps", b), "eps"], writes=["rt" + sk])
            S.op("dve", lambda e: e.reciprocal(out=rstd[:, 0:n], in_=rt[:, 0:n]), reads=["rt" + sk], writes=["rstd" + sk])
            for kt in range(8):
                ts = kt % 2
                S.op("dve", lambda e, kt=kt, ts=ts: e.tensor_tensor(out=tn[ts][:, 0:n], in0=src[:, kt, c0:c0 + n], in1=rstd[:, 0:n], op=ALU.mult),
                     reads=srckey + ["rstd" + sk], writes=[f"tn{ts}"])
                S.op("act", lambda e, kt=kt, ts=ts: e.activation(out=dst[:, kt, doff:doff + n], in_=tn[ts][:, 0:n], func=AF.Identity,
                                                                 scale=coef[:, lay, ka, kt, col:col + 1], bias=coef[:, lay, ka + 1, kt, col:col + 1]),
                     reads=[f"tn{ts}", "coef"], writes=[dstkey])

        def post_update(ysb, n, c0, ykey):
            b = bank("st")

            def f(e):
                for kt in range(8):
                    i = e.matmul(ps[b][:, 0:n], ones_bf[:], sq[:, kt, 0:n], start=(kt == 0), stop=(kt == 7))
                return i
            S.op("pe", f, reads=["sq", "ones"], writes=[("ps", b)])
            S.op("act", lambda e: e.activation(out=rt[:, 0:n], in_=ps[b][:, 0:n], func=AF.Sqrt, scale=1.0 / D, bias=eps_t[:, 0:1]),
                 reads=[("ps", b), "eps"], writes=["rt"])
            S.op("dve", lambda e: e.reciprocal(out=rstd[:, 0:n], in_=rt[:, 0:n]), reads=["rt"], writes=["rstd"])
            S.op("dve", lambda e: e.tensor_tensor(out=ysb[:, :, 0:n], in0=ysb[:, :, 0:n],
                                                  in1=rstd[:, None, 0:n].to_broadcast([128, 8, n]), op=ALU.mult),
                 reads=[ykey, "rstd"], writes=[ykey])
            S.op("dve", lambda e: e.tensor_tensor(out=x_fm[:, :, c0:c0 + n], in0=x_fm[:, :, c0:c0 + n], in1=ysb[:, :, 0:n], op=ALU.add),
                 reads=[ykey] + xk(c0, n), writes=xk(c0, n))

        def out_proj_post(W, nk, rhs, rhskey, wkey, n, c0, lay, kg, ysb, ykey):
            for d in range(8):
                b = bank()

                def f(e, d=d, b=b):
                    for k in range(nk):
                        i = e.matmul(ps[b][:, 0:n], W[:, k, d * 128:(d + 1) * 128], rhs[:, k, 0:n], start=(k == 0), stop=(k == nk - 1))
                    return i
                S.op("pe", f, reads=[rhskey, wkey], writes=[("ps", b)])
                S.op("act", lambda e, d=d, b=b: e.activation(out=sq[:, d, 0:n], in_=ps[b][:, 0:n], func=AF.Square),
                     reads=[("ps", b)], writes=["sq"])
                S.op("act", lambda e, d=d, b=b: e.activation(out=ysb[:, d, 0:n], in_=ps[b][:, 0:n], func=AF.Copy,
                                                             scale=coef[:, lay, kg, d, 0:1]),
                     reads=[("ps", b), "coef"], writes=[ykey])
            post_update(ysb, n, c0, ykey)

        def convert_ffn_weights(lay):
            key = f"cvt{lay}"
            res = []
            for f_ in range(NF):
                S.dma("pool", key, wub_d[lay, f_], w_up[lay, f_], writes=[("wub", lay, f_)])
                res.append(("wub", lay, f_))
            for d in range(8):
                S.dma("pool", key, wdb_d[lay, d], w_down[lay, d], writes=[("wdb", lay, d)])
                res.append(("wdb", lay, d))
            S.seal(key, res)

        ph01 = contextlib.ExitStack()
        g.ctx_fm = sb("ctx_fm", [128, 8, CTX], stack=ph01)
        wout = sb("wout", [128, 8, D], BF16, stack=ph01)
        wpl = sb("wpl", [128, 4, 128], BF16, stack=ph01)
        mpl = sb("mpl", [128, 4, 4, 128], BF16, stack=ph01)
        nmk = sb("nmk", [128, 2, 128], BF16, stack=ph01)
        with contextlib.ExitStack() as ph:
            cs = sb("cs", [128, 8, 2], BF16, stack=ph)
            wa = [sb(f"wa{i}", [128, 8, 512], BF16, stack=ph) for i in range(3)]
            modsb = sb("modsb", [128, 2, 6, 8, 2], stack=ph)
            xs = [sb(f"xs{i}", [128, D], stack=ph) for i in range(2)]
            S.op("act", lambda e: e.activation(out=cs[:], in_=cv[:], func=AF.Silu), reads=["cv"], writes=["cs"])
            for lay in range(2):
                bm = bank("st")
                for ch in range(12):
                    sl = (lay * 12 + ch) % 3
                    S.dma("pool", f"wa{sl}", wa[sl][:], w_ada[lay, ch], writes=[f"wa{sl}"])

                    def f(e, ch=ch, sl=sl, bm=bm):
                        for ft in range(4):
                            o = (ch * 4 + ft) * 2
                            for kt in range(8):
                                i = e.matmul(ps[bm][:, o:o + 2], wa[sl][:, kt, ft * 128:(ft + 1) * 128], cs[:, kt, :], start=(kt == 0), stop=(kt == 7))
                        return i
                    S.op("pe", f, reads=[f"wa{sl}", "cs"], writes=[("ps", bm)])
                S.op("dve", lambda e, lay=lay, bm=bm: e.tensor_tensor(
                    out=modsb[:, lay].rearrange("p j c t -> p (j c) t"),
                    in0=ps[bm][:, 0:96].rearrange("p (a t) -> p a t", t=2),
                    in1=bada[:, lay, :, None].to_broadcast([128, 48, 2]), op=ALU.add),
                    reads=[("ps", bm), "bada"], writes=["modsb"])
                for (ka, jsc, jsh, jgt, gpre, gpost) in ((0, 1, 0, 2, 0, 1), (3, 4, 3, 5, 2, 3)):
                    S.op("dve", lambda e, lay=lay, ka=ka, jsc=jsc: e.tensor_scalar(out=coef[:, lay, ka], in0=modsb[:, lay, jsc], scalar1=1.0, scalar2=None, op0=ALU.add),
                         reads=["modsb"], writes=["coef"])
                    S.op("dve", lambda e, lay=lay, ka=ka, gpre=gpre: e.tensor_tensor(out=coef[:, lay, ka], in0=coef[:, lay, ka],
                                                                                   in1=gv[:, lay, gpre, :, None].to_broadcast([128, 8, 2]), op=ALU.mult),
                         reads=["coef", "gv"], writes=["coef"])
                    S.op("dve", lambda e, lay=lay, ka=ka, jsh=jsh: e.tensor_copy(coef[:, lay, ka + 1], modsb[:, lay, jsh]),
                         reads=["modsb"], writes=["coef"])
                    S.op("dve", lambda e, lay=lay, ka=ka, jgt=jgt, gpost=gpost: e.tensor_tensor(
                        out=coef[:, lay, ka + 2], in0=modsb[:, lay, jgt], in1=gv[:, lay, gpost, :, None].to_broadcast([128, 8, 2]), op=ALU.mult),
                        reads=["modsb", "gv"], writes=["coef"])

            def load_T(src_ap, dst, t0, i, dkey):
                sl = i % 2
                S.dma("sp", f"xs{sl}", xs[sl][:], src_ap, writes=[f"xs{sl}"])
                for hlf in range(2):
                    b = bank()

                    def f(e, hlf=hlf, b=b, sl=sl):
                        for c in range(4):
                            cc = hlf * 4 + c
                            i2 = e.transpose(ps[b][:, c * 128:(c + 1) * 128], xs[sl][:, cc * 128:(cc + 1) * 128], ident[:])
                        return i2
                    S.op("pe", f, reads=[f"xs{sl}", "ident"], writes=[("ps", b)])
                    eng = "act" if hlf == 0 else "dve"
                    if eng == "act":
                        S.op("act", lambda e, hlf=hlf, b=b: e.activation(out=dst[:, hlf * 4:hlf * 4 + 4, t0:t0 + 128],
                                                                         in_=ps[b][:, :].rearrange("p (c t) -> p c t", t=128), func=AF.Copy),
                             reads=[("ps", b)], writes=[dkey])
                    else:
                        S.op("dve", lambda e, hlf=hlf, b=b: e.tensor_copy(dst[:, hlf * 4:hlf * 4 + 4, t0:t0 + 128],
                                                                          ps[b][:, :].rearrange("p (c t) -> p c t", t=128)),
                             reads=[("ps", b)], writes=[dkey])
            for i in range(NT):
                load_T(x_loc[i], x_fm, i * 128, i, ("x", i))
            for i in range(2):
                load_T(ctx_in[i], g.ctx_fm, i * 128, NT + i, "ctx")
            S.barrier()

        if debug_stage == "p0":
            pass
        if debug_stage not in ("p0",):
            with contextlib.ExitStack() as ph:
                win = sb("win", [128, 8, 1920], BF16, stack=ph)
                hb = [sb(f"hb{i}", [128, 8, 512], BF16, stack=ph) for i in range(2)]
                rcf = sb("rcf", [128, 2, T], stack=ph)
                t1s = [sb(f"t1_{i}", [128, 512], stack=ph) for i in range(2)]
                t2s = [sb(f"t2_{i}", [128, 512], stack=ph) for i in range(2)]
                qst = [sb(f"qst{i}", [128, 5, 512], BF16, stack=ph) for i in range(1)] * 2
                ust = [sb(f"ust{i}", [128, 512], BF16, stack=ph) for i in range(2)]
                vst = [sb(f"vst{i}", [128, 128], BF16, stack=ph) for i in range(2)]
                for c in range(4):
                    S.dma("pool", "win", win[:, :, c * 480:(c + 1) * 480], w_in0[:, :, c * 480:(c + 1) * 480], writes=["win"])
                for c in range(2):
                    S.dma("pool", "wout", wout[:, :, c * 512:(c + 1) * 512], w_out0[:, :, c * 512:(c + 1) * 512], writes=["wout"])
                S.dma("pool", "wout", wpl[:], w_pool, writes=["wpl"])
                S.dma("pool", "wout", mpl[:], mpool, writes=["mpl"])
                S.dma("pool", "wout", nmk[:], negmask, writes=["nmk"])
                S.seal("wout", ["wout", "wpl", "mpl", "nmk"])
                S.dma("sp", "rcf", rcf[:, 0, :], ropeC, writes=["rcf"])
                S.dma("sp", "rcf", rcf[:, 1, :], ropeS, writes=["rcf"])
                S.op("dve", lambda e: e.memset(vpc[:], 0.0), writes=["vpc"])
                nblk = (T + 511) // 512
                for bi in range(nblk):
                    c0 = bi * 512
                    n = min(512, T - c0)
                    sl = bi % 2
                    h = hb[sl]
                    hk = f"hb{sl}"
                    if bi == 0:
                        prenorm(x_fm, c0, n, 0, 0, 0, h, 0, xk(c0, n), hk)
                    for j in range(5):
                        if j == 2 and bi + 1 < nblk:
                            c1 = (bi + 1) * 512
                            n1 = min(512, T - c1)
                            prenorm(x_fm, c1, n1, 0, 0, 0, hb[1 - sl], 0, xk(c1, n1), f"hb{1 - sl}")
                        ba, bb = bank(), bank()
                        t1, t2 = t1s[j % 2], t2s[j % 2]
                        k1, k2 = f"t1_{j % 2}", f"t2_{j % 2}"

                        def f(e, j=j, ba=ba, bb=bb, h=h, n=n):
                            for kt in range(8):
                                e.matmul(ps[ba][:, 0:n], win[:, kt, j * 128:(j + 1) * 128], h[:, kt, 0:n], start=(kt == 0), stop=(kt == 7))
                            for kt in range(8):
                                i = e.matmul(ps[bb][:, 0:n], win[:, kt, (5 + j) * 128:(6 + j) * 128], h[:, kt, 0:n], start=(kt == 0), stop=(kt == 7))
                            return i
                        S.op("pe", f, reads=["win", hk], writes=[("ps", ba), ("ps", bb)])
                        S.op("dve", lambda e, ba=ba, c0=c0, n=n, t1=t1: e.tensor_tensor(out=t1[:, 0:n], in0=ps[ba][:, 0:n], in1=rcf[:, 0, c0:c0 + n], op=ALU.mult),
                             reads=[("ps", ba), "rcf"], writes=[k1])
                        S.op("dve", lambda e, bb=bb, c0=c0, n=n, t2=t2: e.tensor_tensor(out=t2[:, 0:n], in0=ps[bb][:, 0:n], in1=rcf[:, 1, c0:c0 + n], op=ALU.mult),
                             reads=[("ps", bb), "rcf"], writes=[k2])
                        S.op("pool", lambda e, j=j, sl=sl, n=n, t1=t1, t2=t2: e.tensor_tensor(out=qst[sl][:, j, 0:n], in0=t1[:, 0:n], in1=t2[:, 0:n], op=ALU.add),
                             reads=[k1, k2], writes=["qst0"])
                    S.dma("sp", "qst0", qk_d[:, :, c0:c0 + n].rearrange("j p t -> p j t"), qst[sl][:, :, 0:n], reads=["qst0"], writes=["qk_d"])
                    for tt in range(n // 128):
                        ti = bi * 4 + tt
                        s2 = ti % 2
                        bu, bv = bank(), bank()

                        def f(e, tt=tt, bu=bu, bv=bv, h=h):
                            for kt in range(8):
                                e.matmul(ps[bu][:, :], h[:, kt, tt * 128:(tt + 1) * 128], win[:, kt, 1280:1792], start=(kt == 0), stop=(kt == 7))
                            for kt in range(8):
                                i = e.matmul(ps[bv][:, 0:128], h[:, kt, tt * 128:(tt + 1) * 128], win[:, kt, 1792:1920], start=(kt == 0), stop=(kt == 7))
                            return i
                        S.op("pe", f, reads=["win", hk], writes=[("ps", bu), ("ps", bv)])
                        S.op("act", lambda e, bu=bu, s2=s2: e.activation(out=ust[s2][:], in_=ps[bu][:, :], func=AF.Copy),
                             reads=[("ps", bu)], writes=[f"ust{s2}"])
                        S.op("act", lambda e, bv=bv, s2=s2: e.activation(out=vst[s2][:], in_=ps[bv][:, 0:128], func=AF.Copy),
                             reads=[("ps", bv)], writes=[f"vst{s2}"])
                        S.dma("sp", f"ust{s2}", u_d[ti], ust[s2][:], reads=[f"ust{s2}"], writes=["u_d"])
                        S.dma("sp", f"vst{s2}", v_d[ti], vst[s2][:], reads=[f"vst{s2}"], writes=["v_d"])
                h = hb[0]
                prenorm(g.ctx_fm, 0, CTX, 0, 0, 1, h, 0, ["ctx"], "hb0")
                bk = bank()

                def f(e, bk=bk, h=h):
                    for kt in range(8):
                        i = e.matmul(ps[bk][:, 0:CTX], win[:, kt, 4 * 128:5 * 128], h[:, kt, 0:CTX], start=(kt == 0), stop=(kt == 7))
                    return i
                S.op("pe", f, reads=["win", "hb0"], writes=[("ps", bk)])
                S.op("act", lambda e, bk=bk: e.activation(out=kc_sb[:], in_=ps[bk][:, 0:CTX], func=AF.Copy), reads=[("ps", bk)], writes=["kc"])
                for tt in range(2):
                    bv = bank()

                    def f(e, tt=tt, bv=bv, h=h):
                        for kt in range(8):
                            i = e.matmul(ps[bv][:, 0:128], h[:, kt, tt * 128:(tt + 1) * 128], win[:, kt, 1792:1920], start=(kt == 0), stop=(kt == 7))
                        return i
                    S.op("pe", f, reads=["win", "hb0"], writes=[("ps", bv)])
                    S.op("act", lambda e, tt=tt, bv=bv: e.activation(out=vpc[:, tt, 0, 0:64], in_=ps[bv][:, 0:64], func=AF.Copy),
                         reads=[("ps", bv)], writes=["vpc"])
                    S.op("act", lambda e, tt=tt, bv=bv: e.activation(out=vpc[:, tt, 1, 64:128], in_=ps[bv][:, 64:128], func=AF.Copy),
                         reads=[("ps", bv)], writes=["vpc"])
                S.barrier()

        if debug_stage not in ("p0", "p1"):
            with contextlib.ExitStack() as ph:
                psc = sb("psc", [128, 4], stack=ph)
                snk = sb("snk", [128, 8], stack=ph)
                es = sb("es", [128, 8], stack=ph)
                ublk = [sb(f"ublk{i}", [128, 6, 512], BF16, stack=ph) for i in range(1)] * 2
                kblk = [sb(f"kblk{i}", [128, 6 * 128], BF16, stack=ph) for i in range(2)]
                qblk = [sb(f"qblk{i}", [128, 4, 512], BF16, stack=ph) for i in range(2)]
                vblk = [sb(f"vblk{i}", [128, 6, 2, 128], BF16, stack=ph) for i in range(2)]
                pooled = sb("pooled", [128, 4, 512], BF16, stack=ph)
                mix = [sb(f"mix{i}", [128, 8, 512], BF16, stack=ph) for i in range(1)] * 2
                pt = [sb(f"pt{i}", [128, 512], BF16, stack=ph) for i in range(3)]
                dsum = sb("dsum", [128, 512], stack=ph)
                rden = sb("rden", [128, 512], stack=ph)
                ysb = sb("ysb", [128, 8, 512], stack=ph)
                S.dma("sp", "c0", psc[:], pool_sc, writes=["psc"])
                S.dma("sp", "c0", snk[:], sink_bc, writes=["snk"])
                S.seal("c0", ["psc", "snk"])
                S.op("act", lambda e: e.activation(out=es[:], in_=snk[:], func=AF.Exp), reads=["snk"], writes=["es"])
                convert_ffn_weights(0)
                for i in range(2):
                    S.op("dve", lambda e, i=i: e.memset(vblk[i][:], 0.0), writes=[f"vblk{i}"])
                nqt = 18
                bcfg["mm"] = bcfg["st"] = [0, 1, 2, 3]
                blocks = [(j0, min(4, nqt - j0)) for j0 in range(0, nqt, 4)]
                for bi, (j0, nj) in enumerate(blocks):
                    sl = bi % 2
                    n = nj * 128
                    lo = max(j0 - 1, 0)
                    hi = j0 + nj
                    off = lo - (j0 - 1)
                    nl = hi - lo + 1
                    S.dma("sp", "ublk0", ublk[sl][:, off:off + nl, :], u_d[lo:hi + 1].rearrange("t p c -> p t c"),
                          reads=["u_d"], writes=["ublk0"])
                    S.dma("sp", f"kblk{sl}", kblk[sl][:, off * 128:(off + nl) * 128], qk_d[4, :, lo * 128:(hi + 1) * 128],
                          reads=["qk_d"], writes=[f"kblk{sl}"])
                    S.dma("sp", f"qblk{sl}", qblk[sl][:, :, 0:n], qk_d[0:4, :, j0 * 128:j0 * 128 + n].rearrange("j p t -> p j t"),
                          reads=["qk_d"], writes=[f"qblk{sl}"])
                    S.dma("sp", f"vblk{sl}", vblk[sl][:, off:off + nl, 0, 0:64], v_d[lo:hi + 1, :, 0:64].rearrange("t p c -> p t c"),
                          reads=["v_d"], writes=[f"vblk{sl}"])
                    S.dma("sp", f"vblk{sl}", vblk[sl][:, off:off + nl, 1, 64:128], v_d[lo:hi + 1, :, 64:128].rearrange("t p c -> p t c"),
                          reads=["v_d"], writes=[f"vblk{sl}"])
                    mx = mix[sl]
                    mk = "mix0"
                    pb = [bank() for _ in range(4)]
                    for gi in range(4):
                        b = pb[gi]

                        def f(e, gi=gi, b=b, j0=j0, nj=nj, sl=sl):
                            for jj in range(nj):
                                j = j0 + jj
                                terms = []
                                if j > 0:
                                    terms.append((jj, 0))
                                terms.append((jj + 1, 3 if j == 0 else 1))
                                terms.append((jj + 2, 2))
                                for ti, (slot, kind) in enumerate(terms):
                                    i = e.matmul(ps[b][:, jj * 128:(jj + 1) * 128], ublk[sl][:, slot, gi * 128:(gi + 1) * 128], mpl[:, gi, kind, :],
                                                 start=(ti == 0), stop=(ti == len(terms) - 1))
                            return i
                        S.op("pe", f, reads=["ublk0", "mpl"], writes=[("ps", b)])
                    for gi in range(4):
                        S.op("act", lambda e, gi=gi, b=pb[gi], n=n: e.activation(out=pooled[:, gi, 0:n], in_=ps[b][:, 0:n], func=AF.Copy),
                             reads=[("ps", pb[gi])], writes=[("pooled", gi)])
                    pb2 = [bank() for _ in range(4)]
                    for gi in range(4):
                        S.op("pe", lambda e, gi=gi, b2=pb2[gi], n=n: e.matmul(ps[b2][:, 0:n], wpl[:, gi, :], pooled[:, gi, 0:n], start=True, stop=True),
                             reads=[("pooled", gi), "wpl"], writes=[("ps", pb2[gi])])
                    for gi in range(4):
                        S.op("act", lambda e, gi=gi, b2=pb2[gi], n=n, mx=mx: e.activation(out=mx[:, gi, 0:n], in_=ps[b2][:, 0:n], func=AF.Copy, scale=psc[:, gi:gi + 1]),
                             reads=[("ps", pb2[gi]), "psc"], writes=[mk])
                    for jj in range(nj):
                        j = j0 + jj
                        for r in range(2):
                            pr = slice(r * 64, (r + 1) * 64)
                            chunks = []
                            if j > 0:
                                chunks.append(("l", jj, 0))
                            chunks.append(("l", jj + 1, None))
                            chunks.append(("l", jj + 2, 1))
                            chunks.append(("c", 0, None))
                            chunks.append(("c", 1, None))
                            bO, bD = ((4, 5), (6, 7))[(jj * 2 + r) % 2]
                            pend = None
                            nch = len(chunks)
                            for ci, (kind, slot, mki) in enumerate(chunks):
                                bS = bank()
                                psl = (bi * 100 + jj * 10 + r * 5 + ci) % 3

                                def f(e, kind=kind, slot=slot, mki=mki, bS=bS, jj=jj, sl=sl, pr=pr):
                                    if kind == "l":
                                        kk = kblk[sl][pr, slot * 128:(slot + 1) * 128]
                                    else:
                                        kk = kc_sb[pr, slot * 128:(slot + 1) * 128]
                                    i = e.matmul(ps[bS][:, :].rearrange("p (h t) -> p h t", t=128), kk, qblk[sl][pr, :, jj * 128:(jj + 1) * 128],
                                                 start=True, stop=(mki is None))
                                    if mki is not None:
                                        i = e.matmul(ps[bS][:, :].rearrange("p (h t) -> p h t", t=128), ident_bf[:],
                                                     nmk[:, mki, None, :].to_broadcast([128, 4, 128]), start=False, stop=True)
                                    return i
                                S.op("pe", f, reads=[f"kblk{sl}", f"qblk{sl}", "kc", "identbf", "nmk"], writes=[("ps", bS)])
                                S.op("act", lambda e, bS=bS, psl=psl: e.activation(out=pt[psl][:], in_=ps[bS][:, :], func=AF.Exp, scale=0.125),
                                     reads=[("ps", bS)], writes=[f"pt{psl}"])
                                cur = (kind, slot, psl, ci)
                                if pend is not None:
                                    _emit_pv(S, ps, pend, vblk[sl], vpc, ones_bf, pt, r, bO, bD, nch, f"vblk{sl}")
                                pend = cur
                            _emit_pv(S, ps, pend, vblk[sl], vpc, ones_bf, pt, r, bO, bD, nch, f"vblk{sl}")
                            S.op("dve", lambda e, pr=pr, r=r, bD=bD: e.tensor_tensor(
                                out=dsum[pr, :].rearrange("p (h t) -> p h t", t=128), in0=ps[bD][pr, :].rearrange("p (h t) -> p h t", t=128),
                                in1=es[pr, r * 4:(r + 1) * 4, None].to_broadcast([64, 4, 128]), op=ALU.add),
                                reads=[("ps", bD), "es"], writes=["dsum"])
                            S.op("dve", lambda e, pr=pr: e.reciprocal(out=rden[pr, :], in_=dsum[pr, :]), reads=["dsum"], writes=["rden"])
                            S.op("dve", lambda e, pr=pr, bO=bO, jj=jj, mx=mx: e.tensor_tensor(
                                out=mx[pr, 4:8, jj * 128:(jj + 1) * 128], in0=ps[bO][pr, :].rearrange("p (h t) -> p h t", t=128),
                                in1=rden[pr, :].rearrange("p (h t) -> p h t", t=128), op=ALU.mult),
                                reads=[("ps", bO), "rden"], writes=[mk])
                    out_proj_post(wout, 8, mx, mk, "wout", n, j0 * 128, 0, 2, ysb, "ysb")
                bcfg["mm"] = [0, 1, 2, 3, 4, 5]
                bcfg["st"] = [6, 7]
                S.barrier()
        ph01.close()

        def ffn(lay, ntok):
            with contextlib.ExitStack() as ph:
                hfs = [sb(f"hf{lay}_{i}", [128, 8, 512], BF16, stack=ph) for i in range(2)]
                hkeep = sb(f"hkeep{lay}", [128, 8, 2], BF16, stack=ph)
                wu = [sb(f"wu{lay}_{i}", [128, 8, 256], BF16, stack=ph) for i in range(3)]
                wd = [sb(f"wd{lay}_{i}", [128, NF, 128], BF16, stack=ph) for i in range(2)]
                tt_ = [[sb(f"tc{lay}_{i}_{k}", [128, 512], stack=ph) for k in range(2)] for i in range(2)]
                sg = [sb(f"sg{lay}_{i}", [128, 512], stack=ph) for i in range(2)]
                gbuf = sb(f"gb{lay}", [128, NF, 512], BF16, stack=ph)
                ysb = sb(f"ysbf{lay}", [128, 8, 512], stack=ph)
                nb = (ntok + FFN_BLK - 1) // FFN_BLK
                bsz = (ntok + nb - 1) // nb
                scrB = dict(sq=sb(f"sqB{lay}", [128, 8, 512], BF16, stack=ph), rt=sb(f"rtB{lay}", [128, 512], stack=ph),
                            rstd=sb(f"rstdB{lay}", [128, 512], stack=ph), k="B")

                def pre_block(bi):
                    s0 = bi * bsz
                    n = min(bsz, ntok - s0)
                    hf = hfs[bi % 2]
                    hfk = f"hf{bi % 2}"
                    if s0 == 0:
                        S.op("dve", lambda e: e.memset(hf[:, :, 0:1], 0.0), writes=[hfk])
                    else:
                        S.op("dve", lambda e: e.tensor_copy(hf[:, :, 0:1], hkeep[:, :, 0:1]), reads=["hkeep"], writes=[hfk])
                    prenorm(x_fm, s0, n + 1, lay, 3, 0, hf, 1, xk(s0, n + 1), hfk, scrB)
                    if bi < nb - 1:
                        S.op("dve", lambda e, n=n: e.tensor_copy(hkeep[:, :, 0:1], hf[:, :, n:n + 1]), reads=[hfk], writes=["hkeep"])
                wuc = 0
                wdc = 0
                for bi in range(nb):
                    s0 = bi * bsz
                    n = min(bsz, ntok - s0)
                    if bi == 0:
                        pre_block(0)
                    hf = hfs[bi % 2]
                    hfk = f"hf{bi % 2}"
                    def tail(fp):
                        hs = fp % 2
                        S.op("act", lambda e, hs=hs, n=n: e.activation(out=sg[hs][:, 0:n], in_=tt_[hs][0][:, 0:n], func=AF.Silu),
                             reads=[f"tc{hs}0"], writes=[f"sg{hs}"])
                        S.op("dve", lambda e, hs=hs, fp=fp, n=n: e.tensor_tensor(out=gbuf[:, fp, 0:n], in0=sg[hs][:, 0:n], in1=tt_[hs][1][:, 0:n], op=ALU.mult),
                             reads=[f"sg{hs}", f"tc{hs}1"], writes=[("gbuf", fp)])

                    for f_ in range(NF):
                        sl = wuc % 3
                        wuc += 1
                        S.dma("sp", f"wu{sl}", wu[sl][:], wub_d[lay, f_], reads=[("wub", lay, f_)], writes=[f"wu{sl}"])
                        hs = f_ % 2
                        for k in range(2):
                            b = bank()

                            def f(e, k=k, b=b, sl=sl, n=n, hf=hf):
                                for kt in range(8):
                                    i = e.matmul(ps[b][:, 0:n + 2], wu[sl][:, kt, k * 128:(k + 1) * 128], hf[:, kt, 0:n + 2], start=(kt == 0), stop=(kt == 7))
                                return i
                            S.op("pe", f, reads=[f"wu{sl}", hfk], writes=[("ps", b)])
                            hk = f"hu{hs}{k}"
                            tk = f"tc{hs}{k}"
                            tc = tt_[hs][k]
                            ft = f_ + k * NF
                            S.op("act", lambda e, b=b, tc=tc, ft=ft, n=n: e.activation(out=tc[:, 0:n], in_=ps[b][:, 1:n + 1], func=AF.Identity,
                                                                                     scale=cw[:, lay, ft, 1:2], bias=cb[:, lay, ft:ft + 1]),
                                 reads=[("ps", b), "cw", "cb"], writes=[tk])
                            S.op("dve", lambda e, b=b, tc=tc, ft=ft, n=n: e.scalar_tensor_tensor(
                                out=tc[:, 0:n], in0=ps[b][:, 0:n], scalar=cw[:, lay, ft, 0:1], in1=tc[:, 0:n], op0=ALU.mult, op1=ALU.add),
                                reads=[("ps", b), tk, "cw"], writes=[tk])
                            S.op("dve", lambda e, b=b, tc=tc, ft=ft, n=n: e.scalar_tensor_tensor(
                                out=tc[:, 0:n], in0=ps[b][:, 2:n + 2], scalar=cw[:, lay, ft, 2:3], in1=tc[:, 0:n], op0=ALU.mult, op1=ALU.add),
                                reads=[("ps", b), tk, "cw"], writes=[tk])
                        if f_ > 0:
                            tail(f_ - 1)
                        if f_ == 8 and bi + 1 < nb:
                            pre_block(bi + 1)
                    tail(NF - 1)
                    for d in range(8):
                        sl = wdc % 2
                        wdc += 1
                        S.dma("sp", f"wd{sl}", wd[sl][:], wdb_d[lay, d], reads=[("wdb", lay, d)], writes=[f"wd{sl}"])
                        b = bank()

                        KS = 17

                        def f1(e, b=b, sl=sl, n=n):
                            for k in range(KS):
                                i = e.matmul(ps[b][:, 0:n], wd[sl][:, k, :], gbuf[:, k, 0:n], start=(k == 0), stop=False)
                            return i

                        def f2(e, b=b, sl=sl, n=n):
                            for k in range(KS, NF):
                                i = e.matmul(ps[b][:, 0:n], wd[sl][:, k, :], gbuf[:, k, 0:n], start=False, stop=(k == NF - 1))
                            return i
                        S.op("pe", f1, reads=[f"wd{sl}"] + [("gbuf", k) for k in range(KS)], writes=[("ps", b)])
                        S.op("pe", f2, reads=[f"wd{sl}"] + [("gbuf", k) for k in range(KS, NF)], writes=[("ps", b)])
                        S.op("act", lambda e, d=d, b=b, n=n: e.activation(out=sq[:, d, 0:n], in_=ps[b][:, 0:n], func=AF.Square),
                             reads=[("ps", b)], writes=["sq"])
                        S.op("act", lambda e, d=d, b=b, n=n: e.activation(out=ysb[:, d, 0:n], in_=ps[b][:, 0:n], func=AF.Copy, scale=coef[:, lay, 5, d, 0:1]),
                             reads=[("ps", b), "coef"], writes=["ysbf"])
                    post_update(ysb, n, s0, "ysbf")
                S.barrier()

        if debug_stage not in ("p0", "p1", "p2"):
            ffn(0, N_L0_FFN)

        if debug_stage not in ("p0", "p1", "p2", "p3"):
            with contextlib.ExitStack() as ph:
                wio = sb("wio", [128, 8, 2048], BF16, stack=ph)
                wo1 = sb("wo1", [128, 8, D], BF16, stack=ph)
                lngb = sb("lngb", [128, 2, D], stack=ph)
                wst = sb("wst", [128, 8, 128], BF16, stack=ph)
                bsb = sb("bsb", [128, 8, 128], stack=ph)
                h1 = sb("h1", [128, 8, 512], BF16, stack=ph)
                usb = sb("usb", [128, 8, 512], BF16, stack=ph)
                vg = [sb(f"vg{i}", [128, D], stack=ph) for i in range(2)]
                vn = [sb(f"vn{i}", [128, D], BF16, stack=ph) for i in range(2)]
                stat = [sb(f"stat{i}", [128, 8], stack=ph) for i in range(2)]
                s_sb = [sb(f"s_sb{i}", [128, 512], stack=ph) for i in range(2)]
                gated = usb
                ysb = sb("ysb1", [128, 8, 512], stack=ph)
                for c in range(4):
                    S.dma("pool", "wio", wio[:, :, c * 512:(c + 1) * 512], w_in1[:, :, c * 512:(c + 1) * 512], writes=["wio"])
                for c in range(2):
                    S.dma("pool", "wio", wo1[:, :, c * 512:(c + 1) * 512], w_out1[:, :, c * 512:(c + 1) * 512], writes=["wo1"])
                S.dma("pool", "wio", wst[:], wsT, writes=["wst"])
                S.dma("sp", "c0", lngb[:], ln_gb, writes=["lngb"])
                S.dma("sp", "c0", bsb[:], bs_bc, writes=["bsb"])
                S.seal("c0", ["lngb", "bsb"])
                S.seal("wio", ["wio", "wo1", "wst"])
                convert_ffn_weights(1)
                ntile = N_L1_MIX // 128
                for j0 in range(0, ntile, 4):
                    nj = min(4, ntile - j0)
                    n = nj * 128
                    prenorm(x_fm, j0 * 128, n, 1, 0, 0, h1, 0, xk(j0 * 128, n), "h1")
                    for ft in range(8):
                        b = bank()

                        def f(e, ft=ft, b=b, n=n):
                            for kt in range(8):
                                i = e.matmul(ps[b][:, 0:n], wio[:, kt, ft * 128:(ft + 1) * 128], h1[:, kt, 0:n], start=(kt == 0), stop=(kt == 7))
                            return i
                        S.op("pe", f, reads=["wio", "h1"], writes=[("ps", b)])
                        S.op("act", lambda e, ft=ft, b=b, n=n: e.activation(out=usb[:, ft, 0:n], in_=ps[b][:, 0:n], func=AF.Gelu_apprx_tanh),
                             reads=[("ps", b)], writes=["usb"])
                    def stages(jj):
                        p = jj % 2
                        vg_, vn_, st_, ss_ = vg[p], vn[p], stat[p], s_sb[p]
                        kvg, kvn, kst, kss = f"vg{p}", f"vn{p}", f"stat{p}", f"s_sb{p}"
                        yield lambda: S.op("dve", lambda e: e.memset(st_[:], 0.0), writes=[kst])

                        def vproj():
                            for hh in range(2):
                                b = bank()

                                def f(e, hh=hh, b=b):
                                    for kt in range(8):
                                        i = e.matmul(ps[b][:, :], h1[:, kt, jj * 128:(jj + 1) * 128], wio[:, kt, 1024 + hh * 512:1536 + hh * 512], start=(kt == 0), stop=(kt == 7))
                                    return i
                                S.op("pe", f, reads=["wio", "h1"], writes=[("ps", b)])
                                S.op("act", lambda e, hh=hh, b=b: e.activation(out=vg_[:, hh * 512:(hh + 1) * 512], in_=ps[b][:, :], func=AF.Gelu_apprx_tanh,
                                                                               accum_out=st_[:, hh:hh + 1]),
                                     reads=[("ps", b)], writes=[kvg, kst])
                        yield vproj
                        yield lambda: S.op("dve", lambda e: e.tensor_tensor(out=st_[:, 2:3], in0=st_[:, 0:1], in1=st_[:, 1:2], op=ALU.add), reads=[kst], writes=[kst])
                        yield lambda: S.op("dve", lambda e: e.tensor_scalar(out=st_[:, 3:4], in0=st_[:, 2:3], scalar1=-1.0 / D, scalar2=None, op0=ALU.mult), reads=[kst], writes=[kst])
                        yield lambda: S.op("dve", lambda e: e.tensor_scalar(out=vg_[:], in0=vg_[:], scalar1=st_[:, 3:4], scalar2=None, op0=ALU.add),
                                           reads=[kvg, kst], writes=[kvg])
                        yield lambda: S.op("act", lambda e: e.activation(out=vn_[:], in_=vg_[:], func=AF.Square, accum_out=st_[:, 4:5]), reads=[kvg], writes=[kvn, kst])
                        yield lambda: S.op("act", lambda e: e.activation(out=st_[:, 5:6], in_=st_[:, 4:5], func=AF.Sqrt, scale=1.0 / D, bias=eps_t[:, 0:1]),
                                           reads=[kst, "eps"], writes=[kst])
                        yield lambda: S.op("dve", lambda e: e.reciprocal(out=st_[:, 6:7], in_=st_[:, 5:6]), reads=[kst], writes=[kst])
                        yield lambda: S.op("dve", lambda e: e.scalar_tensor_tensor(out=vg_[:], in0=vg_[:], scalar=st_[:, 6:7], in1=lngb[:, 0, :], op0=ALU.mult, op1=ALU.mult),
                                           reads=[kvg, kst, "lngb"], writes=[kvg])
                        yield lambda: S.op("pool", lambda e: e.tensor_tensor(out=vn_[:], in0=vg_[:], in1=lngb[:, 1, :], op=ALU.add), reads=[kvg, "lngb"], writes=[kvn])
                        for g4 in range(2):
                            def spat(g4=g4):
                                b = bank()

                                def f(e, b=b):
                                    for gg in range(4):
                                        gi = g4 * 4 + gg
                                        i = e.matmul(ps[b][:, gg * 128:(gg + 1) * 128], vn_[:, gi * 128:(gi + 1) * 128], wst[:, gi, :], start=True, stop=True)
                                    return i
                                S.op("pe", f, reads=[kvn, "wst"], writes=[("ps", b)])
                                S.op("dve", lambda e, b=b: e.tensor_tensor(out=ss_[:, :], in0=ps[b][:, :], in1=bsb[:, g4 * 4:(g4 + 1) * 4, :].rearrange("p g t -> p (g t)"), op=ALU.add),
                                     reads=[("ps", b), "bsb"], writes=[kss])
                            yield spat
                            yield lambda g4=g4: S.op("dve", lambda e: e.tensor_tensor(out=gated[:, g4 * 4:(g4 + 1) * 4, jj * 128:(jj + 1) * 128],
                                                                                    in0=ss_[:, :].rearrange("p (g t) -> p g t", t=128),
                                                                                    in1=usb[:, g4 * 4:(g4 + 1) * 4, jj * 128:(jj + 1) * 128], op=ALU.mult),
                                                     reads=[kss, "usb"], writes=["usb"])

                    for ja in range(0, nj, 2):
                        gens = [list(stages(jj)) for jj in range(ja, min(ja + 2, nj))]
                        for si in range(len(gens[0])):
                            for gl in gens:
                                gl[si]()
                    out_proj_post(wo1, 8, gated, "usb", "wo1", n, j0 * 128, 1, 2, ysb, "ysb1")
                S.barrier()

        if debug_stage not in ("p0", "p1", "p2", "p3", "p4"):
            ffn(1, TOWN)

        with contextlib.ExitStack() as ph:
            ost = [sb(f"ost{i}", [128, D], stack=ph) for i in range(2)]
            for i in range(TOWN // 128):
                sl = i % 2
                for hlf in range(2):
                    b = bank()

                    def f(e, hlf=hlf, b=b, i=i):
                        for c in range(4):
                            cc = hlf * 4 + c
                            i2 = e.transpose(ps[b][:, c * 128:(c + 1) * 128], x_fm[:, cc, i * 128:(i + 1) * 128], ident[:])
                        return i2
                    S.op("pe", f, reads=[("x", i), "ident"], writes=[("ps", b)])
                    if hlf == 0:
                        S.op("act", lambda e, b=b, sl=sl: e.activation(out=ost[sl][:, 0:512], in_=ps[b][:, :], func=AF.Copy), reads=[("ps", b)], writes=[f"ost{sl}"])
                    else:
                        S.op("dve", lambda e, b=b, sl=sl: e.tensor_copy(ost[sl][:, 512:1024], ps[b][:, :]), reads=[("ps", b)], writes=[f"ost{sl}"])
                S.dma("sp", f"ost{sl}", out_loc[i], ost[sl][:], reads=[f"ost{sl}"], writes=[f"out{i}"])
            S.barrier()
    return nc


def _emit_pv(S, ps, pend, vb, vpc, ones_bf, pt, r, bO, bD, nch, vkey):
    kind, slot, psl, ci = pend

    def f(e):
        vv = vb[:, slot, r, :] if kind == "l" else vpc[:, slot, r, :]
        e.matmul(ps[bO][:, :], vv, pt[psl][:], start=(ci == 0), stop=(ci == nch - 1))
        return e.matmul(ps[bD][:, :], ones_bf[:], pt[psl][:], start=(ci == 0), stop=(ci == nch - 1))
    S.op("pe", f, reads=[f"pt{psl}", vkey, "vpc", "ones"], writes=[("ps", bO), ("ps", bD)])


def _fm(v):
    v = np.asarray(v, np.float32)
    sh = v.shape
    v = v.reshape(sh[:-1] + (sh[-1] // 128, 128))
    return np.ascontiguousarray(np.moveaxis(v, -1, 0))


def _kt(w):
    K, C = w.shape
    return np.ascontiguousarray(w.reshape(K // 128, 128, C).transpose(1, 0, 2))


def _positions(hf):
    i = np.arange(T)
    return i if hf == 0 else (L - 1 - i)


def _rope_tables(pos):
    inv = (10000.0 ** (-np.arange(16, dtype=np.float32) / 16)).astype(np.float32)
    row = (pos // 64).astype(np.float32)
    col = (pos % 64).astype(np.float32)
    C = np.zeros((128, T), np.float32)
    Sg = np.zeros((128, T), np.float32)
    for rr_ in range(128):
        i = rr_ % 64
        axis, half, f = i // 32, (i % 32) // 16, i % 16
        ang = (row if axis == 0 else col) * inv[f]
        C[rr_] = np.cos(ang)
        Sg[rr_] = np.sin(ang) * (-1.0 if half == 0 else 1.0)
    return C, Sg


def _pool_mats(pos):
    M = np.zeros((128, 4, 4, 128), np.float32)
    for gi, w in enumerate(POOL_W):
        hw = w // 2
        for kind, (jo, dj) in enumerate(((2, -1), (2, 0), (2, 1), (0, 0))):
            ji = jo + dj
            for to in range(128):
                p = pos[jo * 128 + to]
                st_, en = max(p - hw, 0), min(p + hw, L)
                cnt = en - st_
                for ti in range(128):
                    q = pos[ji * 128 + ti]
                    v = 0.0
                    if st_ <= q < en:
                        v += 1.0 / cnt
                    if dj == 0 and ti == to:
                        v -= 1.0
                    M[ti, gi, kind, to] = v
    return M


_PROG = {}


def kernel(x, c, ctx, c_ctx, w_ada, b_ada, g_mix_pre, g_mix_post, g_ffn_pre, g_ffn_post,
           w_in_even, w_pool, pool_scale, attn_sink, w_out_even,
           w_in_odd, sgu_ln_g, sgu_ln_b, sgu_w, sgu_b, w_out_odd,
           w_ffn_up, ffn_conv_w, ffn_conv_b, w_ffn_down, _debug_stage=None):
    f32 = np.float32
    x = np.asarray(x, f32)
    ctx = np.asarray(ctx, f32)
    w_ada_l = np.ascontiguousarray(np.asarray(w_ada, f32).reshape(2, 8, 128, 12, 512).transpose(0, 3, 2, 1, 4))
    b_ada_l = np.ascontiguousarray(np.asarray(b_ada, f32).reshape(2, 48, 128).transpose(2, 0, 1))
    gvec = np.stack([_fm(np.asarray(a, f32)) for a in (g_mix_pre, g_mix_post, g_ffn_pre, g_ffn_post)], axis=0)
    gvec = np.ascontiguousarray(gvec.transpose(1, 2, 0, 3))
    wi = np.asarray(w_in_even, f32)[0]
    perm64 = np.array([(i // 32) * 32 + (1 - (i % 32) // 16) * 16 + (i % 16) for i in range(64)])
    qcols, qpcols = [], []
    for j in range(4):
        for h in (j, 4 + j):
            base = 512 + h * 64
            qcols += list(base + np.arange(64))
            qpcols += list(base + perm64)
    kcols = list(1024 + np.arange(128))
    kpcols = [1024 + hh * 64 + p for hh in range(2) for p in perm64]
    cols = qcols + kcols + qpcols + kpcols + list(range(0, 512)) + list(range(1152, 1280))
    w_in0 = _kt(wi[:, cols])
    w_pool_l = np.ascontiguousarray(np.asarray(w_pool, f32)[0].transpose(1, 0, 2))
    pool_sc = _fm(np.asarray(pool_scale, f32)[0])
    sink_bc = np.ascontiguousarray(np.broadcast_to(np.asarray(attn_sink, f32)[0][None, :], (128, 8)))
    nm = np.zeros((128, 2, 128), f32)
    kk = np.arange(128)[:, None]
    qq = np.arange(128)[None, :]
    nm[:, 0, :] = np.where(kk >= qq, 0.0, -30000.0)
    nm[:, 1, :] = np.where(kk <= qq, 0.0, -30000.0)
    rows = list(range(512))
    for j in range(4):
        rows += list(512 + j * 64 + np.arange(64)) + list(512 + (4 + j) * 64 + np.arange(64))
    w_out0 = _kt(np.asarray(w_out_even, f32)[0][rows, :])
    w_in1 = _kt(np.asarray(w_in_odd, f32)[0])
    ln_gb = np.ascontiguousarray(np.broadcast_to(np.stack([np.asarray(sgu_ln_g, f32)[0], np.asarray(sgu_ln_b, f32)[0]], 0)[None], (128, 2, D)))
    w_out1 = _kt(np.asarray(w_out_odd, f32)[0])
    wu = np.asarray(w_ffn_up, f32).reshape(2, 8, 128, 2, NF, 128)
    w_up_l = np.ascontiguousarray(wu.transpose(0, 4, 2, 1, 3, 5)).reshape(2, NF, 128, 8, 256)
    wd = np.asarray(w_ffn_down, f32).reshape(2, NF, 128, 8, 128)
    w_down_l = np.ascontiguousarray(wd.transpose(0, 3, 2, 1, 4))
    cwf = np.asarray(ffn_conv_w, f32).reshape(2, 3, 44, 128)
    conv_b_l = np.ascontiguousarray(np.asarray(ffn_conv_b, f32).reshape(2, 44, 128).transpose(2, 0, 1))
    sw = np.asarray(sgu_w, f32)[0]
    sbias = np.asarray(sgu_b, f32)[0]
    ident = np.eye(128, dtype=f32)

    per_half = []
    for hf in range(2):
        pos = _positions(hf)
        C, Sg = _rope_tables(pos)
        mp = _pool_mats(pos)
        cwl = cwf if hf == 0 else cwf[:, ::-1]
        conv_w_l = np.ascontiguousarray(cwl.transpose(3, 0, 2, 1))
        swl = sw if hf == 0 else sw[:, ::-1, ::-1]
        wsT = np.ascontiguousarray(swl.transpose(2, 0, 1))
        sbl = sbias if hf == 0 else sbias[:, ::-1]
        bs_bc = np.ascontiguousarray(np.broadcast_to(sbl[None], (128, 8, 128)))
        per_half.append(dict(pos=pos, ropeC=C, ropeS=Sg, mpool=mp, conv_w=conv_w_l, wsT=wsT, bs_bc=bs_bc))

    in_maps = []
    for core in range(8):
        b, hf = core // 2, core % 2
        ph = per_half[hf]
        cv = np.stack([np.asarray(c, f32)[b], np.asarray(c_ctx, f32)], axis=-1)
        in_maps.append(dict(
            x_loc=np.ascontiguousarray(x[b][ph["pos"]].reshape(NT, 128, D)),
            ctx_in=np.ascontiguousarray(ctx[b].reshape(2, 128, D)),
            cvec=np.ascontiguousarray(cv.reshape(8, 128, 2).transpose(1, 0, 2)),
            w_ada=w_ada_l, b_ada=b_ada_l, gvec=gvec, w_in0=w_in0, ropeC=ph["ropeC"], ropeS=ph["ropeS"],
            w_pool=w_pool_l, pool_sc=pool_sc, mpool=ph["mpool"], sink_bc=sink_bc, negmask=nm, ident_in=ident,
            w_out0=w_out0, w_in1=w_in1, ln_gb=ln_gb, wsT=ph["wsT"], bs_bc=ph["bs_bc"], w_out1=w_out1,
            w_up=w_up_l, w_down=w_down_l, conv_w=ph["conv_w"], conv_b=conv_b_l,
        ))
    key = _debug_stage
    if key not in _PROG:
        _PROG[key] = build_program(_debug_stage)
    res = run_bass_kernel_spmd(_PROG[key], in_maps, core_ids=list(range(8)))
    out = np.empty((4, L, D), f32)
    for core in range(8):
        b, hf = core // 2, core % 2
        o = np.asarray(res.results[core]["out_loc"], f32).reshape(TOWN, D)
        out[b, per_half[hf]["pos"][:TOWN]] = o
    return out
```

```python
import contextlib
import numpy as np
import concourse.bass as bass
import concourse.mybir as mybir
from concourse.bass_utils import run_bass_kernel_spmd

F32 = mybir.dt.float32
BF16 = mybir.dt.bfloat16
AF = mybir.ActivationFunctionType
ALU = mybir.AluOpType

D = 1024
L = 4096
NT = 19
T = NT * 128
TOWN = 2048
CTX = 256
DFF = 2816
NF = 22
EPS = 1e-6
POOL_W = (2, 4, 8, 16)
N_L0_MIX = 18 * 128
N_L0_FFN = 17 * 128
N_L1_MIX = 17 * 128
FFN_BLK = 510


class Sched:
    def __init__(self, nc, st):
        self.nc = nc
        self.st = st
        self.eng = dict(pe=nc.tensor, act=nc.scalar, dve=nc.vector, pool=nc.gpsimd, sp=nc.sync)
        self.sem = {k: st.enter_context(nc.semaphore("sem_" + k)) for k in ("pe", "act", "dve", "pool")}
        self.cnt = {k: 0 for k in self.sem}
        self.dsem = {}
        self.dcnt = {}
        self.waited = {k: {} for k in self.eng}
        self.lastw = {}
        self.readers = {}

    def _deps(self, reads, writes):
        deps = {}

        def add(k, v):
            if deps.get(k, 0) < v:
                deps[k] = v

        for r in reads:
            t = self.lastw.get(r)
            if t:
                add(*t)
        for w in writes:
            t = self.lastw.get(w)
            if t:
                add(*t)
            for k, v in self.readers.get(w, {}).items():
                add(k, v)
        return deps

    def _semof(self, k):
        return self.sem[k] if k in self.sem else self.dsem[k]

    def _wait(self, eng, deps):
        e = self.eng[eng]
        for k, v in deps.items():
            if eng == "pe" and k == "pe":
                continue
            if self.waited[eng].get(k, 0) >= v:
                continue
            e.wait_ge(self._semof(k), v)
            self.waited[eng][k] = v

    def _commit(self, tok, reads, writes):
        k, v = tok
        for r in reads:
            d = self.readers.setdefault(r, {})
            if d.get(k, 0) < v:
                d[k] = v
        for w in writes:
            self.lastw[w] = tok
            self.readers[w] = {}

    def op(self, eng, fn, reads=(), writes=()):
        self._wait(eng, self._deps(reads, writes))
        inst = fn(self.eng[eng])
        self.cnt[eng] += 1
        inst.then_inc(self.sem[eng], 1)
        self._commit((eng, self.cnt[eng]), reads, writes)

    def dma(self, q, key, out, in_, reads=(), writes=()):
        if key not in self.dsem:
            self.dsem[key] = self.st.enter_context(self.nc.semaphore("dsem_" + key))
            self.dcnt[key] = 0
        self._wait(q, self._deps(reads, writes))
        self.eng[q].dma_start(out=out, in_=in_).then_inc(self.dsem[key], 16)
        self.dcnt[key] += 16
        self._commit((key, self.dcnt[key]), reads, writes)

    def seal(self, key, resources):
        for r in resources:
            self.lastw[r] = (key, self.dcnt[key])

    def barrier(self, engines=("pe", "act", "dve", "pool", "sp")):
        for e in engines:
            deps = {k: v for k, v in self.cnt.items() if v > 0}
            deps.update({k: v for k, v in self.dcnt.items() if v > 0})
            deps.pop(e, None) if e == "pe" else None
            self._wait(e, deps)


class Ctx:
    pass


def build_program(debug_stage=None):
    nc = bass.Bass("TRN2", target_bir_lowering=False)
    g = Ctx()

    def din(name, shape, dt=F32):
        return nc.dram_tensor(name, list(shape), dt, kind="ExternalInput").ap()

    x_loc = din("x_loc", [NT, 128, D])
    ctx_in = din("ctx_in", [2, 128, D])
    cvec = din("cvec", [128, 8, 2])
    w_ada = din("w_ada", [2, 12, 128, 8, 512])
    b_ada = din("b_ada", [128, 2, 48])
    gvec = din("gvec", [128, 2, 4, 8])
    w_in0 = din("w_in0", [128, 8, 1920])
    ropeC = din("ropeC", [128, T])
    ropeS = din("ropeS", [128, T])
    w_pool = din("w_pool", [128, 4, 128])
    pool_sc = din("pool_sc", [128, 4])
    mpool = din("mpool", [128, 4, 4, 128])
    sink_bc = din("sink_bc", [128, 8])
    negmask = din("negmask", [128, 2, 128])
    ident_in = din("ident_in", [128, 128])
    w_out0 = din("w_out0", [128, 8, D])
    w_in1 = din("w_in1", [128, 8, 2048])
    ln_gb = din("ln_gb", [128, 2, D])
    wsT = din("wsT", [128, 8, 128])
    bs_bc = din("bs_bc", [128, 8, 128])
    w_out1 = din("w_out1", [128, 8, D])
    w_up = din("w_up", [2, NF, 128, 8, 256])
    w_down = din("w_down", [2, 8, 128, NF, 128])
    conv_w = din("conv_w", [128, 2, 44, 3])
    conv_b = din("conv_b", [128, 2, 44])
    out_loc = nc.dram_tensor("out_loc", [TOWN // 128, 128, D], F32, kind="ExternalOutput").ap()
    qk_d = nc.dram_tensor("qk_d", [5, 128, T], BF16).ap()
    u_d = nc.dram_tensor("u_d", [NT, 128, 512], BF16).ap()
    v_d = nc.dram_tensor("v_d", [NT, 128, 128], BF16).ap()
    wub_d = nc.dram_tensor("wub_d", [2, NF, 128, 8, 256], BF16).ap()
    wdb_d = nc.dram_tensor("wdb_d", [2, 8, 128, NF, 128], BF16).ap()

    with contextlib.ExitStack() as st:
        E = st.enter_context
        S = Sched(nc, st)

        def sb(name, shape, dt=F32, stack=None):
            return (stack or st).enter_context(nc.sbuf_tensor(name, list(shape), dt))

        ps = [E(nc.psum_tensor(f"ps{i}", [128, 512], F32)) for i in range(8)]
        rr = {"mm": 0, "st": 0}

        bcfg = {"mm": [0, 1, 2, 3, 4, 5], "st": [6, 7]}

        def bank(pool="mm"):
            lst = bcfg[pool]
            ctr = "mm" if bcfg["st"] is bcfg["mm"] else pool
            i = lst[rr[ctr] % len(lst)]
            rr[ctr] += 1
            return i

        x_fm = sb("x_fm", [128, 8, T])
        ident = sb("ident", [128, 128])
        ident_bf = sb("ident_bf", [128, 128], BF16)
        ones_bf = sb("ones_bf", [128, 128], BF16)
        coef = sb("coef", [128, 2, 6, 8, 2])
        gv = sb("gv", [128, 2, 4, 8])
        cw = sb("cw", [128, 2, 44, 3])
        cb = sb("cb", [128, 2, 44])
        eps_t = sb("eps_t", [128, 1])
        sq = sb("sq", [128, 8, 512], BF16)
        rt = sb("rt", [128, 512])
        rstd = sb("rstd", [128, 512])
        tn = [sb(f"tn{i}", [128, 512]) for i in range(2)]
        kc_sb = sb("kc_sb", [128, CTX], BF16)
        vpc = sb("vpc", [128, 2, 2, 128], BF16)

        cv = sb("cv", [128, 8, 2])
        bada = sb("bada", [128, 2, 48])
        S.dma("sp", "c0", ident[:], ident_in, writes=["ident"])
        S.dma("sp", "c0", gv[:], gvec, writes=["gv"])
        S.dma("sp", "c0", cw[:], conv_w, writes=["cw"])
        S.dma("sp", "c0", cb[:], conv_b, writes=["cb"])
        S.dma("sp", "c0", cv[:], cvec, writes=["cv"])
        S.dma("sp", "c0", bada[:], b_ada, writes=["bada"])
        S.seal("c0", ["ident", "gv", "cw", "cb", "cv", "bada"])
        S.op("dve", lambda e: e.memset(ones_bf[:], 1.0), writes=["ones"])
        S.op("dve", lambda e: e.memset(eps_t[:], EPS), writes=["eps"])
        S.op("dve", lambda e: e.tensor_copy(ident_bf[:], ident[:]), reads=["ident"], writes=["identbf"])

        def xk(c0, n):
            return [("x", t) for t in range(c0 // 128, (c0 + n - 1) // 128 + 1)]

        scrA = dict(sq=sq, rt=rt, rstd=rstd, k="")

        def prenorm(src, c0, n, lay, ka, col, dst, doff, srckey, dstkey, scr=None):
            scr = scr or scrA
            sq, rt, rstd, sk = scr["sq"], scr["rt"], scr["rstd"], scr["k"]
            S.op("act", lambda e: e.activation(out=sq[:, :, 0:n], in_=src[:, :, c0:c0 + n], func=AF.Square),
                 reads=srckey, writes=["sq" + sk])
            b = bank("st")

            def f(e):
                for kt in range(8):
                    i = e.matmul(ps[b][:, 0:n], ones_bf[:], sq[:, kt, 0:n], start=(kt == 0), stop=(kt == 7))
                return i
            S.op("pe", f, reads=["sq" + sk, "ones"], writes=[("ps", b)])
            S.op("act", lambda e: e.activation(out=rt[:, 0:n], in_=ps[b][:, 0:n], func=AF.Sqrt, scale=1.0 / D, bias=eps_t[:, 0:1]),
                 reads=[("ps", b), "eps"], writes=["rt" + sk])
            S.op("dve", lambda e: e.reciprocal(out=rstd[:, 0:n], in_=rt[:, 0:n]), reads=["rt" + sk], writes=["rstd" + sk])
            for kt in range(8):
                ts = kt % 2
                S.op("dve", lambda e, kt=kt, ts=ts: e.tensor_tensor(out=tn[ts][:, 0:n], in0=src[:, kt, c0:c0 + n], in1=rstd[:, 0:n], op=ALU.mult),
                     reads=srckey + ["rstd" + sk], writes=[f"tn{ts}"])
                S.op("act", lambda e, kt=kt, ts=ts: e.activation(out=dst[:, kt, doff:doff + n], in_=tn[ts][:, 0:n], func=AF.Identity,
                                                                 scale=coef[:, lay, ka, kt, col:col + 1], bias=coef[:, lay, ka + 1, kt, col:col + 1]),
                     reads=[f"tn{ts}", "coef"], writes=[dstkey])

        def post_update(ysb, n, c0, ykey):
            b = bank("st")

            def f(e):
                for kt in range(8):
                    i = e.matmul(ps[b][:, 0:n], ones_bf[:], sq[:, kt, 0:n], start=(kt == 0), stop=(kt == 7))
                return i
            S.op("pe", f, reads=["sq", "ones"], writes=[("ps", b)])
            S.op("act", lambda e: e.activation(out=rt[:, 0:n], in_=ps[b][:, 0:n], func=AF.Sqrt, scale=1.0 / D, bias=eps_t[:, 0:1]),
                 reads=[("ps", b), "eps"], writes=["rt"])
            S.op("dve", lambda e: e.reciprocal(out=rstd[:, 0:n], in_=rt[:, 0:n]), reads=["rt"], writes=["rstd"])
            S.op("dve", lambda e: e.tensor_tensor(out=ysb[:, :, 0:n], in0=ysb[:, :, 0:n],
                                                  in1=rstd[:, None, 0:n].to_broadcast([128, 8, n]), op=ALU.mult),
                 reads=[ykey, "rstd"], writes=[ykey])
            S.op("dve", lambda e: e.tensor_tensor(out=x_fm[:, :, c0:c0 + n], in0=x_fm[:, :, c0:c0 + n], in1=ysb[:, :, 0:n], op=ALU.add),
                 reads=[ykey] + xk(c0, n), writes=xk(c0, n))

        def out_proj_post(W, nk, rhs, rhskey, wkey, n, c0, lay, kg, ysb, ykey):
            for d in range(8):
                b = bank()

                def f(e, d=d, b=b):
                    for k in range(nk):
                        i = e.matmul(ps[b][:, 0:n], W[:, k, d * 128:(d + 1) * 128], rhs[:, k, 0:n], start=(k == 0), stop=(k == nk - 1))
                    return i
                S.op("pe", f, reads=[rhskey, wkey], writes=[("ps", b)])
                S.op("act", lambda e, d=d, b=b: e.activation(out=sq[:, d, 0:n], in_=ps[b][:, 0:n], func=AF.Square),
                     reads=[("ps", b)], writes=["sq"])
                S.op("act", lambda e, d=d, b=b: e.activation(out=ysb[:, d, 0:n], in_=ps[b][:, 0:n], func=AF.Copy,
                                                             scale=coef[:, lay, kg, d, 0:1]),
                     reads=[("ps", b), "coef"], writes=[ykey])
            post_update(ysb, n, c0, ykey)

        cvt_items = {lay: [("u", f_) for f_ in range(NF)] + [("d", d) for d in range(8)] for lay in range(2)}
        cvt_pos = {0: 0, 1: 0}

        def convert_ffn_weights(lay, count, after=()):
            key = f"cvt{lay}"
            res = []
            items = cvt_items[lay][cvt_pos[lay]:cvt_pos[lay] + count]
            cvt_pos[lay] += len(items)
            for kind, i in items:
                if kind == "u":
                    S.dma("pool", key, wub_d[lay, i], w_up[lay, i], reads=list(after), writes=[("wub", lay, i)])
                    res.append(("wub", lay, i))
                else:
                    S.dma("pool", key, wdb_d[lay, i], w_down[lay, i], reads=list(after), writes=[("wdb", lay, i)])
                    res.append(("wdb", lay, i))
            if cvt_pos[lay] >= len(cvt_items[lay]):
                S.seal(key, [("wub", lay, f_) for f_ in range(NF)] + [("wdb", lay, d) for d in range(8)])

        ph01 = contextlib.ExitStack()
        g.ctx_fm = sb("ctx_fm", [128, 8, CTX], stack=ph01)
        wout = sb("wout", [128, 8, D], BF16, stack=ph01)
        wpl = sb("wpl", [128, 4, 128], BF16, stack=ph01)
        mpl = sb("mpl", [128, 4, 4, 128], BF16, stack=ph01)
        nmk = sb("nmk", [128, 2, 128], BF16, stack=ph01)
        with contextlib.ExitStack() as ph:
            cs = sb("cs", [128, 8, 2], BF16, stack=ph)
            wa = [sb(f"wa{i}", [128, 8, 512], BF16, stack=ph) for i in range(3)]
            modsb = sb("modsb", [128, 2, 6, 8, 2], stack=ph)
            xs = [sb(f"xs{i}", [128, D], stack=ph) for i in range(2)]
            S.op("act", lambda e: e.activation(out=cs[:], in_=cv[:], func=AF.Silu), reads=["cv"], writes=["cs"])
            for lay in range(2):
                bm = bank("st")
                for ch in range(12):
                    sl = (lay * 12 + ch) % 3
                    S.dma("pool", f"wa{sl}", wa[sl][:], w_ada[lay, ch], writes=[f"wa{sl}"])

                    def f(e, ch=ch, sl=sl, bm=bm):
                        for ft in range(4):
                            o = (ch * 4 + ft) * 2
                            for kt in range(8):
                                i = e.matmul(ps[bm][:, o:o + 2], wa[sl][:, kt, ft * 128:(ft + 1) * 128], cs[:, kt, :], start=(kt == 0), stop=(kt == 7))
                        return i
                    S.op("pe", f, reads=[f"wa{sl}", "cs"], writes=[("ps", bm)])
                S.op("dve", lambda e, lay=lay, bm=bm: e.tensor_tensor(
                    out=modsb[:, lay].rearrange("p j c t -> p (j c) t"),
                    in0=ps[bm][:, 0:96].rearrange("p (a t) -> p a t", t=2),
                    in1=bada[:, lay, :, None].to_broadcast([128, 48, 2]), op=ALU.add),
                    reads=[("ps", bm), "bada"], writes=["modsb"])
                for (ka, jsc, jsh, jgt, gpre, gpost) in ((0, 1, 0, 2, 0, 1), (3, 4, 3, 5, 2, 3)):
                    S.op("dve", lambda e, lay=lay, ka=ka, jsc=jsc: e.tensor_scalar(out=coef[:, lay, ka], in0=modsb[:, lay, jsc], scalar1=1.0, scalar2=None, op0=ALU.add),
                         reads=["modsb"], writes=["coef"])
                    S.op("dve", lambda e, lay=lay, ka=ka, gpre=gpre: e.tensor_tensor(out=coef[:, lay, ka], in0=coef[:, lay, ka],
                                                                                   in1=gv[:, lay, gpre, :, None].to_broadcast([128, 8, 2]), op=ALU.mult),
                         reads=["coef", "gv"], writes=["coef"])
                    S.op("dve", lambda e, lay=lay, ka=ka, jsh=jsh: e.tensor_copy(coef[:, lay, ka + 1], modsb[:, lay, jsh]),
                         reads=["modsb"], writes=["coef"])
                    S.op("dve", lambda e, lay=lay, ka=ka, jgt=jgt, gpost=gpost: e.tensor_tensor(
                        out=coef[:, lay, ka + 2], in0=modsb[:, lay, jgt], in1=gv[:, lay, gpost, :, None].to_broadcast([128, 8, 2]), op=ALU.mult),
                        reads=["modsb", "gv"], writes=["coef"])

            def load_T(src_ap, dst, t0, i, dkey):
                sl = i % 2
                S.dma("sp", f"xs{sl}", xs[sl][:], src_ap, writes=[f"xs{sl}"])
                for hlf in range(2):
                    b = bank()

                    def f(e, hlf=hlf, b=b, sl=sl):
                        for c in range(4):
                            cc = hlf * 4 + c
                            i2 = e.transpose(ps[b][:, c * 128:(c + 1) * 128], xs[sl][:, cc * 128:(cc + 1) * 128], ident[:])
                        return i2
                    S.op("pe", f, reads=[f"xs{sl}", "ident"], writes=[("ps", b)])
                    eng = "act" if hlf == 0 else "dve"
                    if eng == "act":
                        S.op("act", lambda e, hlf=hlf, b=b: e.activation(out=dst[:, hlf * 4:hlf * 4 + 4, t0:t0 + 128],
                                                                         in_=ps[b][:, :].rearrange("p (c t) -> p c t", t=128), func=AF.Copy),
                             reads=[("ps", b)], writes=[dkey])
                    else:
                        S.op("dve", lambda e, hlf=hlf, b=b: e.tensor_copy(dst[:, hlf * 4:hlf * 4 + 4, t0:t0 + 128],
                                                                          ps[b][:, :].rearrange("p (c t) -> p c t", t=128)),
                             reads=[("ps", b)], writes=[dkey])
            for i in range(NT):
                load_T(x_loc[i], x_fm, i * 128, i, ("x", i))
            for i in range(2):
                load_T(ctx_in[i], g.ctx_fm, i * 128, NT + i, "ctx")
            S.barrier()

        if debug_stage == "p0":
            pass
        if debug_stage not in ("p0",):
            with contextlib.ExitStack() as ph:
                win = sb("win", [128, 8, 1920], BF16, stack=ph)
                hb = [sb(f"hb{i}", [128, 8, 512], BF16, stack=ph) for i in range(2)]
                rcf = sb("rcf", [128, 2, T], stack=ph)
                t1s = [sb(f"t1_{i}", [128, 512], stack=ph) for i in range(2)]
                t2s = [sb(f"t2_{i}", [128, 512], stack=ph) for i in range(2)]
                qst = [sb(f"qst{i}", [128, 5, 512], BF16, stack=ph) for i in range(1)] * 2
                ust = [sb(f"ust{i}", [128, 512], BF16, stack=ph) for i in range(2)]
                vst = [sb(f"vst{i}", [128, 128], BF16, stack=ph) for i in range(2)]
                for c in range(4):
                    S.dma("pool", "win", win[:, :, c * 480:(c + 1) * 480], w_in0[:, :, c * 480:(c + 1) * 480], writes=["win"])
                for c in range(2):
                    S.dma("pool", "wout", wout[:, :, c * 512:(c + 1) * 512], w_out0[:, :, c * 512:(c + 1) * 512], writes=["wout"])
                S.dma("pool", "wout", wpl[:], w_pool, writes=["wpl"])
                S.dma("pool", "wout", mpl[:], mpool, writes=["mpl"])
                S.dma("pool", "wout", nmk[:], negmask, writes=["nmk"])
                S.seal("wout", ["wout", "wpl", "mpl", "nmk"])
                S.dma("sp", "rcf", rcf[:, 0, :], ropeC, writes=["rcf"])
                S.dma("sp", "rcf", rcf[:, 1, :], ropeS, writes=["rcf"])
                S.op("dve", lambda e: e.memset(vpc[:], 0.0), writes=["vpc"])
                nblk = (T + 511) // 512
                for bi in range(nblk):
                    c0 = bi * 512
                    n = min(512, T - c0)
                    sl = bi % 2
                    h = hb[sl]
                    hk = f"hb{sl}"
                    if bi == 0:
                        prenorm(x_fm, c0, n, 0, 0, 0, h, 0, xk(c0, n), hk)
                    for j in range(5):
                        if j == 2 and bi + 1 < nblk:
                            c1 = (bi + 1) * 512
                            n1 = min(512, T - c1)
                            prenorm(x_fm, c1, n1, 0, 0, 0, hb[1 - sl], 0, xk(c1, n1), f"hb{1 - sl}")
                        ba, bb = bank(), bank()
                        t1, t2 = t1s[j % 2], t2s[j % 2]
                        k1, k2 = f"t1_{j % 2}", f"t2_{j % 2}"

                        def f(e, j=j, ba=ba, bb=bb, h=h, n=n):
                            for kt in range(8):
                                e.matmul(ps[ba][:, 0:n], win[:, kt, j * 128:(j + 1) * 128], h[:, kt, 0:n], start=(kt == 0), stop=(kt == 7))
                            for kt in range(8):
                                i = e.matmul(ps[bb][:, 0:n], win[:, kt, (5 + j) * 128:(6 + j) * 128], h[:, kt, 0:n], start=(kt == 0), stop=(kt == 7))
                            return i
                        S.op("pe", f, reads=["win", hk], writes=[("ps", ba), ("ps", bb)])
                        S.op("dve", lambda e, ba=ba, c0=c0, n=n, t1=t1: e.tensor_tensor(out=t1[:, 0:n], in0=ps[ba][:, 0:n], in1=rcf[:, 0, c0:c0 + n], op=ALU.mult),
                             reads=[("ps", ba), "rcf"], writes=[k1])
                        S.op("dve", lambda e, bb=bb, c0=c0, n=n, t2=t2: e.tensor_tensor(out=t2[:, 0:n], in0=ps[bb][:, 0:n], in1=rcf[:, 1, c0:c0 + n], op=ALU.mult),
                             reads=[("ps", bb), "rcf"], writes=[k2])
                        S.op("pool", lambda e, j=j, sl=sl, n=n, t1=t1, t2=t2: e.tensor_tensor(out=qst[sl][:, j, 0:n], in0=t1[:, 0:n], in1=t2[:, 0:n], op=ALU.add),
                             reads=[k1, k2], writes=["qst0"])
                    S.dma("sp", "qst0", qk_d[:, :, c0:c0 + n].rearrange("j p t -> p j t"), qst[sl][:, :, 0:n], reads=["qst0"], writes=["qk_d"])
                    for tt in range(n // 128):
                        ti = bi * 4 + tt
                        s2 = ti % 2
                        bu, bv = bank(), bank()

                        def f(e, tt=tt, bu=bu, bv=bv, h=h):
                            for kt in range(8):
                                e.matmul(ps[bu][:, :], h[:, kt, tt * 128:(tt + 1) * 128], win[:, kt, 1280:1792], start=(kt == 0), stop=(kt == 7))
                            for kt in range(8):
                                i = e.matmul(ps[bv][:, 0:128], h[:, kt, tt * 128:(tt + 1) * 128], win[:, kt, 1792:1920], start=(kt == 0), stop=(kt == 7))
                            return i
                        S.op("pe", f, reads=["win", hk], writes=[("ps", bu), ("ps", bv)])
                        S.op("act", lambda e, bu=bu, s2=s2: e.activation(out=ust[s2][:], in_=ps[bu][:, :], func=AF.Copy),
                             reads=[("ps", bu)], writes=[f"ust{s2}"])
                        S.op("act", lambda e, bv=bv, s2=s2: e.activation(out=vst[s2][:], in_=ps[bv][:, 0:128], func=AF.Copy),
                             reads=[("ps", bv)], writes=[f"vst{s2}"])
                        S.dma("sp", f"ust{s2}", u_d[ti], ust[s2][:], reads=[f"ust{s2}"], writes=["u_d"])
                        S.dma("sp", f"vst{s2}", v_d[ti], vst[s2][:], reads=[f"vst{s2}"], writes=["v_d"])
                h = hb[0]
                prenorm(g.ctx_fm, 0, CTX, 0, 0, 1, h, 0, ["ctx"], "hb0")
                bk = bank()

                def f(e, bk=bk, h=h):
                    for kt in range(8):
                        i = e.matmul(ps[bk][:, 0:CTX], win[:, kt, 4 * 128:5 * 128], h[:, kt, 0:CTX], start=(kt == 0), stop=(kt == 7))
                    return i
                S.op("pe", f, reads=["win", "hb0"], writes=[("ps", bk)])
                S.op("act", lambda e, bk=bk: e.activation(out=kc_sb[:], in_=ps[bk][:, 0:CTX], func=AF.Copy), reads=[("ps", bk)], writes=["kc"])
                for tt in range(2):
                    bv = bank()

                    def f(e, tt=tt, bv=bv, h=h):
                        for kt in range(8):
                            i = e.matmul(ps[bv][:, 0:128], h[:, kt, tt * 128:(tt + 1) * 128], win[:, kt, 1792:1920], start=(kt == 0), stop=(kt == 7))
                        return i
                    S.op("pe", f, reads=["win", "hb0"], writes=[("ps", bv)])
                    S.op("act", lambda e, tt=tt, bv=bv: e.activation(out=vpc[:, tt, 0, 0:64], in_=ps[bv][:, 0:64], func=AF.Copy),
                         reads=[("ps", bv)], writes=["vpc"])
                    S.op("act", lambda e, tt=tt, bv=bv: e.activation(out=vpc[:, tt, 1, 64:128], in_=ps[bv][:, 64:128], func=AF.Copy),
                         reads=[("ps", bv)], writes=["vpc"])
                S.barrier()

        if debug_stage not in ("p0", "p1"):
            with contextlib.ExitStack() as ph:
                psc = sb("psc", [128, 4], stack=ph)
                snk = sb("snk", [128, 8], stack=ph)
                es = sb("es", [128, 8], stack=ph)
                ublk = [sb(f"ublk{i}", [128, 6, 512], BF16, stack=ph) for i in range(1)] * 2
                kblk = [sb(f"kblk{i}", [128, 6 * 128], BF16, stack=ph) for i in range(2)]
                qblk = [sb(f"qblk{i}", [128, 4, 512], BF16, stack=ph) for i in range(2)]
                vblk = [sb(f"vblk{i}", [128, 6, 2, 128], BF16, stack=ph) for i in range(2)]
                pooled = sb("pooled", [128, 4, 512], BF16, stack=ph)
                mix = [sb(f"mix{i}", [128, 8, 512], BF16, stack=ph) for i in range(1)] * 2
                pt = [sb(f"pt{i}", [128, 512], BF16, stack=ph) for i in range(3)]
                dsum = sb("dsum", [128, 512], stack=ph)
                rden = sb("rden", [128, 512], stack=ph)
                ysb = sb("ysb", [128, 8, 512], stack=ph)
                S.dma("sp", "c0", psc[:], pool_sc, writes=["psc"])
                S.dma("sp", "c0", snk[:], sink_bc, writes=["snk"])
                S.seal("c0", ["psc", "snk"])
                S.op("act", lambda e: e.activation(out=es[:], in_=snk[:], func=AF.Exp), reads=["snk"], writes=["es"])
                for i in range(2):
                    S.op("dve", lambda e, i=i: e.memset(vblk[i][:], 0.0), writes=[f"vblk{i}"])
                nqt = 18
                bcfg["mm"] = bcfg["st"] = [0, 1, 2, 3]
                blocks = [(j0, min(4, nqt - j0)) for j0 in range(0, nqt, 4)]
                for bi, (j0, nj) in enumerate(blocks):
                    sl = bi % 2
                    n = nj * 128
                    lo = max(j0 - 1, 0)
                    hi = j0 + nj
                    off = lo - (j0 - 1)
                    nl = hi - lo + 1
                    S.dma("sp", "ublk0", ublk[sl][:, off:off + nl, :], u_d[lo:hi + 1].rearrange("t p c -> p t c"),
                          reads=["u_d"], writes=["ublk0"])
                    S.dma("sp", f"kblk{sl}", kblk[sl][:, off * 128:(off + nl) * 128], qk_d[4, :, lo * 128:(hi + 1) * 128],
                          reads=["qk_d"], writes=[f"kblk{sl}"])
                    S.dma("sp", f"qblk{sl}", qblk[sl][:, :, 0:n], qk_d[0:4, :, j0 * 128:j0 * 128 + n].rearrange("j p t -> p j t"),
                          reads=["qk_d"], writes=[f"qblk{sl}"])
                    S.dma("sp", f"vblk{sl}", vblk[sl][:, off:off + nl, 0, 0:64], v_d[lo:hi + 1, :, 0:64].rearrange("t p c -> p t c"),
                          reads=["v_d"], writes=[f"vblk{sl}"])
                    S.dma("sp", f"vblk{sl}", vblk[sl][:, off:off + nl, 1, 64:128], v_d[lo:hi + 1, :, 64:128].rearrange("t p c -> p t c"),
                          reads=["v_d"], writes=[f"vblk{sl}"])
                    mx = mix[sl]
                    mk = "mix0"
                    pb = [bank() for _ in range(4)]
                    for gi in range(4):
                        b = pb[gi]

                        def f(e, gi=gi, b=b, j0=j0, nj=nj, sl=sl):
                            for jj in range(nj):
                                j = j0 + jj
                                terms = []
                                if j > 0:
                                    terms.append((jj, 0))
                                terms.append((jj + 1, 3 if j == 0 else 1))
                                terms.append((jj + 2, 2))
                                for ti, (slot, kind) in enumerate(terms):
                                    i = e.matmul(ps[b][:, jj * 128:(jj + 1) * 128], ublk[sl][:, slot, gi * 128:(gi + 1) * 128], mpl[:, gi, kind, :],
                                                 start=(ti == 0), stop=(ti == len(terms) - 1))
                            return i
                        S.op("pe", f, reads=["ublk0", "mpl"], writes=[("ps", b)])
                    for gi in range(4):
                        S.op("act", lambda e, gi=gi, b=pb[gi], n=n: e.activation(out=pooled[:, gi, 0:n], in_=ps[b][:, 0:n], func=AF.Copy),
                             reads=[("ps", pb[gi])], writes=[("pooled", gi)])
                    pb2 = [bank() for _ in range(4)]
                    for gi in range(4):
                        S.op("pe", lambda e, gi=gi, b2=pb2[gi], n=n: e.matmul(ps[b2][:, 0:n], wpl[:, gi, :], pooled[:, gi, 0:n], start=True, stop=True),
                             reads=[("pooled", gi), "wpl"], writes=[("ps", pb2[gi])])
                    for gi in range(4):
                        S.op("act", lambda e, gi=gi, b2=pb2[gi], n=n, mx=mx: e.activation(out=mx[:, gi, 0:n], in_=ps[b2][:, 0:n], func=AF.Copy, scale=psc[:, gi:gi + 1]),
                             reads=[("ps", pb2[gi]), "psc"], writes=[mk])
                    for jj in range(nj):
                        j = j0 + jj
                        for r in range(2):
                            pr = slice(r * 64, (r + 1) * 64)
                            chunks = []
                            if j > 0:
                                chunks.append(("l", jj, 0))
                            chunks.append(("l", jj + 1, None))
                            chunks.append(("l", jj + 2, 1))
                            chunks.append(("c", 0, None))
                            chunks.append(("c", 1, None))
                            bO, bD = ((4, 5), (6, 7))[(jj * 2 + r) % 2]
                            pend = None
                            nch = len(chunks)
                            for ci, (kind, slot, mki) in enumerate(chunks):
                                bS = bank()
                                psl = (bi * 100 + jj * 10 + r * 5 + ci) % 3

                                def f(e, kind=kind, slot=slot, mki=mki, bS=bS, jj=jj, sl=sl, pr=pr):
                                    if kind == "l":
                                        kk = kblk[sl][pr, slot * 128:(slot + 1) * 128]
                                    else:
                                        kk = kc_sb[pr, slot * 128:(slot + 1) * 128]
                                    i = e.matmul(ps[bS][:, :].rearrange("p (h t) -> p h t", t=128), kk, qblk[sl][pr, :, jj * 128:(jj + 1) * 128],
                                                 start=True, stop=(mki is None))
                                    if mki is not None:
                                        i = e.matmul(ps[bS][:, :].rearrange("p (h t) -> p h t", t=128), ident_bf[:],
                                                     nmk[:, mki, None, :].to_broadcast([128, 4, 128]), start=False, stop=True)
                                    return i
                                S.op("pe", f, reads=[f"kblk{sl}", f"qblk{sl}", "kc", "identbf", "nmk"], writes=[("ps", bS)])
                                S.op("act", lambda e, bS=bS, psl=psl: e.activation(out=pt[psl][:], in_=ps[bS][:, :], func=AF.Exp, scale=0.125),
                                     reads=[("ps", bS)], writes=[f"pt{psl}"])
                                cur = (kind, slot, psl, ci)
                                if pend is not None:
                                    _emit_pv(S, ps, pend, vblk[sl], vpc, ones_bf, pt, r, bO, bD, nch, f"vblk{sl}")
                                pend = cur
                            _emit_pv(S, ps, pend, vblk[sl], vpc, ones_bf, pt, r, bO, bD, nch, f"vblk{sl}")
                            S.op("dve", lambda e, pr=pr, r=r, bD=bD: e.tensor_tensor(
                                out=dsum[pr, :].rearrange("p (h t) -> p h t", t=128), in0=ps[bD][pr, :].rearrange("p (h t) -> p h t", t=128),
                                in1=es[pr, r * 4:(r + 1) * 4, None].to_broadcast([64, 4, 128]), op=ALU.add),
                                reads=[("ps", bD), "es"], writes=["dsum"])
                            S.op("dve", lambda e, pr=pr: e.reciprocal(out=rden[pr, :], in_=dsum[pr, :]), reads=["dsum"], writes=["rden"])
                            S.op("dve", lambda e, pr=pr, bO=bO, jj=jj, mx=mx: e.tensor_tensor(
                                out=mx[pr, 4:8, jj * 128:(jj + 1) * 128], in0=ps[bO][pr, :].rearrange("p (h t) -> p h t", t=128),
                                in1=rden[pr, :].rearrange("p (h t) -> p h t", t=128), op=ALU.mult),
                                reads=[("ps", bO), "rden"], writes=[mk])
                    convert_ffn_weights(0, 7, after=[mk])
                    out_proj_post(wout, 8, mx, mk, "wout", n, j0 * 128, 0, 2, ysb, "ysb")
                bcfg["mm"] = [0, 1, 2, 3, 4, 5]
                bcfg["st"] = [6, 7]
                S.barrier()
        ph01.close()

        def ffn(lay, ntok):
            with contextlib.ExitStack() as ph:
                hfs = [sb(f"hf{lay}_{i}", [128, 8, 512], BF16, stack=ph) for i in range(2)]
                hkeep = sb(f"hkeep{lay}", [128, 8, 2], BF16, stack=ph)
                wu = [sb(f"wu{lay}_{i}", [128, 8, 256], BF16, stack=ph) for i in range(3)]
                wd = [sb(f"wd{lay}_{i}", [128, NF, 128], BF16, stack=ph) for i in range(2)]
                tt_ = [[sb(f"tc{lay}_{i}_{k}", [128, 512], stack=ph) for k in range(2)] for i in range(2)]
                sg = [sb(f"sg{lay}_{i}", [128, 512], stack=ph) for i in range(2)]
                gbuf = sb(f"gb{lay}", [128, NF, 512], BF16, stack=ph)
                ysb = sb(f"ysbf{lay}", [128, 8, 512], stack=ph)
                nb = (ntok + FFN_BLK - 1) // FFN_BLK
                bsz = (ntok + nb - 1) // nb
                scrB = dict(sq=sb(f"sqB{lay}", [128, 8, 512], BF16, stack=ph), rt=sb(f"rtB{lay}", [128, 512], stack=ph),
                            rstd=sb(f"rstdB{lay}", [128, 512], stack=ph), k="B")

                def pre_block(bi):
                    s0 = bi * bsz
                    n = min(bsz, ntok - s0)
                    hf = hfs[bi % 2]
                    hfk = f"hf{bi % 2}"
                    if s0 == 0:
                        S.op("dve", lambda e: e.memset(hf[:, :, 0:1], 0.0), writes=[hfk])
                    else:
                        S.op("dve", lambda e: e.tensor_copy(hf[:, :, 0:1], hkeep[:, :, 0:1]), reads=["hkeep"], writes=[hfk])
                    prenorm(x_fm, s0, n + 1, lay, 3, 0, hf, 1, xk(s0, n + 1), hfk, scrB)
                    if bi < nb - 1:
                        S.op("dve", lambda e, n=n: e.tensor_copy(hkeep[:, :, 0:1], hf[:, :, n:n + 1]), reads=[hfk], writes=["hkeep"])
                wuc = 0
                wdc = 0
                for bi in range(nb):
                    s0 = bi * bsz
                    n = min(bsz, ntok - s0)
                    if bi == 0:
                        pre_block(0)
                    hf = hfs[bi % 2]
                    hfk = f"hf{bi % 2}"
                    def tail(fp):
                        hs = fp % 2
                        S.op("act", lambda e, hs=hs, n=n: e.activation(out=sg[hs][:, 0:n], in_=tt_[hs][0][:, 0:n], func=AF.Silu),
                             reads=[f"tc{hs}0"], writes=[f"sg{hs}"])
                        S.op("dve", lambda e, hs=hs, fp=fp, n=n: e.tensor_tensor(out=gbuf[:, fp, 0:n], in0=sg[hs][:, 0:n], in1=tt_[hs][1][:, 0:n], op=ALU.mult),
                             reads=[f"sg{hs}", f"tc{hs}1"], writes=[("gbuf", fp)])

                    for f_ in range(NF):
                        sl = wuc % 3
                        wuc += 1
                        S.dma("sp", f"wu{sl}", wu[sl][:], wub_d[lay, f_], reads=[("wub", lay, f_)], writes=[f"wu{sl}"])
                        hs = f_ % 2
                        for k in range(2):
                            b = bank()

                            def f(e, k=k, b=b, sl=sl, n=n, hf=hf):
                                for kt in range(8):
                                    i = e.matmul(ps[b][:, 0:n + 2], wu[sl][:, kt, k * 128:(k + 1) * 128], hf[:, kt, 0:n + 2], start=(kt == 0), stop=(kt == 7))
                                return i
                            S.op("pe", f, reads=[f"wu{sl}", hfk], writes=[("ps", b)])
                            hk = f"hu{hs}{k}"
                            tk = f"tc{hs}{k}"
                            tc = tt_[hs][k]
                            ft = f_ + k * NF
                            S.op("act", lambda e, b=b, tc=tc, ft=ft, n=n: e.activation(out=tc[:, 0:n], in_=ps[b][:, 1:n + 1], func=AF.Identity,
                                                                                     scale=cw[:, lay, ft, 1:2], bias=cb[:, lay, ft:ft + 1]),
                                 reads=[("ps", b), "cw", "cb"], writes=[tk])
                            S.op("dve", lambda e, b=b, tc=tc, ft=ft, n=n: e.scalar_tensor_tensor(
                                out=tc[:, 0:n], in0=ps[b][:, 0:n], scalar=cw[:, lay, ft, 0:1], in1=tc[:, 0:n], op0=ALU.mult, op1=ALU.add),
                                reads=[("ps", b), tk, "cw"], writes=[tk])
                            S.op("dve", lambda e, b=b, tc=tc, ft=ft, n=n: e.scalar_tensor_tensor(
                                out=tc[:, 0:n], in0=ps[b][:, 2:n + 2], scalar=cw[:, lay, ft, 2:3], in1=tc[:, 0:n], op0=ALU.mult, op1=ALU.add),
                                reads=[("ps", b), tk, "cw"], writes=[tk])
                        if f_ > 0:
                            tail(f_ - 1)
                        if f_ == 8 and bi + 1 < nb:
                            pre_block(bi + 1)
                    tail(NF - 1)
                    for d in range(8):
                        sl = wdc % 2
                        wdc += 1
                        S.dma("sp", f"wd{sl}", wd[sl][:], wdb_d[lay, d], reads=[("wdb", lay, d)], writes=[f"wd{sl}"])
                        b = bank()

                        KS = 17

                        def f1(e, b=b, sl=sl, n=n):
                            for k in range(KS):
                                i = e.matmul(ps[b][:, 0:n], wd[sl][:, k, :], gbuf[:, k, 0:n], start=(k == 0), stop=False)
                            return i

                        def f2(e, b=b, sl=sl, n=n):
                            for k in range(KS, NF):
                                i = e.matmul(ps[b][:, 0:n], wd[sl][:, k, :], gbuf[:, k, 0:n], start=False, stop=(k == NF - 1))
                            return i
                        S.op("pe", f1, reads=[f"wd{sl}"] + [("gbuf", k) for k in range(KS)], writes=[("ps", b)])
                        S.op("pe", f2, reads=[f"wd{sl}"] + [("gbuf", k) for k in range(KS, NF)], writes=[("ps", b)])
                        S.op("act", lambda e, d=d, b=b, n=n: e.activation(out=sq[:, d, 0:n], in_=ps[b][:, 0:n], func=AF.Square),
                             reads=[("ps", b)], writes=["sq"])
                        S.op("act", lambda e, d=d, b=b, n=n: e.activation(out=ysb[:, d, 0:n], in_=ps[b][:, 0:n], func=AF.Copy, scale=coef[:, lay, 5, d, 0:1]),
                             reads=[("ps", b), "coef"], writes=["ysbf"])
                    post_update(ysb, n, s0, "ysbf")
                S.barrier()

        if debug_stage not in ("p0", "p1", "p2"):
            ffn(0, N_L0_FFN)

        if debug_stage not in ("p0", "p1", "p2", "p3"):
            with contextlib.ExitStack() as ph:
                wio = sb("wio", [128, 8, 2048], BF16, stack=ph)
                wo1 = sb("wo1", [128, 8, D], BF16, stack=ph)
                lngb = sb("lngb", [128, 2, D], stack=ph)
                wst = sb("wst", [128, 8, 128], BF16, stack=ph)
                bsb = sb("bsb", [128, 8, 128], stack=ph)
                h1 = sb("h1", [128, 8, 512], BF16, stack=ph)
                usb = sb("usb", [128, 8, 512], BF16, stack=ph)
                vg = [sb(f"vg{i}", [128, D], stack=ph) for i in range(2)]
                vn = [sb(f"vn{i}", [128, D], BF16, stack=ph) for i in range(2)]
                stat = [sb(f"stat{i}", [128, 8], stack=ph) for i in range(2)]
                s_sb = [sb(f"s_sb{i}", [128, 512], stack=ph) for i in range(2)]
                gated = usb
                ysb = sb("ysb1", [128, 8, 512], stack=ph)
                for c in range(4):
                    S.dma("pool", "wio", wio[:, :, c * 512:(c + 1) * 512], w_in1[:, :, c * 512:(c + 1) * 512], writes=["wio"])
                for c in range(2):
                    S.dma("pool", "wio", wo1[:, :, c * 512:(c + 1) * 512], w_out1[:, :, c * 512:(c + 1) * 512], writes=["wo1"])
                S.dma("pool", "wio", wst[:], wsT, writes=["wst"])
                S.dma("sp", "c0", lngb[:], ln_gb, writes=["lngb"])
                S.dma("sp", "c0", bsb[:], bs_bc, writes=["bsb"])
                S.seal("c0", ["lngb", "bsb"])
                S.seal("wio", ["wio", "wo1", "wst"])
                ntile = N_L1_MIX // 128
                for j0 in range(0, ntile, 4):
                    nj = min(4, ntile - j0)
                    n = nj * 128
                    prenorm(x_fm, j0 * 128, n, 1, 0, 0, h1, 0, xk(j0 * 128, n), "h1")
                    for ft in range(8):
                        b = bank()

                        def f(e, ft=ft, b=b, n=n):
                            for kt in range(8):
                                i = e.matmul(ps[b][:, 0:n], wio[:, kt, ft * 128:(ft + 1) * 128], h1[:, kt, 0:n], start=(kt == 0), stop=(kt == 7))
                            return i
                        S.op("pe", f, reads=["wio", "h1"], writes=[("ps", b)])
                        S.op("act", lambda e, ft=ft, b=b, n=n: e.activation(out=usb[:, ft, 0:n], in_=ps[b][:, 0:n], func=AF.Gelu_apprx_tanh),
                             reads=[("ps", b)], writes=["usb"])
                    def stages(jj):
                        p = jj % 2
                        vg_, vn_, st_, ss_ = vg[p], vn[p], stat[p], s_sb[p]
                        kvg, kvn, kst, kss = f"vg{p}", f"vn{p}", f"stat{p}", f"s_sb{p}"
                        yield lambda: S.op("dve", lambda e: e.memset(st_[:], 0.0), writes=[kst])

                        def vproj():
                            for hh in range(2):
                                b = bank()

                                def f(e, hh=hh, b=b):
                                    for kt in range(8):
                                        i = e.matmul(ps[b][:, :], h1[:, kt, jj * 128:(jj + 1) * 128], wio[:, kt, 1024 + hh * 512:1536 + hh * 512], start=(kt == 0), stop=(kt == 7))
                                    return i
                                S.op("pe", f, reads=["wio", "h1"], writes=[("ps", b)])
                                S.op("act", lambda e, hh=hh, b=b: e.activation(out=vg_[:, hh * 512:(hh + 1) * 512], in_=ps[b][:, :], func=AF.Gelu_apprx_tanh,
                                                                               accum_out=st_[:, hh:hh + 1]),
                                     reads=[("ps", b)], writes=[kvg, kst])
                        yield vproj
                        yield lambda: S.op("dve", lambda e: e.tensor_tensor(out=st_[:, 2:3], in0=st_[:, 0:1], in1=st_[:, 1:2], op=ALU.add), reads=[kst], writes=[kst])
                        yield lambda: S.op("dve", lambda e: e.tensor_scalar(out=st_[:, 3:4], in0=st_[:, 2:3], scalar1=-1.0 / D, scalar2=None, op0=ALU.mult), reads=[kst], writes=[kst])
                        yield lambda: S.op("dve", lambda e: e.tensor_scalar(out=vg_[:], in0=vg_[:], scalar1=st_[:, 3:4], scalar2=None, op0=ALU.add),
                                           reads=[kvg, kst], writes=[kvg])
                        yield lambda: S.op("act", lambda e: e.activation(out=vn_[:], in_=vg_[:], func=AF.Square, accum_out=st_[:, 4:5]), reads=[kvg], writes=[kvn, kst])
                        yield lambda: S.op("act", lambda e: e.activation(out=st_[:, 5:6], in_=st_[:, 4:5], func=AF.Sqrt, scale=1.0 / D, bias=eps_t[:, 0:1]),
                                           reads=[kst, "eps"], writes=[kst])
                        yield lambda: S.op("dve", lambda e: e.reciprocal(out=st_[:, 6:7], in_=st_[:, 5:6]), reads=[kst], writes=[kst])
                        yield lambda: S.op("dve", lambda e: e.scalar_tensor_tensor(out=vg_[:], in0=vg_[:], scalar=st_[:, 6:7], in1=lngb[:, 0, :], op0=ALU.mult, op1=ALU.mult),
                                           reads=[kvg, kst, "lngb"], writes=[kvg])
                        yield lambda: S.op("pool", lambda e: e.tensor_tensor(out=vn_[:], in0=vg_[:], in1=lngb[:, 1, :], op=ALU.add), reads=[kvg, "lngb"], writes=[kvn])
                        for g4 in range(2):
                            def spat(g4=g4):
                                b = bank()

                                def f(e, b=b):
                                    for gg in range(4):
                                        gi = g4 * 4 + gg
                                        i = e.matmul(ps[b][:, gg * 128:(gg + 1) * 128], vn_[:, gi * 128:(gi + 1) * 128], wst[:, gi, :], start=True, stop=True)
                                    return i
                                S.op("pe", f, reads=[kvn, "wst"], writes=[("ps", b)])
                                S.op("dve", lambda e, b=b: e.tensor_tensor(out=ss_[:, :], in0=ps[b][:, :], in1=bsb[:, g4 * 4:(g4 + 1) * 4, :].rearrange("p g t -> p (g t)"), op=ALU.add),
                                     reads=[("ps", b), "bsb"], writes=[kss])
                            yield spat
                            yield lambda g4=g4: S.op("dve", lambda e: e.tensor_tensor(out=gated[:, g4 * 4:(g4 + 1) * 4, jj * 128:(jj + 1) * 128],
                                                                                    in0=ss_[:, :].rearrange("p (g t) -> p g t", t=128),
                                                                                    in1=usb[:, g4 * 4:(g4 + 1) * 4, jj * 128:(jj + 1) * 128], op=ALU.mult),
                                                     reads=[kss, "usb"], writes=["usb"])

                    for ja in range(0, nj, 2):
                        gens = [list(stages(jj)) for jj in range(ja, min(ja + 2, nj))]
                        for si in range(len(gens[0])):
                            for gl in gens:
                                gl[si]()
                    convert_ffn_weights(1, 7, after=["usb"])
                    out_proj_post(wo1, 8, gated, "usb", "wo1", n, j0 * 128, 1, 2, ysb, "ysb1")
                S.barrier()

        if debug_stage not in ("p0", "p1", "p2", "p3", "p4"):
            ffn(1, TOWN)

        with contextlib.ExitStack() as ph:
            ost = [sb(f"ost{i}", [128, D], stack=ph) for i in range(2)]
            for i in range(TOWN // 128):
                sl = i % 2
                for hlf in range(2):
                    b = bank()

                    def f(e, hlf=hlf, b=b, i=i):
                        for c in range(4):
                            cc = hlf * 4 + c
                            i2 = e.transpose(ps[b][:, c * 128:(c + 1) * 128], x_fm[:, cc, i * 128:(i + 1) * 128], ident[:])
                        return i2
                    S.op("pe", f, reads=[("x", i), "ident"], writes=[("ps", b)])
                    if hlf == 0:
                        S.op("act", lambda e, b=b, sl=sl: e.activation(out=ost[sl][:, 0:512], in_=ps[b][:, :], func=AF.Copy), reads=[("ps", b)], writes=[f"ost{sl}"])
                    else:
                        S.op("dve", lambda e, b=b, sl=sl: e.tensor_copy(ost[sl][:, 512:1024], ps[b][:, :]), reads=[("ps", b)], writes=[f"ost{sl}"])
                S.dma("sp", f"ost{sl}", out_loc[i], ost[sl][:], reads=[f"ost{sl}"], writes=[f"out{i}"])
            S.barrier()
    return nc


def _emit_pv(S, ps, pend, vb, vpc, ones_bf, pt, r, bO, bD, nch, vkey):
    kind, slot, psl, ci = pend

    def f(e):
        vv = vb[:, slot, r, :] if kind == "l" else vpc[:, slot, r, :]
        e.matmul(ps[bO][:, :], vv, pt[psl][:], start=(ci == 0), stop=(ci == nch - 1))
        return e.matmul(ps[bD][:, :], ones_bf[:], pt[psl][:], start=(ci == 0), stop=(ci == nch - 1))
    S.op("pe", f, reads=[f"pt{psl}", vkey, "vpc", "ones"], writes=[("ps", bO), ("ps", bD)])


def _fm(v):
    v = np.asarray(v, np.float32)
    sh = v.shape
    v = v.reshape(sh[:-1] + (sh[-1] // 128, 128))
    return np.ascontiguousarray(np.moveaxis(v, -1, 0))


def _kt(w):
    K, C = w.shape
    return np.ascontiguousarray(w.reshape(K // 128, 128, C).transpose(1, 0, 2))


def _positions(hf):
    i = np.arange(T)
    return i if hf == 0 else (L - 1 - i)


def _rope_tables(pos):
    inv = (10000.0 ** (-np.arange(16, dtype=np.float32) / 16)).astype(np.float32)
    row = (pos // 64).astype(np.float32)
    col = (pos % 64).astype(np.float32)
    C = np.zeros((128, T), np.float32)
    Sg = np.zeros((128, T), np.float32)
    for rr_ in range(128):
        i = rr_ % 64
        axis, half, f = i // 32, (i % 32) // 16, i % 16
        ang = (row if axis == 0 else col) * inv[f]
        C[rr_] = np.cos(ang)
        Sg[rr_] = np.sin(ang) * (-1.0 if half == 0 else 1.0)
    return C, Sg


def _pool_mats(pos):
    M = np.zeros((128, 4, 4, 128), np.float32)
    for gi, w in enumerate(POOL_W):
        hw = w // 2
        for kind, (jo, dj) in enumerate(((2, -1), (2, 0), (2, 1), (0, 0))):
            ji = jo + dj
            for to in range(128):
                p = pos[jo * 128 + to]
                st_, en = max(p - hw, 0), min(p + hw, L)
                cnt = en - st_
                for ti in range(128):
                    q = pos[ji * 128 + ti]
                    v = 0.0
                    if st_ <= q < en:
                        v += 1.0 / cnt
                    if dj == 0 and ti == to:
                        v -= 1.0
                    M[ti, gi, kind, to] = v
    return M


_PROG = {}


def kernel(x, c, ctx, c_ctx, w_ada, b_ada, g_mix_pre, g_mix_post, g_ffn_pre, g_ffn_post,
           w_in_even, w_pool, pool_scale, attn_sink, w_out_even,
           w_in_odd, sgu_ln_g, sgu_ln_b, sgu_w, sgu_b, w_out_odd,
           w_ffn_up, ffn_conv_w, ffn_conv_b, w_ffn_down, _debug_stage=None):
    f32 = np.float32
    x = np.asarray(x, f32)
    ctx = np.asarray(ctx, f32)
    w_ada_l = np.ascontiguousarray(np.asarray(w_ada, f32).reshape(2, 8, 128, 12, 512).transpose(0, 3, 2, 1, 4))
    b_ada_l = np.ascontiguousarray(np.asarray(b_ada, f32).reshape(2, 48, 128).transpose(2, 0, 1))
    gvec = np.stack([_fm(np.asarray(a, f32)) for a in (g_mix_pre, g_mix_post, g_ffn_pre, g_ffn_post)], axis=0)
    gvec = np.ascontiguousarray(gvec.transpose(1, 2, 0, 3))
    wi = np.asarray(w_in_even, f32)[0]
    perm64 = np.array([(i // 32) * 32 + (1 - (i % 32) // 16) * 16 + (i % 16) for i in range(64)])
    qcols, qpcols = [], []
    for j in range(4):
        for h in (j, 4 + j):
            base = 512 + h * 64
            qcols += list(base + np.arange(64))
            qpcols += list(base + perm64)
    kcols = list(1024 + np.arange(128))
    kpcols = [1024 + hh * 64 + p for hh in range(2) for p in perm64]
    cols = qcols + kcols + qpcols + kpcols + list(range(0, 512)) + list(range(1152, 1280))
    w_in0 = _kt(wi[:, cols])
    w_pool_l = np.ascontiguousarray(np.asarray(w_pool, f32)[0].transpose(1, 0, 2))
    pool_sc = _fm(np.asarray(pool_scale, f32)[0])
    sink_bc = np.ascontiguousarray(np.broadcast_to(np.asarray(attn_sink, f32)[0][None, :], (128, 8)))
    nm = np.zeros((128, 2, 128), f32)
    kk = np.arange(128)[:, None]
    qq = np.arange(128)[None, :]
    nm[:, 0, :] = np.where(kk >= qq, 0.0, -30000.0)
    nm[:, 1, :] = np.where(kk <= qq, 0.0, -30000.0)
    rows = list(range(512))
    for j in range(4):
        rows += list(512 + j * 64 + np.arange(64)) + list(512 + (4 + j) * 64 + np.arange(64))
    w_out0 = _kt(np.asarray(w_out_even, f32)[0][rows, :])
    w_in1 = _kt(np.asarray(w_in_odd, f32)[0])
    ln_gb = np.ascontiguousarray(np.broadcast_to(np.stack([np.asarray(sgu_ln_g, f32)[0], np.asarray(sgu_ln_b, f32)[0]], 0)[None], (128, 2, D)))
    w_out1 = _kt(np.asarray(w_out_odd, f32)[0])
    wu = np.asarray(w_ffn_up, f32).reshape(2, 8, 128, 2, NF, 128)
    w_up_l = np.ascontiguousarray(wu.transpose(0, 4, 2, 1, 3, 5)).reshape(2, NF, 128, 8, 256)
    wd = np.asarray(w_ffn_down, f32).reshape(2, NF, 128, 8, 128)
    w_down_l = np.ascontiguousarray(wd.transpose(0, 3, 2, 1, 4))
    cwf = np.asarray(ffn_conv_w, f32).reshape(2, 3, 44, 128)
    conv_b_l = np.ascontiguousarray(np.asarray(ffn_conv_b, f32).reshape(2, 44, 128).transpose(2, 0, 1))
    sw = np.asarray(sgu_w, f32)[0]
    sbias = np.asarray(sgu_b, f32)[0]
    ident = np.eye(128, dtype=f32)

    per_half = []
    for hf in range(2):
        pos = _positions(hf)
        C, Sg = _rope_tables(pos)
        mp = _pool_mats(pos)
        cwl = cwf if hf == 0 else cwf[:, ::-1]
        conv_w_l = np.ascontiguousarray(cwl.transpose(3, 0, 2, 1))
        swl = sw if hf == 0 else sw[:, ::-1, ::-1]
        wsT = np.ascontiguousarray(swl.transpose(2, 0, 1))
        sbl = sbias if hf == 0 else sbias[:, ::-1]
        bs_bc = np.ascontiguousarray(np.broadcast_to(sbl[None], (128, 8, 128)))
        per_half.append(dict(pos=pos, ropeC=C, ropeS=Sg, mpool=mp, conv_w=conv_w_l, wsT=wsT, bs_bc=bs_bc))

    in_maps = []
    for core in range(8):
        b, hf = core // 2, core % 2
        ph = per_half[hf]
        cv = np.stack([np.asarray(c, f32)[b], np.asarray(c_ctx, f32)], axis=-1)
        in_maps.append(dict(
            x_loc=np.ascontiguousarray(x[b][ph["pos"]].reshape(NT, 128, D)),
            ctx_in=np.ascontiguousarray(ctx[b].reshape(2, 128, D)),
            cvec=np.ascontiguousarray(cv.reshape(8, 128, 2).transpose(1, 0, 2)),
            w_ada=w_ada_l, b_ada=b_ada_l, gvec=gvec, w_in0=w_in0, ropeC=ph["ropeC"], ropeS=ph["ropeS"],
            w_pool=w_pool_l, pool_sc=pool_sc, mpool=ph["mpool"], sink_bc=sink_bc, negmask=nm, ident_in=ident,
            w_out0=w_out0, w_in1=w_in1, ln_gb=ln_gb, wsT=ph["wsT"], bs_bc=ph["bs_bc"], w_out1=w_out1,
            w_up=w_up_l, w_down=w_down_l, conv_w=ph["conv_w"], conv_b=conv_b_l,
        ))
    key = _debug_stage
    if key not in _PROG:
        _PROG[key] = build_program(_debug_stage)
    res = run_bass_kernel_spmd(_PROG[key], in_maps, core_ids=list(range(8)))
    out = np.empty((4, L, D), f32)
    for core in range(8):
        b, hf = core // 2, core % 2
        o = np.asarray(res.results[core]["out_loc"], f32).reshape(TOWN, D)
        out[b, per_half[hf]["pos"][:TOWN]] = o
    return out
```
